# Optimizing a Trainium2 kernel written in Bass

```python
import jax, jax.numpy as jnp
from jax import lax
import numpy as np

D_MODEL = 1024
BATCH = 4
SEQ = 8192
DEPTH = 4

N_MIXERS = 2
N_RET_LAYERS = (DEPTH + 1) // 2
N_MLA_LAYERS = DEPTH // 2

ROPE_BASE = 10000.0
NORM_EPS = 1e-6

RET_HEADS = 4
RET_QK_DIM = D_MODEL // RET_HEADS
RET_V_DIM = 2 * RET_QK_DIM
RET_WIDTH = RET_HEADS * RET_V_DIM
RET_CHUNK = 128

MLA_HEADS = 16
MLA_NOPE = 128
MLA_ROPE = 64
MLA_V = 128
Q_LORA = 3 * D_MODEL // 4
KV_LORA = D_MODEL // 4
MLA_WIDTH = MLA_HEADS * MLA_V
Q_BLOCK = 128

PLE_DIM = 256

kernel_name = "hybrid_retention_mla_sandwich_ple"


def rms_norm(x, g):
    xf = x.astype(jnp.float32)
    y = xf * lax.rsqrt(jnp.mean(xf * xf, axis=-1, keepdims=True) + NORM_EPS)
    return (y * g.astype(jnp.float32)).astype(x.dtype)


def rope(x, pos):
    half = x.shape[-1] // 2
    inv_freq = ROPE_BASE ** (-jnp.arange(half, dtype=jnp.float32) / half)
    ang = pos.astype(jnp.float32)[..., None] * inv_freq
    cos = jnp.cos(ang)[:, :, None, :]
    sin = jnp.sin(ang)[:, :, None, :]
    xf = x.astype(jnp.float32)
    x1, x2 = xf[..., :half], xf[..., half:]
    return jnp.concatenate([x1 * cos - x2 * sin, x2 * cos + x1 * sin], axis=-1).astype(x.dtype)


def retention_branch(h, pos, w_in, gn_g, w_out):
    B, S, _ = h.shape
    n_chunks = S // RET_CHUNK
    z = h @ w_in
    qk_w = RET_HEADS * RET_QK_DIM
    q = z[..., :qk_w].reshape(B, S, RET_HEADS, RET_QK_DIM)
    k = z[..., qk_w:2 * qk_w].reshape(B, S, RET_HEADS, RET_QK_DIM)
    v = z[..., 2 * qk_w:2 * qk_w + RET_WIDTH].reshape(B, S, RET_HEADS, RET_V_DIM)
    gate = z[..., 2 * qk_w + RET_WIDTH:]
    q = rope(q, pos)
    k = rope(k, pos) * (RET_QK_DIM ** -0.5)

    log_g = jnp.log(1.0 - 2.0 ** (-5.0 - jnp.arange(RET_HEADS, dtype=jnp.float32)))
    idx = jnp.arange(RET_CHUNK, dtype=jnp.float32)
    rel = idx[:, None] - idx[None, :]
    dmask = jnp.where(rel[None] >= 0, jnp.exp(jnp.maximum(rel, 0.0)[None] * log_g[:, None, None]), 0.0)
    q_decay = jnp.exp((idx + 1.0)[None, :] * log_g[:, None])
    k_decay = jnp.exp((RET_CHUNK - 1.0 - idx)[None, :] * log_g[:, None])
    c_decay = jnp.exp(RET_CHUNK * log_g)

    def to_chunks(t):
        return t.astype(jnp.float32).reshape(B, n_chunks, RET_CHUNK, RET_HEADS, t.shape[-1]).transpose(1, 0, 3, 2, 4)

    qc, kc, vc = to_chunks(q), to_chunks(k), to_chunks(v)

    def step(state, inp):
        qb, kb, vb = inp
        a = jnp.einsum('bhid,bhjd->bhij', qb, kb) * dmask
        o = jnp.einsum('bhij,bhjv->bhiv', a, vb) + jnp.einsum(
            'bhid,bhdv->bhiv', qb * q_decay[..., None], state)
        state = state * c_decay[:, None, None] + jnp.einsum(
            'bhjd,bhjv->bhdv', kb * k_decay[..., None], vb)
        return state, o

    s0 = jnp.zeros((B, RET_HEADS, RET_QK_DIM, RET_V_DIM), jnp.float32)
    _, o = lax.scan(step, s0, (qc, kc, vc))
    o = o.transpose(1, 0, 3, 2, 4).reshape(B, S, RET_HEADS, RET_V_DIM)
    o = o * lax.rsqrt(jnp.mean(o * o, axis=-1, keepdims=True) + NORM_EPS)
    o = (o.reshape(B, S, RET_WIDTH) * gn_g.astype(jnp.float32)).astype(h.dtype)
    return (jax.nn.silu(gate) * o) @ w_out


def mla_branch(h, pos, w_in, q_norm_g, kv_norm_g, w_uq, w_uk, w_uv, w_out):
    B, S, _ = h.shape
    n_blocks = S // Q_BLOCK
    z = h @ w_in
    o1, o2, o3 = Q_LORA, Q_LORA + KV_LORA, Q_LORA + KV_LORA + MLA_ROPE
    c_q = rms_norm(z[..., :o1], q_norm_g)
    c_kv = rms_norm(z[..., o1:o2], kv_norm_g)
    k_rope = rope(z[..., o2:o3][:, :, None, :], pos)[:, :, 0, :]
    gate = z[..., o3:]

    q = (c_q @ w_uq).reshape(B, S, MLA_HEADS, MLA_NOPE + MLA_ROPE)
    q_nope = q[..., :MLA_NOPE]
    q_rope = rope(q[..., MLA_NOPE:], pos)
    q_abs = jnp.einsum('bshd,chd->bshc', q_nope, w_uk)
    scale = (MLA_NOPE + MLA_ROPE) ** -0.5
    q_cat = jnp.concatenate([q_abs, q_rope], axis=-1) * scale
    k_cat = jnp.concatenate([c_kv, k_rope], axis=-1)
    q_blocks = q_cat.reshape(B, n_blocks, Q_BLOCK, MLA_HEADS, -1).transpose(1, 0, 2, 3, 4)
    key_idx = jnp.arange(S)
    neg = jnp.finfo(jnp.float32).min

    def attend(args):
        qb, bi = args
        s = jnp.einsum('bqhc,bkc->bhqk', qb, k_cat).astype(jnp.float32)
        q_idx = bi * Q_BLOCK + jnp.arange(Q_BLOCK)
        s = jnp.where(key_idx[None, :] <= q_idx[:, None], s, neg)
        pr = jax.nn.softmax(s, axis=-1).astype(c_kv.dtype)
        return jnp.einsum('bhqk,bkc->bqhc', pr, c_kv)

    o_lat = lax.map(attend, (q_blocks, jnp.arange(n_blocks)))
    o_lat = o_lat.transpose(1, 0, 2, 3, 4).reshape(B, S, MLA_HEADS, KV_LORA)
    o = jnp.einsum('bshc,chd->bshd', o_lat, w_uv).reshape(B, S, MLA_WIDTH)
    return (jax.nn.silu(gate) * o) @ w_out


def setup_inputs(seed: int = 0) -> dict:
    key = jax.random.key(seed)
    ks = jax.random.split(key, 20)
    f32 = jnp.float32

    def w(k, shape, fan_in):
        return jax.random.normal(k, shape, f32) * (fan_in ** -0.5)

    def gain(k, shape):
        return 1.0 + 0.02 * jax.random.normal(k, shape, f32)

    ret_in_w = 2 * RET_HEADS * RET_QK_DIM + 2 * RET_WIDTH
    mla_in_w = Q_LORA + KV_LORA + MLA_ROPE + MLA_WIDTH
    offset = jax.random.randint(ks[2], (BATCH, 1), 0, 4096, dtype=jnp.int32)
    return {
        "x": jax.random.normal(ks[0], (BATCH, SEQ, D_MODEL), f32),
        "p": jax.random.normal(ks[1], (DEPTH, BATCH, SEQ, PLE_DIM), f32),
        "positions": offset + jnp.arange(SEQ, dtype=jnp.int32)[None, :],
        "pre_norm_g": gain(ks[3], (DEPTH, D_MODEL)),
        "post_norm_g": gain(ks[4], (DEPTH, D_MODEL)),
        "ret_w_in": w(ks[5], (N_RET_LAYERS, D_MODEL, ret_in_w), D_MODEL),
        "ret_gn_g": gain(ks[6], (N_RET_LAYERS, RET_WIDTH)),
        "ret_w_out": w(ks[7], (N_RET_LAYERS, RET_WIDTH, D_MODEL), RET_WIDTH),
        "mla_w_in": w(ks[8], (N_MLA_LAYERS, D_MODEL, mla_in_w), D_MODEL),
        "mla_q_norm_g": gain(ks[9], (N_MLA_LAYERS, Q_LORA)),
        "mla_kv_norm_g": gain(ks[10], (N_MLA_LAYERS, KV_LORA)),
        "mla_w_uq": w(ks[11], (N_MLA_LAYERS, Q_LORA, MLA_HEADS * (MLA_NOPE + MLA_ROPE)), Q_LORA),
        "mla_w_uk": w(ks[12], (N_MLA_LAYERS, KV_LORA, MLA_HEADS, MLA_NOPE), KV_LORA),
        "mla_w_uv": w(ks[13], (N_MLA_LAYERS, KV_LORA, MLA_HEADS, MLA_V), KV_LORA),
        "mla_w_out": w(ks[14], (N_MLA_LAYERS, MLA_WIDTH, D_MODEL), MLA_WIDTH),
        "ple_w_proj": w(ks[15], (DEPTH, PLE_DIM, D_MODEL), PLE_DIM),
        "ple_w_gate": w(ks[16], (DEPTH, D_MODEL, D_MODEL), D_MODEL),
    }


def reference(x, p, positions, pre_norm_g, post_norm_g, ret_w_in, ret_gn_g, ret_w_out,
              mla_w_in, mla_q_norm_g, mla_kv_norm_g, mla_w_uq, mla_w_uk, mla_w_uv, mla_w_out,
              ple_w_proj, ple_w_gate):
    for i in range(DEPTH):
        h = rms_norm(x, pre_norm_g[i])
        j = i // N_MIXERS
        if i % N_MIXERS == 0:
            y = retention_branch(h, positions, ret_w_in[j], ret_gn_g[j], ret_w_out[j])
        else:
            y = mla_branch(h, positions, mla_w_in[j], mla_q_norm_g[j], mla_kv_norm_g[j],
                           mla_w_uq[j], mla_w_uk[j], mla_w_uv[j], mla_w_out[j])
        x = x + rms_norm(y, post_norm_g[i])
        x = x + (p[i] @ ple_w_proj[i]) * jax.nn.sigmoid(x @ ple_w_gate[i])
    return x
```

```python
from contextlib import ExitStack
import numpy as np
import concourse.bass as bass
import concourse.mybir as mybir
from concourse.bass_utils import run_bass_kernel_spmd

F32 = mybir.dt.float32
BF16 = mybir.dt.bfloat16
I32 = mybir.dt.int32
ALU = mybir.AluOpType
AF = mybir.ActivationFunctionType

T = 8192
D = 1024
NT = 256
NTILES = T // NT
NBLK = NT // 128
EPS = 1e-6
GAMMAS = [1.0 - 2.0 ** (-5.0 - h) for h in range(4)]
CDEC = [g ** 128 for g in GAMMAS]
SM_SCALE = float(192.0 ** -0.5)

GC_PRE, GC_POST, GC_GN, GC_QN, GC_KVN, GC_INVF, GC_SGN, GC_N = 0, 32, 64, 96, 108, 112, 114, 115

EPOCH = 24000
DBG = {'stop': 99, 'tiles': None}
DMA_RING = 8
RING = {'sp': 8, 'pool': 2}


class Tok:
    __slots__ = ("eng", "idx", "needed", "sem", "val")

    def __init__(self, eng, idx):
        self.eng = eng
        self.idx = idx
        self.needed = False
        self.sem = None
        self.val = None


class Buf:
    __slots__ = ("name", "w", "r", "rd")

    def __init__(self, name=""):
        self.name = name
        self.w = None
        self.r = {}
        self.rd = []


def bufs(n, name=""):
    return [Buf(f"{name}{i}") for i in range(n)]


class Sched:
    QUEUES = ("sp", "pool")

    def __init__(self, nc):
        self.nc = nc
        self.ops = {e: [] for e in ("pe", "act", "dve", "pool", "sp")}
        self.dma_count = {"sp": 0, "pool": 0}
        self.dma_last = {q: [None] * RING[q] for q in ("sp", "pool")}
        self.last_tok = {}

    def _deps(self, eng, reads, writes, is_dma):
        waits = []
        for b in reads:
            if b.w is not None:
                waits.append(b.w)
        for b in writes:
            if b.w is not None:
                waits.append(b.w)
            waits.extend(b.r.values())
            waits.extend(b.rd)
        if eng == "pe" and not is_dma:
            waits = [t for t in waits if t.eng != "pe"]
        return waits

    def op(self, eng, fn, reads=(), writes=()):
        tok = Tok(eng, len(self.ops[eng]))
        waits = self._deps(eng, reads, writes, False)
        for t in waits:
            t.needed = True
        for b in reads:
            b.r[eng] = tok
        for b in writes:
            b.w = tok
            b.r = {}
            b.rd = []
        self.ops[eng].append((fn, waits, tok, False))
        self.last_tok[eng] = tok
        return tok

    def dma(self, q, out_ap, in_ap, reads=(), writes=()):
        n = self.dma_count[q]
        self.dma_count[q] = n + 1
        slot = n % RING[q]
        tok = Tok("dma_" + q, slot)
        tok.val = 16 * (n // RING[q] + 1)
        tok.needed = True
        waits = self._deps(q, reads, writes, True)
        prev = self.dma_last[q][slot]
        if prev is not None:
            waits.append(prev)
        self.dma_last[q][slot] = tok
        for t in waits:
            t.needed = True
        for b in reads:
            b.rd.append(tok)
        for b in writes:
            b.w = tok
            b.r = {}
            b.rd = []

        def fn(e, out_ap=out_ap, in_ap=in_ap):
            return e.dma_start(out=out_ap, in_=in_ap)
        self.ops[q].append((fn, waits, tok, True))
        return tok

    def barrier(self):
        lasts = []
        for e in ("pe", "act", "dve", "pool"):
            t = self.last_tok.get(e)
            if t is not None:
                t.needed = True
                lasts.append(t)
        for q in self.QUEUES:
            for t in self.dma_last[q]:
                if t is not None:
                    lasts.append(t)
        for e in ("act", "dve", "pool", "sp"):
            self.ops[e].append((None, list(lasts), None, False))

    def emit(self, stack):
        nc = self.nc
        n_sig = {e: sum(1 for o in self.ops[e] if o[0] is not None and (not o[3]) and o[2].needed)
                 for e in self.ops}
        eng_sems = {}
        for e in self.ops:
            k = max(1, (n_sig[e] + EPOCH - 1) // EPOCH)
            eng_sems[e] = [stack.enter_context(nc.semaphore(f"s_{e}_{i}")) for i in range(k)]
        dma_sems = {q: [stack.enter_context(nc.semaphore(f"d_{q}_{i}")) for i in range(RING[q])]
                    for q in self.QUEUES}
        for e in self.ops:
            c = 0
            for (fn, waits, tok, is_dma) in self.ops[e]:
                if fn is None:
                    continue
                if is_dma:
                    tok.sem = dma_sems[e][tok.idx]
                elif tok.needed:
                    tok.sem = eng_sems[e][c // EPOCH]
                    tok.val = c % EPOCH + 1
                    c += 1
        block = stack.enter_context(nc.Block())
        engobj = {"pe": "tensor", "act": "scalar", "dve": "vector", "pool": "gpsimd", "sp": "sync"}
        stats = {}
        for e in ("sp", "pool", "pe", "act", "dve"):
            ops = self.ops[e]

            def body(eng, ops=ops, e=e):
                seen = {}
                nw = 0
                for (fn, waits, tok, is_dma) in ops:
                    need = {}
                    for t in waits:
                        key = id(t.sem)
                        if seen.get(key, 0) >= t.val:
                            continue
                        if key not in need or need[key][1] < t.val:
                            need[key] = (t.sem, t.val)
                    for key, (sem, val) in need.items():
                        eng.wait_ge(sem, val)
                        seen[key] = val
                        nw += 1
                    if fn is None:
                        continue
                    ins = fn(eng)
                    if is_dma:
                        ins.then_inc(tok.sem, 16)
                    elif tok.needed:
                        ins.then_inc(tok.sem, 1)
                if e in self.QUEUES:
                    for t in self.dma_last[e]:
                        if t is not None and seen.get(id(t.sem), 0) < t.val:
                            eng.wait_ge(t.sem, t.val)
                            seen[id(t.sem)] = t.val
                stats[e] = (len(ops), nw)
            getattr(block, engobj[e])(body)
        return stats


def build_program(n_layers=4, layers=None):
    nc = bass.Bass("TRN2", target_bir_lowering=False)

    def din(name, shape, dt=F32):
        return nc.dram_tensor(name, shape, dt, kind="ExternalInput").ap()

    x_d = din("x", [T, D])
    p_d = din("p", [4, T, 256])
    pos_d = din("pos", [1, T], I32)
    gc_d = din("gcols", [128, GC_N])
    ident_d = din("ident", [128, 128])
    dec_d = din("dec", [128, 8, 128])
    mask01_d = din("mask01", [128, 128])
    maskb_d = din("maskb", [128, 4 * NT])
    ret_w_in = din("ret_w_in", [2, D, 6144])
    ret_w_out = din("ret_w_out", [2, 2048, D])
    mla_w_in = din("mla_w_in", [2, D, 3136])
    mla_w_uq = din("mla_w_uq", [2, 768, 3072])
    mla_w_uk = din("mla_w_uk", [2, 256, 2048])
    mla_w_uv = din("mla_w_uv", [2, 256, 2048])
    mla_w_out = din("mla_w_out", [2, 2048, D])
    ple_w_proj = din("ple_w_proj", [4, 256, D])
    ple_w_gate = din("ple_w_gate", [4, D, D])
    out_d = nc.dram_tensor("out", [T, D], F32, kind="ExternalOutput").ap()
    xT_d = nc.dram_tensor("xT_s", [128, 8, T], F32).ap()
    pT_d = nc.dram_tensor("pT_s", [4, 128, 2, T], BF16).ap()
    tab_d = nc.dram_tensor("tab_s", [4, 128, T], F32).ap()

    S = Sched(nc)
    st = ExitStack()

    def sb(name, shape, dt):
        return st.enter_context(nc.sbuf_tensor(name, shape, dt))

    def psum(name):
        return st.enter_context(nc.psum_tensor(name, [128, 512], F32))

    def mm(out, lhsT, rhs, start, stop, reads, writes):
        S.op("pe", lambda e: e.matmul(out, lhsT=lhsT, rhs=rhs, start=start, stop=stop), reads, writes)

    def tr(out, in_, ident, reads, writes):
        S.op("pe", lambda e: e.transpose(out=out, in_=in_, identity=ident), reads, writes)

    def act(out, in_, func, reads, writes, scale=1.0, bias=None):
        if bias is None:
            S.op("act", lambda e: e.activation(out=out, in_=in_, func=func, scale=scale), reads, writes)
        else:
            S.op("act", lambda e: e.activation(out=out, in_=in_, func=func, scale=scale, bias=bias),
                 reads, writes)

    def cp(eng, out, in_, reads, writes):
        if eng == "act":
            S.op("act", lambda e: e.copy(out=out, in_=in_), reads, writes)
        else:
            S.op(eng, lambda e: e.tensor_copy(out=out, in_=in_), reads, writes)

    def tt(eng, out, in0, in1, op, reads, writes):
        S.op(eng, lambda e: e.tensor_tensor(out=out, in0=in0, in1=in1, op=op), reads, writes)

    def ts(eng, out, in0, s1, s2, op0, op1, reads, writes):
        if s2 is None:
            S.op(eng, lambda e: e.tensor_scalar(out=out, in0=in0, scalar1=s1, scalar2=None, op0=op0),
                 reads, writes)
        else:
            S.op(eng, lambda e: e.tensor_scalar(out=out, in0=in0, scalar1=s1, scalar2=s2, op0=op0, op1=op1),
                 reads, writes)

    def stt(eng, out, in0, scalar, in1, op0, op1, reads, writes):
        eng = "dve"
        S.op(eng, lambda e: e.scalar_tensor_tensor(out=out, in0=in0, scalar=scalar, in1=in1, op0=op0, op1=op1),
             reads, writes)

    def memset(eng, ap, val, writes):
        S.op(eng, lambda e: e.memset(ap, val), (), writes)

    dumps = []

    def dump(name, ap, reads):
        if not DBG.get('dump'):
            return
        dd = nc.dram_tensor(name, list(ap.shape), ap.dtype, kind="ExternalOutput").ap()
        S.dma("sp", dd, ap, reads=reads)
        dumps.append(name)

    gc = sb("gc", [128, GC_N], F32); GCb = Buf()
    identf = sb("identf", [128, 128], F32); IDF = Buf()
    identb = sb("identb", [128, 128], BF16); IDB = Buf()
    onesb = sb("onesb", [128, 128], BF16); ONB = Buf()
    onesf = sb("onesf", [128, 128], F32); ONF = Buf()
    epst = sb("epst", [128, 1], F32); EPSb = Buf()
    xt = sb("xt", [128, 8, NT], F32); XT = bufs(8)
    hT = sb("hT", [128, 8, NT], BF16); HT = bufs(8)
    rstd = sb("rstd", [128, NT], F32); RSTD = Buf()
    NW = 4
    wp = [sb(f"wp{i}", [128, 8, 512], BF16) for i in range(NW)]
    WP = bufs(NW)
    ybuf = sb("ybuf", [128, 8, NT], F32); YB = bufs(8)
    ysq = sb("ysq", [128, 8, NT], BF16); YSQ = bufs(8)
    u = sb("u", [128, 16, NT], BF16); U = bufs(16)
    pt = sb("pt", [128, 2, NT], BF16); PTb = Buf()
    sig = sb("sig", [128, NT], F32); SIG = Buf()
    tA = sb("tA", [128, NT], F32); TA = Buf()
    tB = sb("tB", [128, NT], F32); TB = Buf()
    cs = sb("cs", [128, 2, NT], F32); CS = Buf()
    wr_t = sb("wr_t", [128, 6, 512], BF16); WRb = Buf()
    wrs_t = sb("wrs_t", [128, 6, 512], BF16); WRSb = Buf()
    UN_BYTES = 121 * 1024
    un = sb("un", [128, UN_BYTES // 2], BF16)

    class Carver:
        def __init__(self):
            self.off = 0

        def get(self, shape, dt):
            n = int(np.prod(shape[1:]))
            nb = n * (4 if dt == F32 else 2)
            nb_al = (nb + 63) // 64 * 64
            assert self.off + nb_al <= UN_BYTES, (self.off, nb_al)
            v = un[:, self.off // 2: self.off // 2 + nb // 2]
            self.off += nb_al
            if dt == F32:
                v = v.bitcast(F32)
            if len(shape) == 3:
                v = v.rearrange("p (a b) -> p a b", a=shape[1])
            return v

    pb = [psum(f"pb{i}") for i in range(8)]
    PB = bufs(8, "pb")
    pj_rr = [0]

    def pj():
        i = 1 + (pj_rr[0] % 2)
        pj_rr[0] += 1
        return pb[i], PB[i]

    wp_rr = [0]

    def wslot():
        i = wp_rr[0] % NW
        wp_rr[0] += 1
        return wp[i], WP[i]

    WSC = {}

    def wscr(key, shape):
        ap = nc.dram_tensor("wb_" + "_".join(str(k) for k in key), shape, BF16).ap()
        WSC[key] = (ap, Buf())
        return WSC[key]

    def conv_jobs(l):
        jobs = []

        def full(key, src2d):
            ap, B = wscr(key, list(src2d.shape))
            jobs.append(lambda: S.dma("pool", ap.rearrange("(a b) n -> a (b n)", a=128),
                                      src2d.rearrange("(a b) n -> a (b n)", a=128), writes=[B]))
        j = l // 2
        if l % 2 == 0:
            full(("ret_w_in", j), ret_w_in[j])
            full(("ret_w_out", j), ret_w_out[j])
        else:
            full(("mla_w_in", j), mla_w_in[j])
            full(("mla_w_uk", j), mla_w_uk[j])
            full(("mla_w_uv", j), mla_w_uv[j])
            apk, BK = wscr(("wk", j), [1024, 256])
            wi = mla_w_in[j]
            for (d0, s0, n) in ((0, 1024, 64), (64, 1024, 64), (128, 1056, 32), (160, 1024, 32),
                                (192, 1056, 32), (224, 1024, 32)):
                jobs.append(lambda d0=d0, s0=s0, n=n: S.dma("pool", apk[:, d0:d0 + n], wi[:, s0:s0 + n],
                                                           writes=[BK]))
            apr, BR = wscr(("wr", j), [768, 1024])
            aps, BS = wscr(("wrs", j), [768, 1024])
            uq4 = mla_w_uq[j].rearrange("r (h e) -> r h e", e=192)
            apr4 = apr.rearrange("r (h e) -> r h e", e=64)
            aps4 = aps.rearrange("r (h e) -> r h e", e=64)
            for kc in range(6):
                rs = slice(kc * 128, (kc + 1) * 128)
                jobs.append(lambda rs=rs: S.dma("pool", apr4[rs, :, :], uq4[rs, :, 128:192], writes=[BR]))
                jobs.append(lambda rs=rs: S.dma("pool", aps4[rs, :, 0:32], uq4[rs, :, 160:192], writes=[BS]))
                jobs.append(lambda rs=rs: S.dma("pool", aps4[rs, :, 32:64], uq4[rs, :, 128:160], writes=[BS]))
            full(("mla_w_uq", j), mla_w_uq[j])
            full(("mla_w_out", j), mla_w_out[j])
        full(("ple_w_gate", l), ple_w_gate[l])
        full(("ple_w_proj", l), ple_w_proj[l])
        return jobs

    pending = []

    def conv_step(n=1):
        for _ in range(n):
            if pending:
                pending.pop(0)()

    def wload(key, r0, nrows, cols, slot=None):
        w2d, WBUF = WSC[key]
        tile_, b = slot if slot is not None else wslot()
        kc = nrows // 128
        o = 0
        for (c0, ncols) in cols:
            src = w2d[r0:r0 + nrows, c0:c0 + ncols].rearrange("(k p) n -> p k n", p=128)
            S.dma("sp", tile_[:, 0:kc, o:o + ncols], src, reads=[WBUF], writes=[b])
            o += ncols
        return tile_, b

    if layers is None:
        layers = list(range(n_layers))
    if layers:
        for jb in conv_jobs(layers[0]):
            jb()
    S.dma("sp", gc[:], gc_d, writes=[GCb])
    S.dma("sp", identf[:], ident_d, writes=[IDF])
    S.dma("pool", identb[:], ident_d, writes=[IDB])
    memset("dve", onesb[:], 1.0, [ONB])
    memset("dve", onesf[:], 1.0, [ONF])
    memset("dve", epst[:], EPS, [EPSb])

    C1 = 6.28125
    C2 = float(2 * np.pi - 6.28125)
    PCH = 2048
    cvp = Carver()
    posi = cvp.get([128, PCH], F32).bitcast(I32); POSI = Buf()
    posf = cvp.get([128, PCH], F32); POSF = Buf()
    angb = cvp.get([128, PCH], F32); ANG = Buf()
    a2b = cvp.get([128, PCH], F32); A2 = Buf()
    kfb = cvp.get([128, PCH], F32); KF = Buf()
    kib = cvp.get([128, PCH], F32).bitcast(I32); KI = Buf()
    for ch in range(T // PCH):
        tsl = slice(ch * PCH, (ch + 1) * PCH)
        S.dma("sp", posi[:], pos_d[0:1, tsl].partition_broadcast(128), writes=[POSI])
        cp("dve", posf[:], posi[:], [POSI], [POSF])
        for sset in range(2):
            ts("dve", angb[:], posf[:], gc[:, GC_INVF + sset:GC_INVF + sset + 1], None, ALU.mult, None,
               [POSF, GCb], [ANG])
            for which in range(2):
                shift = float(np.pi / 2) if which == 0 else 0.0
                ts("dve", a2b[:], angb[:], shift, None, ALU.add, None, [ANG], [A2])
                ts("dve", kfb[:], a2b[:], float(1.0 / (2 * np.pi)), None, ALU.mult, None, [A2], [KF])
                cp("dve", kib[:], kfb[:], [KF], [KI])
                cp("dve", kfb[:], kib[:], [KI], [KF])
                stt("dve", a2b[:], kfb[:], -C1, a2b[:], ALU.mult, ALU.add, [KF, A2], [A2])
                stt("dve", a2b[:], kfb[:], -C2, a2b[:], ALU.mult, ALU.add, [KF, A2], [A2])
                ts("dve", kfb[:], a2b[:], float(np.pi), float(-2 * np.pi), ALU.is_gt, ALU.mult, [A2], [KF])
                tt("dve", a2b[:], a2b[:], kfb[:], ALU.add, [A2, KF], [A2])
                ts("dve", a2b[:], a2b[:], -3.1415925, 3.1415925, ALU.max, ALU.min, [A2], [A2])
                act(a2b[:], a2b[:], AF.Sin, [A2], [A2])
                if sset == 1 and which == 1:
                    ts("dve", a2b[:], a2b[:], gc[:, GC_SGN:GC_SGN + 1], None, ALU.mult, None, [A2, GCb], [A2])
                S.dma("pool", tab_d[sset * 2 + which, :, tsl], a2b[:], reads=[A2])

    xin = cvp.get([128, D], F32); XIN = Buf()
    pin = cvp.get([128, 4, 256], F32); PIN = Buf()
    xst = cvp.get([128, 8, 512], F32); XST = Buf()
    pst = cvp.get([128, 8, 512], BF16); PST = Buf()
    for it in range(T // 512):
        for bl in range(4):
            t0 = it * 512 + bl * 128
            S.dma("sp", xin[:], x_d[t0:t0 + 128, :], writes=[XIN])
            S.dma("sp", pin[:], p_d[:, t0:t0 + 128, :].rearrange("l t e -> t l e"), writes=[PIN])
            for half in range(2):
                bank, B = pb[3 + half], PB[3 + half]
                for c4 in range(4):
                    c = half * 4 + c4
                    tr(bank[:, c4 * 128:(c4 + 1) * 128], xin[:, c * 128:(c + 1) * 128], identf[:],
                       [XIN, IDF], [B])
                cp("act" if half == 0 else "dve",
                   xst[:, half * 4:half * 4 + 4, bl * 128:(bl + 1) * 128],
                   bank[:].rearrange("p (c n) -> p c n", c=4), [B], [XST])
            for half in range(2):
                bank, B = pb[5 + half], PB[5 + half]
                for c4 in range(4):
                    c = half * 4 + c4
                    tr(bank[:, c4 * 128:(c4 + 1) * 128], pin[:, c // 2, (c % 2) * 128:(c % 2 + 1) * 128],
                       identf[:], [PIN, IDF], [B])
                cp("act" if half == 1 else "dve",
                   pst[:, half * 4:half * 4 + 4, bl * 128:(bl + 1) * 128],
                   bank[:].rearrange("p (c n) -> p c n", c=4), [B], [PST])
        S.dma("pool", xT_d[:, :, it * 512:(it + 1) * 512], xst[:], reads=[XST])
        for l in range(4):
            S.dma("pool", pT_d[l, :, :, it * 512:(it + 1) * 512], pst[:, l * 2:l * 2 + 2, :], reads=[PST])
    XTD = Buf()
    XTD_t = bufs(NTILES)
    S.barrier()

    def prenorm(l, ti):
        tsl = slice(ti * NT, (ti + 1) * NT)
        conv_step(2)
        S.dma("sp", xt[:], xT_d[:, :, tsl], reads=[XTD_t[ti]], writes=XT)
        act(hT[:].rearrange("p c n -> p (c n)"), xt[:].rearrange("p c n -> p (c n)"), AF.Square, XT, HT)
        for c in range(8):
            mm(pb[0][:, 0:NT], onesb[:], hT[:, c, :], c == 0, c == 7, [ONB, HT[c]], [PB[0]])
        act(rstd[:], pb[0][:, 0:NT], AF.Sqrt, [PB[0], EPSb], [RSTD], scale=1.0 / D, bias=epst[:])
        S.op("dve", lambda e: e.reciprocal(out=rstd[:], in_=rstd[:]), [RSTD], [RSTD])
        for c in range(8):
            stt("dve" if c % 2 == 0 else "pool", hT[:, c, :], xt[:, c, :],
                gc[:, GC_PRE + l * 8 + c:GC_PRE + l * 8 + c + 1], rstd[:], ALU.mult, ALU.mult,
                [XT[c], GCb, RSTD], [HT[c]])

    def tail(l, ti, wout_key, last):
        tsl = slice(ti * NT, (ti + 1) * NT)
        for og in range(2):
            wa, WA = wload(wout_key, 0, 1024, [(og * 512, 512)])
            wb, WB = wload(wout_key, 1024, 1024, [(og * 512, 512)])
            for o4 in range(4):
                oc = og * 4 + o4
                bank, B = pj()
                KK = DBG.get('kk', 16)
                for kc in range(KK):
                    w_, W_ = (wa, WA) if kc < 8 else (wb, WB)
                    mm(bank[:, 0:NT], w_[:, kc % 8, o4 * 128:(o4 + 1) * 128], u[:, kc, :], kc == 0, kc == KK - 1,
                       [W_, U[kc]], [B])
                cp("act", ybuf[:, oc, :], bank[:, 0:NT], [B], [YB[oc]])
                act(ysq[:, oc, :], bank[:, 0:NT], AF.Square, [B], [YSQ[oc]])
        if DBG['stop'] < 6:
            return
        for c in range(8):
            mm(pb[0][:, 0:NT], onesb[:], ysq[:, c, :], c == 0, c == 7, [ONB, YSQ[c]], [PB[0]])
        act(rstd[:], pb[0][:, 0:NT], AF.Sqrt, [PB[0], EPSb], [RSTD], scale=1.0 / D, bias=epst[:])
        S.op("dve", lambda e: e.reciprocal(out=rstd[:], in_=rstd[:]), [RSTD], [RSTD])
        for c in range(8):
            e1 = "dve" if c % 2 == 0 else "pool"
            stt(e1, ybuf[:, c, :], ybuf[:, c, :], gc[:, GC_POST + l * 8 + c:GC_POST + l * 8 + c + 1], rstd[:],
                ALU.mult, ALU.mult, [YB[c], GCb, RSTD], [YB[c]])
            tt(e1, xt[:, c, :], xt[:, c, :], ybuf[:, c, :], ALU.add, [XT[c], YB[c]], [XT[c]])
            cp("act", hT[:, c, :], xt[:, c, :], [XT[c]], [HT[c]])
        if DBG['stop'] < 7:
            return
        S.dma("sp", pt[:], pT_d[l, :, :, tsl], writes=[PTb])
        for og in range(2):
            wg, WG = wload(("ple_w_gate", l), 0, 1024, [(og * 512, 512)])
            wq, WQ = wload(("ple_w_proj", l), 0, 256, [(og * 512, 512)])
            for o4 in range(4):
                oc = og * 4 + o4
                bank, B = pj()
                for kc in range(8):
                    mm(bank[:, 0:NT], wg[:, kc, o4 * 128:(o4 + 1) * 128], hT[:, kc, :], kc == 0, kc == 7,
                       [WG, HT[kc]], [B])
                for kc in range(2):
                    mm(bank[:, NT:2 * NT], wq[:, kc, o4 * 128:(o4 + 1) * 128], pt[:, kc, :], kc == 0, kc == 1,
                       [WQ, PTb], [B])
                act(sig[:], bank[:, 0:NT], AF.Sigmoid, [B], [SIG])
                tt("dve", tA[:], bank[:, NT:2 * NT], sig[:], ALU.mult, [B, SIG], [TA])
                tt("pool", xt[:, oc, :], xt[:, oc, :], tA[:], ALU.add, [XT[oc], TA], [XT[oc]])
        if DBG['stop'] < 8:
            return
        if not last:
            S.dma("pool", xT_d[:, :, tsl], xt[:], reads=XT, writes=[XTD_t[ti]])
        else:
            otile = ybuf[:].rearrange("p c n -> p (c n)")[:, 0:1024]
            for bl in range(NBLK):
                for half in range(2):
                    bank, B = pb[3 + half], PB[3 + half]
                    for c4 in range(4):
                        c = half * 4 + c4
                        tr(bank[:, c4 * 128:(c4 + 1) * 128], xt[:, c, bl * 128:(bl + 1) * 128], identf[:],
                           [XT[c], IDF], [B])
                    cp("act" if half == 0 else "dve", otile[:, half * 512:(half + 1) * 512], bank[:],
                       [B], YB)
                t0 = ti * NT + bl * 128
                S.dma("pool", out_d[t0:t0 + 128, :], otile, reads=YB)

    def ret_layer(l, last):
        j = l // 2
        w_in = ("ret_w_in", j)
        cv = Carver()
        dec = cv.get([128, 8, 128], F32); DEC = Buf()
        mask01 = cv.get([128, 128], F32); M01 = Buf()
        stage = cv.get([128, 2, NT], F32); STG = bufs(2)
        qh = cv.get([128, 2, NT], BF16); QH = Buf()
        kh = cv.get([128, 2, NT], BF16); KH = Buf()
        ktok = cv.get([128, NBLK, 256], BF16); KTOK = bufs(NBLK)
        vt = cv.get([128, NBLK, 512], BF16); VT = bufs(NBLK)
        sg = cv.get([128, 4, NT], BF16); SG = bufs(4)
        Sst = [cv.get([128, 2, 512], F32) for _ in range(4)]; SST = [bufs(2) for _ in range(4)]
        Sbf = [cv.get([128, 2, 512], BF16) for _ in range(4)]; SBF = [bufs(2) for _ in range(4)]
        am = cv.get([128, 128], BF16); AM = Buf()
        oraw = cv.get([128, 4, NT], BF16); ORAW = Buf()
        osq = cv.get([128, 4, NT], BF16); OSQ = Buf()
        rso = cv.get([128, NT], F32); RSO = Buf()
        S.dma("sp", dec, dec_d, writes=[DEC])
        S.dma("sp", mask01, mask01_d, writes=[M01])
        for h in range(4):
            for dc in range(2):
                memset("pool", Sst[h][:, dc, :], 0.0, [SST[h][dc]])
                memset("pool", Sbf[h][:, dc, :], 0.0, [SBF[h][dc]])
        for ti in range(NTILES if DBG['tiles'] is None else DBG['tiles']):
            tsl = slice(ti * NT, (ti + 1) * NT)
            prenorm(l, ti)
            S.dma("sp", cs[:, 0, :], tab_d[0, :, tsl], writes=[CS])
            S.dma("sp", cs[:, 1, :], tab_d[1, :, tsl], writes=[CS])
            for h in range(4 if DBG['stop'] > 0 else 0):
                wqk, WQK = wload(w_in, 0, 1024, [(h * 256, 256), (1024 + h * 256, 256)])
                wv, WV = wload(w_in, 0, 1024, [(2048 + h * 512, 512)])
                wg, WG = wload(w_in, 0, 1024, [(4096 + h * 512, 512)])
                for which in range(2):
                    dst, DST = (qh, QH) if which == 0 else (kh, KH)
                    for half in range(2):
                        bank, B = pj()
                        c0 = which * 256 + half * 128
                        for kc in range(8):
                            mm(bank[:, 0:NT], wqk[:, kc, c0:c0 + 128], hT[:, kc, :], kc == 0, kc == 7,
                               [WQK, HT[kc]], [B])
                        tt("dve", stage[:, half, :].rearrange("p (c n) -> p c n", c=NBLK),
                           bank[:, 0:NT].rearrange("p (c n) -> p c n", c=NBLK),
                           dec[:, which * 4 + h, :].unsqueeze(1).to_broadcast([128, NBLK, 128]),
                           ALU.mult, [B, DEC], [STG[half]])
                    tt("dve", tA[:], stage[:, 0, :], cs[:, 0, :], ALU.mult, [STG[0], CS], [TA])
                    tt("pool", tB[:], stage[:, 1, :], cs[:, 1, :], ALU.mult, [STG[1], CS], [TB])
                    tt("dve", dst[:, 0, :], tA[:], tB[:], ALU.subtract, [TA, TB], [DST])
                    tt("pool", tA[:], stage[:, 1, :], cs[:, 0, :], ALU.mult, [STG[1], CS], [TA])
                    tt("dve", tB[:], stage[:, 0, :], cs[:, 1, :], ALU.mult, [STG[0], CS], [TB])
                    tt("pool", dst[:, 1, :], tA[:], tB[:], ALU.add, [TA, TB], [DST])
                if DBG['stop'] < 2:
                    continue
                pbT = pb[7][:].bitcast(BF16)
                for c in range(NBLK):
                    for dc in range(2):
                        tr(pbT[:, dc * 128:(dc + 1) * 128], kh[:, dc, c * 128:(c + 1) * 128], identb[:],
                           [KH, IDB], [PB[7]])
                    S.op("act", lambda e, c=c, h=h: e.mul(out=ktok[:, c, :], in_=pbT[:, 0:256], mul=float(CDEC[h])),
                         [PB[7]], [KTOK[c]])
                for c in range(NBLK):
                    bank, B = pj()
                    for kc in range(8):
                        mm(bank[:, 0:512], hT[:, kc, c * 128:(c + 1) * 128], wv[:, kc, :], kc == 0, kc == 7,
                           [WV, HT[kc]], [B])
                    cp("act", vt[:, c, :], bank[:, 0:512], [B], [VT[c]])
                for vc in range(4):
                    bank, B = pj()
                    for kc in range(8):
                        mm(bank[:, 0:NT], wg[:, kc, vc * 128:(vc + 1) * 128], hT[:, kc, :], kc == 0, kc == 7,
                           [WG, HT[kc]], [B])
                    act(sg[:, vc, :], bank[:, 0:NT], AF.Silu, [B], [SG[vc]])
                if DBG['stop'] < 3:
                    continue
                for c in range(NBLK):
                    csl = slice(c * 128, (c + 1) * 128)
                    sb_i = 3 + (c % 2)
                    for dc in range(2):
                        mm(pb[sb_i][:, 0:128], kh[:, dc, csl], qh[:, dc, csl], dc == 0, dc == 1,
                           [KH, QH], [PB[sb_i]])
                    tt("dve", am, pb[sb_i][:, 0:128], mask01, ALU.mult, [PB[sb_i], M01], [AM])
                    ob, OB = pb[5], PB[5]
                    first = (ti == 0 and c == 0)
                    for vc in range(4):
                        mm(ob[:, vc * 128:(vc + 1) * 128], vt[:, c, vc * 128:(vc + 1) * 128], am, True, first,
                           [VT[c], AM], [OB])
                        if not first:
                            for dc in range(2):
                                mm(ob[:, vc * 128:(vc + 1) * 128], Sbf[h][:, dc, vc * 128:(vc + 1) * 128],
                                   qh[:, dc, csl], False, dc == 1, [SBF[h][dc], QH], [OB])
                    obv = ob[:].rearrange("p (v n) -> p v n", v=4)
                    cp("act", oraw[:, :, csl], obv, [OB], [ORAW])
                    act(osq[:, :, csl], obv, AF.Square, [OB], [OSQ])
                    for dc in range(2):
                        sbk, SBK = pb[6], PB[6]
                        mm(sbk[:, 0:512], ktok[:, c, dc * 128:(dc + 1) * 128], vt[:, c, :], True, True,
                           [KTOK[c], VT[c]], [SBK])
                        stt("dve", Sst[h][:, dc, :], Sst[h][:, dc, :], float(CDEC[h]), sbk[:, 0:512],
                            ALU.mult, ALU.add, [SST[h][dc], SBK], [SST[h][dc]])
                        cp("pool", Sbf[h][:, dc, :], Sst[h][:, dc, :], [SST[h][dc]], [SBF[h][dc]])
                if DBG['stop'] < 4:
                    continue
                for vc in range(4):
                    mm(pb[0][:, 0:NT], onesb[:], osq[:, vc, :], vc == 0, vc == 3, [ONB, OSQ], [PB[0]])
                act(rso, pb[0][:, 0:NT], AF.Sqrt, [PB[0], EPSb], [RSO], scale=1.0 / 512, bias=epst[:])
                S.op("dve", lambda e: e.reciprocal(out=rso, in_=rso), [RSO], [RSO])
                for vc in range(4):
                    gcol = GC_GN + j * 16 + h * 4 + vc
                    stt("dve", oraw[:, vc, :], oraw[:, vc, :], gc[:, gcol:gcol + 1], rso, ALU.mult, ALU.mult,
                        [ORAW, GCb, RSO], [ORAW])
                    tt("dve", u[:, h * 4 + vc, :], oraw[:, vc, :], sg[:, vc, :], ALU.mult,
                       [ORAW, SG[vc]], [U[h * 4 + vc]])
            if DBG['stop'] < 5:
                continue
            tail(l, ti, ("ret_w_out", j), last)

    def mla_layer(l, last):
        j = l // 2
        w_in = ("mla_w_in", j)
        w_uq = ("mla_w_uq", j)
        cv = Carver()
        KT = cv.get([128, 3, T], BF16); KTB = bufs(T // 128)
        V = cv.get([128, T // 128, 256], BF16); VB = bufs(T // 128)
        cqn = cv.get([128, 6, NT], BF16); CQN = bufs(6)
        wukT = cv.get([128, 16, 256], BF16); WUKT = Buf()
        wuv = cv.get([128, 2, 2048], BF16); WUV = Buf()
        qrz = [cv.get([128, 2 * NT], BF16) for _ in range(2)]; QRZ = bufs(2)
        qn = cv.get([128, NT], BF16); QN = Buf()
        qc = [cv.get([128, 2, 2 * NT], BF16) for _ in range(2)]; QC = bufs(2)
        Pt = [cv.get([128, 2 * NT], BF16) for _ in range(3)]; PT3 = bufs(3)
        acc = [cv.get([128, 2 * NT], F32) for _ in range(2)]; ACC = bufs(2)
        rinv = cv.get([128, 2 * NT], F32); RINV = Buf()
        olat = cv.get([128, 2, 2 * NT], BF16); OLAT = Buf()
        sgm = [cv.get([128, 2 * NT], BF16) for _ in range(2)]; SGM = bufs(2)
        maskb = cv.get([128, 2, 2 * NT], BF16); MKB = Buf()
        for s_ in range(2):
            memset("pool", qrz[s_], 0.0, [QRZ[s_]])
        pbT = pb[7][:].bitcast(BF16)
        S.dma("pool", maskb.rearrange("p a b -> p (a b)"), maskb_d, writes=[MKB])
        S.dma("sp", wuv, WSC[("mla_w_uv", j)][0].rearrange("(k p) n -> p k n", p=128),
              reads=[WSC[("mla_w_uv", j)][1]], writes=[WUV])
        wl, WL = wslot()
        wlv = wl[:].rearrange("p a b -> p (a b)").rearrange("p (k n) -> p k n", k=2)
        S.dma("sp", wlv, WSC[("mla_w_uk", j)][0].rearrange("(k p) n -> p k n", p=128),
              reads=[WSC[("mla_w_uk", j)][1]], writes=[WL])
        for h in range(16):
            for cc in range(2):
                tr(pbT[:, cc * 128:(cc + 1) * 128], wlv[:, cc, h * 128:(h + 1) * 128], identb[:],
                   [WL, IDB], [PB[7]])
            cp("act", wukT[:, h, :], pbT[:, 0:256], [PB[7]], [WUKT])
        pcount = [0]
        ptcount = [0]
        for ti in range(NTILES if DBG['tiles'] is None else DBG['tiles']):
            tsl = slice(ti * NT, (ti + 1) * NT)
            prenorm(l, ti)
            S.dma("sp", cs[:, 0, :], tab_d[2, :, tsl], writes=[CS])
            S.dma("sp", cs[:, 1, :], tab_d[3, :, tsl], writes=[CS])
            for og in range(2):
                w_, W_ = wload(w_in, 0, 1024, [(og * 512, 512)])
                for o4 in range(4):
                    oc = og * 4 + o4
                    bank, B = pj()
                    for kc in range(8):
                        mm(bank[:, 0:NT], w_[:, kc, o4 * 128:(o4 + 1) * 128], hT[:, kc, :], kc == 0, kc == 7,
                           [W_, HT[kc]], [B])
                    cp("act", ybuf[:, oc, :], bank[:, 0:NT], [B], [YB[oc]])
                    act(ysq[:, oc, :], bank[:, 0:NT], AF.Square, [B], [YSQ[oc]])
            for c in range(6):
                mm(pb[0][:, 0:NT], onesb[:], ysq[:, c, :], c == 0, c == 5, [ONB, YSQ[c]], [PB[0]])
            act(rstd[:], pb[0][:, 0:NT], AF.Sqrt, [PB[0], EPSb], [RSTD], scale=1.0 / 768, bias=epst[:])
            S.op("dve", lambda e: e.reciprocal(out=rstd[:], in_=rstd[:]), [RSTD], [RSTD])
            for c in range(6):
                gcol = GC_QN + j * 6 + c
                stt("dve" if c % 2 == 0 else "pool", cqn[:, c, :], ybuf[:, c, :], gc[:, gcol:gcol + 1], rstd[:],
                    ALU.mult, ALU.mult, [YB[c], GCb, RSTD], [CQN[c]])
            kb0 = ti * NBLK
            for c in range(2):
                mm(pb[0][:, 0:NT], onesb[:], ysq[:, 6 + c, :], c == 0, c == 1, [ONB, YSQ[6 + c]], [PB[0]])
            act(rstd[:], pb[0][:, 0:NT], AF.Sqrt, [PB[0], EPSb], [RSTD], scale=1.0 / 256, bias=epst[:])
            S.op("dve", lambda e: e.reciprocal(out=rstd[:], in_=rstd[:]), [RSTD], [RSTD])
            for c in range(2):
                gcol = GC_KVN + j * 2 + c
                stt("dve" if c % 2 == 0 else "pool", KT[:, c, tsl], ybuf[:, 6 + c, :], gc[:, gcol:gcol + 1], rstd[:],
                    ALU.mult, ALU.mult, [YB[6 + c], GCb, RSTD], [KTB[kb0 + b] for b in range(NBLK)])
            for b in range(NBLK):
                for c in range(2):
                    tr(pbT[:, c * 128:(c + 1) * 128], KT[:, c, (kb0 + b) * 128:(kb0 + b + 1) * 128], identb[:],
                       [KTB[kb0 + b], IDB], [PB[7]])
                cp("act", V[:, kb0 + b, :], pbT[:, 0:256], [PB[7]], [VB[kb0 + b]])
            wk, WK = wload(("wk", j), 0, 1024, [(0, 256)])
            bank, B = pj()
            for kc in range(8):
                mm(bank[:, 0:NT], wk[:, kc, 0:128], hT[:, kc, :], kc == 0, kc == 7, [WK, HT[kc]], [B])
            for kc in range(8):
                mm(bank[:, NT:2 * NT], wk[:, kc, 128:256], hT[:, kc, :], kc == 0, kc == 7, [WK, HT[kc]], [B])
            tt("dve", tA[:], bank[:, 0:NT], cs[:, 0, :], ALU.mult, [B, CS], [TA])
            tt("dve", tB[:], bank[:, NT:2 * NT], cs[:, 1, :], ALU.mult, [B, CS], [TB])
            tt("pool", KT[:, 2, tsl], tA[:], tB[:], ALU.add, [TA, TB], [KTB[kb0 + b] for b in range(NBLK)])
            if ti == DBG.get('dump_ti', 0):
                dump("d_hT", hT[:], HT)
                dump("d_cqn", cqn, CQN)
                dump("d_KT", KT[:, :, 0:(ti + 1) * NT], KTB[0:(ti + 1) * NBLK])
                dump("d_V", V[:, 0:(ti + 1) * NBLK, :], VB[0:(ti + 1) * NBLK])
                dump("d_cs", cs[:], [CS])
            nkt = 2 * (ti + 1)
            pctx = {}
            gstate = {}

            def prep(m):
                hg, m_loc = divmod(m, 4)
                s_ = m % 2
                if m_loc == 0:
                    for (dst, DSTB, key) in ((wr_t, WRb, ("wr", j)), (wrs_t, WRSb, ("wrs", j))):
                        S.dma("sp", dst[:], WSC[key][0][:, hg * 512:(hg + 1) * 512].rearrange(
                            "(k p) n -> p k n", p=128), reads=[WSC[key][1]], writes=[DSTB])
                if m % 2 == 0:
                    gstate['wgt'] = wload(w_in, 0, 1024, [(1088 + 2 * m * 128, 512)])
                wgt, WGT = gstate['wgt']
                bank, B = pj()
                for kc in range(6):
                    mm(bank[:, 0:NT], wr_t[:, kc, m_loc * 128:(m_loc + 1) * 128], cqn[:, kc, :],
                       kc == 0, kc == 5, [WRb, CQN[kc]], [B])
                for kc in range(6):
                    mm(bank[:, NT:2 * NT], wrs_t[:, kc, m_loc * 128:(m_loc + 1) * 128], cqn[:, kc, :],
                       kc == 0, kc == 5, [WRSb, CQN[kc]], [B])
                tt("dve", tA[:], bank[:, 0:NT], cs[:, 0, :], ALU.mult, [B, CS], [TA])
                tt("dve", tB[:], bank[:, NT:2 * NT], cs[:, 1, :], ALU.mult, [B, CS], [TB])
                tt("pool", qrz[s_][0:64, 0:NT], tA[0:64, :], tB[0:64, :], ALU.add, [TA, TB], [QRZ[s_]])
                tt("pool", qrz[s_][64:128, NT:2 * NT], tA[64:128, :], tB[64:128, :], ALU.add, [TA, TB], [QRZ[s_]])
                wn, WN = wload(w_uq, 0, 768, [(2 * m * 192, 128), ((2 * m + 1) * 192, 128)])
                for hh in range(2):
                    h = 2 * m + hh
                    hsl = slice(hh * NT, (hh + 1) * NT)
                    bank, B = pj()
                    for kc in range(6):
                        mm(bank[:, 0:NT], wn[:, kc, hh * 128:(hh + 1) * 128], cqn[:, kc, :],
                           kc == 0, kc == 5, [WN, CQN[kc]], [B])
                    hl = h % 4
                    for kc in range(8):
                        mm(bank[:, NT:2 * NT], wgt[:, kc, hl * 128:(hl + 1) * 128], hT[:, kc, :], kc == 0,
                           kc == 7, [WGT, HT[kc]], [B])
                    cp("act", qn, bank[:, 0:NT], [B], [QN])
                    act(sgm[s_][:, hsl], bank[:, NT:2 * NT], AF.Silu, [B], [SGM[s_]])
                    bank, B = pj()
                    for cc in range(2):
                        mm(bank[:, cc * NT:(cc + 1) * NT], wukT[:, h, cc * 128:(cc + 1) * 128], qn,
                           True, True, [WUKT, QN], [B])
                    cp("dve", qc[s_][:, :, hsl], bank[:].rearrange("p (a b) -> p a b", a=2), [B], [QC[s_]])
                pctx[m] = dict(sb={}, pt={})

            def emit_qk(m, kt):
                c = pctx[m]
                if kt in c['sb']:
                    return
                s_ = m % 2
                sbi = 3 + (pcount[0] % 2)
                pcount[0] += 1
                c['sb'][kt] = sbi
                sbank, SBK = pb[sbi], PB[sbi]
                diag = kt >= 2 * ti
                ksl = slice(kt * 128, (kt + 1) * 128)
                mm(sbank[:], KT[:, 0, ksl], qc[s_][:, 0, :], True, False, [KTB[kt], QC[s_]], [SBK])
                mm(sbank[:], KT[:, 1, ksl], qc[s_][:, 1, :], False, False, [KTB[kt], QC[s_]], [SBK])
                mm(sbank[:], KT[:, 2, ksl], qrz[s_], False, not diag, [KTB[kt], QRZ[s_]], [SBK])
                if diag:
                    mm(sbank[:], identb[:], maskb[:, kt - 2 * ti, :], False, True, [IDB, MKB], [SBK])

            def emit_exp_acc(m, kt):
                c = pctx[m]
                sbi = c['sb'][kt]
                pti = ptcount[0] % 3
                ptcount[0] += 1
                c['pt'][kt] = pti
                s_ = m % 2
                act(Pt[pti], pb[sbi][:], AF.Exp, [PB[sbi]], [PT3[pti]], scale=SM_SCALE)
                if kt == 0:
                    cp("dve", acc[s_], Pt[pti], [PT3[pti]], [ACC[s_]])
                else:
                    tt("dve", acc[s_], acc[s_], Pt[pti], ALU.add, [ACC[s_], PT3[pti]], [ACC[s_]])

            def emit_pv(m, kt):
                pti = pctx[m]['pt'][kt]
                for cc in range(2):
                    mm(pb[5 + cc][:], V[:, kt, cc * 128:(cc + 1) * 128], Pt[pti], kt == 0, kt == nkt - 1,
                       [VB[kt], PT3[pti]], [PB[5 + cc]])

            def finish_den(m):
                s_ = m % 2
                mm(pb[0][:], onesf[:], acc[s_], True, True, [ONF, ACC[s_]], [PB[0]])
                S.op("dve", lambda e: e.reciprocal(out=rinv, in_=pb[0][:]), [PB[0]], [RINV])
                for cc in range(2):
                    tt("dve", olat[:, cc, :], pb[5 + cc][:], rinv, ALU.mult, [PB[5 + cc], RINV], [OLAT])

            def finish_uv(m):
                s_ = m % 2
                bank, B = pj()
                for hh in range(2):
                    h = 2 * m + hh
                    for cc in range(2):
                        mm(bank[:, hh * NT:(hh + 1) * NT], wuv[:, cc, h * 128:(h + 1) * 128],
                           olat[:, cc, hh * NT:(hh + 1) * NT], cc == 0, cc == 1, [WUV, OLAT], [B])
                tt("dve", u[:, 2 * m:2 * m + 2, :].rearrange("p a b -> p (a b)"), bank[:], sgm[s_], ALU.mult,
                   [B, SGM[s_]], [U[2 * m], U[2 * m + 1]])

            prep(0)
            emit_qk(0, 0)
            for m in range(8):
                for kt in range(nkt):
                    lastk = (kt == nkt - 1)
                    if not lastk:
                        emit_qk(m, kt + 1)
                    emit_exp_acc(m, kt)
                    if lastk and m + 1 < 8:
                        prep(m + 1)
                    emit_pv(m, kt)
                if m + 1 < 8:
                    emit_qk(m + 1, 0)
                finish_den(m)
                if m + 1 < 8:
                    emit_qk(m + 1, 1)
                finish_uv(m)
            if ti == DBG.get('dump_ti', 0):
                dump("d_u", u[:], U)
            tail(l, ti, ("mla_w_out", j), last)

    if n_layers == 0:
        for ti in range(NTILES):
            tsl = slice(ti * NT, (ti + 1) * NT)
            S.dma("sp", xt[:], xT_d[:, :, tsl], writes=XT)
            otile = ybuf[:].rearrange("p c n -> p (c n)")[:, 0:1024]
            for bl in range(NBLK):
                for half in range(2):
                    bank, B = pb[3 + half], PB[3 + half]
                    for c4 in range(4):
                        c = half * 4 + c4
                        tr(bank[:, c4 * 128:(c4 + 1) * 128], xt[:, c, bl * 128:(bl + 1) * 128], identf[:],
                           [XT[c], IDF], [B])
                    cp("act" if half == 0 else "dve", otile[:, half * 512:(half + 1) * 512], bank[:],
                       [B], YB)
                t0 = ti * NT + bl * 128
                S.dma("pool", out_d[t0:t0 + 128, :], otile, reads=YB)
    for li, l in enumerate(layers):
        last = (li == len(layers) - 1)
        if li + 1 < len(layers):
            pending.extend(conv_jobs(layers[li + 1]))
        if l % 2 == 0:
            ret_layer(l, last)
        else:
            mla_layer(l, last)
        conv_step(len(pending))
        S.barrier()

    stats = S.emit(st)
    st.close()
    return nc, stats


def _consts():
    ident = np.eye(128, dtype=np.float32)
    i = np.arange(128, dtype=np.float64)
    dec = np.zeros((128, 8, 128), np.float32)
    for h in range(4):
        lg = np.log(GAMMAS[h])
        dec[:, h, :] = np.exp((i + 1.0) * lg)[None, :]
        dec[:, 4 + h, :] = (np.exp(-(i + 1.0) * lg) * (256.0 ** -0.5))[None, :]
    jj = np.arange(128)[:, None]
    ii = np.arange(128)[None, :]
    mask01 = (ii >= jj).astype(np.float32)
    qi = np.arange(NT)[None, :]
    maskb = np.zeros((128, 2, 2, NT), np.float32)
    for o in range(2):
        maskb[:, o, :, :] = np.where((128 * o + jj) <= qi, 0.0, -30000.0)[:, None, :]
    return ident, dec, mask01, maskb.reshape(128, 4 * NT)


def _gcols(inp):
    g = np.zeros((128, GC_N), np.float32)

    def put(col0, vec):
        k = vec.shape[0] // 128
        g[:, col0:col0 + k] = vec.reshape(k, 128).T
    for l in range(4):
        put(GC_PRE + l * 8, inp["pre_norm_g"][l])
        put(GC_POST + l * 8, inp["post_norm_g"][l])
    for j in range(2):
        put(GC_GN + j * 16, inp["ret_gn_g"][j])
        put(GC_QN + j * 6, inp["mla_q_norm_g"][j])
        put(GC_KVN + j * 2, inp["mla_kv_norm_g"][j])
    r = np.arange(128)
    g[:, GC_INVF] = (np.float32(10000.0) ** (-(r.astype(np.float32)) / np.float32(128.0))).astype(np.float32)
    g[:, GC_INVF + 1] = (np.float32(10000.0) ** (-((r % 32).astype(np.float32)) / np.float32(32.0))).astype(np.float32)
    g[:, GC_SGN] = np.where((r % 64) < 32, -1.0, 1.0)
    return g


_CACHE = {}


def kernel(**inputs):
    inp = {k: np.asarray(v) for k, v in inputs.items()}
    n_layers = 4
    if "prog" not in _CACHE:
        _CACHE["prog"] = build_program(n_layers)
    nc, _ = _CACHE["prog"]
    ident, dec, mask01, maskb = _consts()
    gcols = _gcols(inp)
    shared = {
        "gcols": gcols, "ident": ident, "dec": dec, "mask01": mask01, "maskb": maskb,
        "ret_w_in": np.ascontiguousarray(inp["ret_w_in"], dtype=np.float32),
        "ret_w_out": np.ascontiguousarray(inp["ret_w_out"], dtype=np.float32),
        "mla_w_in": np.ascontiguousarray(inp["mla_w_in"], dtype=np.float32),
        "mla_w_uq": np.ascontiguousarray(inp["mla_w_uq"], dtype=np.float32),
        "mla_w_uk": np.ascontiguousarray(inp["mla_w_uk"], dtype=np.float32).reshape(2, 256, 2048),
        "mla_w_uv": np.ascontiguousarray(inp["mla_w_uv"], dtype=np.float32).reshape(2, 256, 2048),
        "mla_w_out": np.ascontiguousarray(inp["mla_w_out"], dtype=np.float32),
        "ple_w_proj": np.ascontiguousarray(inp["ple_w_proj"], dtype=np.float32),
        "ple_w_gate": np.ascontiguousarray(inp["ple_w_gate"], dtype=np.float32),
    }
    in_maps = []
    for core in range(8):
        b = core % 4
        m = dict(shared)
        m["x"] = np.ascontiguousarray(inp["x"][b], dtype=np.float32)
        m["p"] = np.ascontiguousarray(inp["p"][:, b], dtype=np.float32)
        m["pos"] = np.ascontiguousarray(inp["positions"][b].reshape(1, T), dtype=np.int32)
        in_maps.append(m)
    res = run_bass_kernel_spmd(nc, in_maps, core_ids=list(range(8)))
    out = np.stack([np.asarray(res.results[b]["out"], dtype=np.float32) for b in range(4)], axis=0)
    return out
```

```python
from contextlib import ExitStack
import numpy as np
import concourse.bass as bass
import concourse.mybir as mybir
from concourse.bass_utils import run_bass_kernel_spmd

F32 = mybir.dt.float32
BF16 = mybir.dt.bfloat16
I32 = mybir.dt.int32
ALU = mybir.AluOpType
AF = mybir.ActivationFunctionType

T = 8192
D = 1024
NT = 256
NTILES = T // NT
NBLK = NT // 128
EPS = 1e-6
GAMMAS = [1.0 - 2.0 ** (-5.0 - h) for h in range(4)]
CDEC = [g ** 128 for g in GAMMAS]
SM_SCALE = float(192.0 ** -0.5)

GC_PRE, GC_POST, GC_GN, GC_QN, GC_KVN, GC_INVF, GC_SGN, GC_N = 0, 32, 64, 96, 108, 112, 114, 115

EPOCH = 24000
DBG = {'stop': 99, 'tiles': None}
DMA_RING = 8
RING = {'sp': 8, 'pool': 2}


class Tok:
    __slots__ = ("eng", "idx", "needed", "sem", "val")

    def __init__(self, eng, idx):
        self.eng = eng
        self.idx = idx
        self.needed = False
        self.sem = None
        self.val = None


class Buf:
    __slots__ = ("name", "w", "r", "rd")

    def __init__(self, name=""):
        self.name = name
        self.w = None
        self.r = {}
        self.rd = []


def bufs(n, name=""):
    return [Buf(f"{name}{i}") for i in range(n)]


class Sched:
    QUEUES = ("sp", "pool")

    def __init__(self, nc):
        self.nc = nc
        self.ops = {e: [] for e in ("pe", "act", "dve", "pool", "sp")}
        self.dma_count = {"sp": 0, "pool": 0}
        self.dma_last = {q: [None] * RING[q] for q in ("sp", "pool")}
        self.last_tok = {}

    def _deps(self, eng, reads, writes, is_dma):
        waits = []
        for b in reads:
            if b.w is not None:
                waits.append(b.w)
        for b in writes:
            if b.w is not None:
                waits.append(b.w)
            waits.extend(b.r.values())
            waits.extend(b.rd)
        if eng == "pe" and not is_dma:
            waits = [t for t in waits if t.eng != "pe"]
        return waits

    def op(self, eng, fn, reads=(), writes=()):
        tok = Tok(eng, len(self.ops[eng]))
        waits = self._deps(eng, reads, writes, False)
        for t in waits:
            t.needed = True
        for b in reads:
            b.r[eng] = tok
        for b in writes:
            b.w = tok
            b.r = {}
            b.rd = []
        self.ops[eng].append((fn, waits, tok, False))
        self.last_tok[eng] = tok
        return tok

    def dma(self, q, out_ap, in_ap, reads=(), writes=()):
        n = self.dma_count[q]
        self.dma_count[q] = n + 1
        slot = n % RING[q]
        tok = Tok("dma_" + q, slot)
        tok.val = 16 * (n // RING[q] + 1)
        tok.needed = True
        waits = self._deps(q, reads, writes, True)
        prev = self.dma_last[q][slot]
        if prev is not None:
            waits.append(prev)
        self.dma_last[q][slot] = tok
        for t in waits:
            t.needed = True
        for b in reads:
            b.rd.append(tok)
        for b in writes:
            b.w = tok
            b.r = {}
            b.rd = []

        def fn(e, out_ap=out_ap, in_ap=in_ap):
            return e.dma_start(out=out_ap, in_=in_ap)
        self.ops[q].append((fn, waits, tok, True))
        return tok

    def barrier(self):
        lasts = []
        for e in ("pe", "act", "dve", "pool"):
            t = self.last_tok.get(e)
            if t is not None:
                t.needed = True
                lasts.append(t)
        for q in self.QUEUES:
            for t in self.dma_last[q]:
                if t is not None:
                    lasts.append(t)
        for e in ("act", "dve", "pool", "sp"):
            self.ops[e].append((None, list(lasts), None, False))

    def emit(self, stack):
        nc = self.nc
        n_sig = {e: sum(1 for o in self.ops[e] if o[0] is not None and (not o[3]) and o[2].needed)
                 for e in self.ops}
        eng_sems = {}
        for e in self.ops:
            k = max(1, (n_sig[e] + EPOCH - 1) // EPOCH)
            eng_sems[e] = [stack.enter_context(nc.semaphore(f"s_{e}_{i}")) for i in range(k)]
        dma_sems = {q: [stack.enter_context(nc.semaphore(f"d_{q}_{i}")) for i in range(RING[q])]
                    for q in self.QUEUES}
        for e in self.ops:
            c = 0
            for (fn, waits, tok, is_dma) in self.ops[e]:
                if fn is None:
                    continue
                if is_dma:
                    tok.sem = dma_sems[e][tok.idx]
                elif tok.needed:
                    tok.sem = eng_sems[e][c // EPOCH]
                    tok.val = c % EPOCH + 1
                    c += 1
        block = stack.enter_context(nc.Block())
        engobj = {"pe": "tensor", "act": "scalar", "dve": "vector", "pool": "gpsimd", "sp": "sync"}
        stats = {}
        for e in ("sp", "pool", "pe", "act", "dve"):
            ops = self.ops[e]

            def body(eng, ops=ops, e=e):
                seen = {}
                nw = 0
                for (fn, waits, tok, is_dma) in ops:
                    need = {}
                    for t in waits:
                        key = id(t.sem)
                        if seen.get(key, 0) >= t.val:
                            continue
                        if key not in need or need[key][1] < t.val:
                            need[key] = (t.sem, t.val)
                    for key, (sem, val) in need.items():
                        eng.wait_ge(sem, val)
                        seen[key] = val
                        nw += 1
                    if fn is None:
                        continue
                    ins = fn(eng)
                    if is_dma:
                        ins.then_inc(tok.sem, 16)
                    elif tok.needed:
                        ins.then_inc(tok.sem, 1)
                if e in self.QUEUES:
                    for t in self.dma_last[e]:
                        if t is not None and seen.get(id(t.sem), 0) < t.val:
                            eng.wait_ge(t.sem, t.val)
                            seen[id(t.sem)] = t.val
                stats[e] = (len(ops), nw)
            getattr(block, engobj[e])(body)
        return stats


def build_program(n_layers=4, layers=None):
    nc = bass.Bass("TRN2", target_bir_lowering=False)

    def din(name, shape, dt=F32):
        return nc.dram_tensor(name, shape, dt, kind="ExternalInput").ap()

    x_d = din("x", [T, D])
    p_d = din("p", [4, T, 256])
    pos_d = din("pos", [1, T], I32)
    gc_d = din("gcols", [128, GC_N])
    ident_d = din("ident", [128, 128])
    dec_d = din("dec", [128, 8, 128])
    mask01_d = din("mask01", [128, 128])
    maskb_d = din("maskb", [128, 4 * NT])
    ret_w_in = din("ret_w_in", [2, D, 6144])
    ret_w_out = din("ret_w_out", [2, 2048, D])
    mla_w_in = din("mla_w_in", [2, D, 3136])
    mla_w_uq = din("mla_w_uq", [2, 768, 3072])
    mla_w_uk = din("mla_w_uk", [2, 256, 2048])
    mla_w_uv = din("mla_w_uv", [2, 256, 2048])
    mla_w_out = din("mla_w_out", [2, 2048, D])
    ple_w_proj = din("ple_w_proj", [4, 256, D])
    ple_w_gate = din("ple_w_gate", [4, D, D])
    out_d = nc.dram_tensor("out", [T, D], F32, kind="ExternalOutput").ap()
    xT_d = nc.dram_tensor("xT_s", [128, 8, T], F32).ap()
    pT_d = nc.dram_tensor("pT_s", [4, 128, 2, T], BF16).ap()
    tab_d = nc.dram_tensor("tab_s", [4, 128, T], F32).ap()

    S = Sched(nc)
    st = ExitStack()

    def sb(name, shape, dt):
        return st.enter_context(nc.sbuf_tensor(name, shape, dt))

    def psum(name):
        return st.enter_context(nc.psum_tensor(name, [128, 512], F32))

    def mm(out, lhsT, rhs, start, stop, reads, writes):
        S.op("pe", lambda e: e.matmul(out, lhsT=lhsT, rhs=rhs, start=start, stop=stop), reads, writes)

    def tr(out, in_, ident, reads, writes):
        S.op("pe", lambda e: e.transpose(out=out, in_=in_, identity=ident), reads, writes)

    def act(out, in_, func, reads, writes, scale=1.0, bias=None):
        if bias is None:
            S.op("act", lambda e: e.activation(out=out, in_=in_, func=func, scale=scale), reads, writes)
        else:
            S.op("act", lambda e: e.activation(out=out, in_=in_, func=func, scale=scale, bias=bias),
                 reads, writes)

    def cp(eng, out, in_, reads, writes):
        if eng == "act":
            S.op("act", lambda e: e.copy(out=out, in_=in_), reads, writes)
        else:
            S.op(eng, lambda e: e.tensor_copy(out=out, in_=in_), reads, writes)

    def tt(eng, out, in0, in1, op, reads, writes):
        S.op(eng, lambda e: e.tensor_tensor(out=out, in0=in0, in1=in1, op=op), reads, writes)

    def ts(eng, out, in0, s1, s2, op0, op1, reads, writes):
        if s2 is None:
            S.op(eng, lambda e: e.tensor_scalar(out=out, in0=in0, scalar1=s1, scalar2=None, op0=op0),
                 reads, writes)
        else:
            S.op(eng, lambda e: e.tensor_scalar(out=out, in0=in0, scalar1=s1, scalar2=s2, op0=op0, op1=op1),
                 reads, writes)

    def stt(eng, out, in0, scalar, in1, op0, op1, reads, writes):
        eng = "dve"
        S.op(eng, lambda e: e.scalar_tensor_tensor(out=out, in0=in0, scalar=scalar, in1=in1, op0=op0, op1=op1),
             reads, writes)

    def memset(eng, ap, val, writes):
        S.op(eng, lambda e: e.memset(ap, val), (), writes)

    dumps = []

    def dump(name, ap, reads):
        if not DBG.get('dump'):
            return
        dd = nc.dram_tensor(name, list(ap.shape), ap.dtype, kind="ExternalOutput").ap()
        S.dma("sp", dd, ap, reads=reads)
        dumps.append(name)

    gc = sb("gc", [128, GC_N], F32); GCb = Buf()
    identf = sb("identf", [128, 128], F32); IDF = Buf()
    identb = sb("identb", [128, 128], BF16); IDB = Buf()
    onesb = sb("onesb", [128, 128], BF16); ONB = Buf()
    onesf = sb("onesf", [128, 128], F32); ONF = Buf()
    epst = sb("epst", [128, 1], F32); EPSb = Buf()
    xt = sb("xt", [128, 8, NT], F32); XT = bufs(8)
    hT = sb("hT", [128, 8, NT], BF16); HT = bufs(8)
    rstd = sb("rstd", [128, NT], F32); RSTD = Buf()
    NW = 4
    wp = [sb(f"wp{i}", [128, 8, 512], BF16) for i in range(NW)]
    WP = bufs(NW)
    ybuf = sb("ybuf", [128, 8, NT], F32); YB = bufs(8)
    ysq = sb("ysq", [128, 8, NT], BF16); YSQ = bufs(8)
    u = sb("u", [128, 16, NT], BF16); U = bufs(16)
    pt = sb("pt", [128, 2, NT], BF16); PTb = Buf()
    sig = sb("sig", [128, NT], F32); SIG = Buf()
    tA = sb("tA", [128, NT], F32); TA = Buf()
    tB = sb("tB", [128, NT], F32); TB = Buf()
    cs = sb("cs", [128, 2, NT], F32); CS = Buf()
    wr_t = sb("wr_t", [128, 6, 512], BF16); WRb = Buf()
    wrs_t = sb("wrs_t", [128, 6, 512], BF16); WRSb = Buf()
    UN_BYTES = 121 * 1024
    un = sb("un", [128, UN_BYTES // 2], BF16)

    class Carver:
        def __init__(self):
            self.off = 0

        def get(self, shape, dt):
            n = int(np.prod(shape[1:]))
            nb = n * (4 if dt == F32 else 2)
            nb_al = (nb + 63) // 64 * 64
            assert self.off + nb_al <= UN_BYTES, (self.off, nb_al)
            v = un[:, self.off // 2: self.off // 2 + nb // 2]
            self.off += nb_al
            if dt == F32:
                v = v.bitcast(F32)
            if len(shape) == 3:
                v = v.rearrange("p (a b) -> p a b", a=shape[1])
            return v

    pb = [psum(f"pb{i}") for i in range(8)]
    PB = bufs(8, "pb")
    pj_rr = [0]

    def pj():
        i = 1 + (pj_rr[0] % 2)
        pj_rr[0] += 1
        return pb[i], PB[i]

    wp_rr = [0]

    def wslot():
        i = wp_rr[0] % NW
        wp_rr[0] += 1
        return wp[i], WP[i]

    WSC = {}

    def wscr(key, shape):
        ap = nc.dram_tensor("wb_" + "_".join(str(k) for k in key), shape, BF16).ap()
        WSC[key] = (ap, Buf())
        return WSC[key]

    def conv_jobs(l):
        jobs = []

        def full(key, src2d):
            ap, B = wscr(key, list(src2d.shape))
            jobs.append(lambda: S.dma("pool", ap.rearrange("(a b) n -> a (b n)", a=128),
                                      src2d.rearrange("(a b) n -> a (b n)", a=128), writes=[B]))
        j = l // 2
        if l % 2 == 0:
            full(("ret_w_in", j), ret_w_in[j])
            full(("ret_w_out", j), ret_w_out[j])
        else:
            full(("mla_w_in", j), mla_w_in[j])
            full(("mla_w_uk", j), mla_w_uk[j])
            full(("mla_w_uv", j), mla_w_uv[j])
            apk, BK = wscr(("wk", j), [1024, 256])
            wi = mla_w_in[j]
            for (d0, s0, n) in ((0, 1024, 64), (64, 1024, 64), (128, 1056, 32), (160, 1024, 32),
                                (192, 1056, 32), (224, 1024, 32)):
                jobs.append(lambda d0=d0, s0=s0, n=n: S.dma("pool", apk[:, d0:d0 + n], wi[:, s0:s0 + n],
                                                           writes=[BK]))
            apr, BR = wscr(("wr", j), [768, 1024])
            aps, BS = wscr(("wrs", j), [768, 1024])
            uq4 = mla_w_uq[j].rearrange("r (h e) -> r h e", e=192)
            apr4 = apr.rearrange("r (h e) -> r h e", e=64)
            aps4 = aps.rearrange("r (h e) -> r h e", e=64)
            for kc in range(6):
                rs = slice(kc * 128, (kc + 1) * 128)
                jobs.append(lambda rs=rs: S.dma("pool", apr4[rs, :, :], uq4[rs, :, 128:192], writes=[BR]))
                jobs.append(lambda rs=rs: S.dma("pool", aps4[rs, :, 0:32], uq4[rs, :, 160:192], writes=[BS]))
                jobs.append(lambda rs=rs: S.dma("pool", aps4[rs, :, 32:64], uq4[rs, :, 128:160], writes=[BS]))
            full(("mla_w_uq", j), mla_w_uq[j])
            full(("mla_w_out", j), mla_w_out[j])
        full(("ple_w_gate", l), ple_w_gate[l])
        full(("ple_w_proj", l), ple_w_proj[l])
        return jobs

    pending = []

    def conv_step(n=1):
        for _ in range(n):
            if pending:
                pending.pop(0)()

    def wload(key, r0, nrows, cols, slot=None):
        w2d, WBUF = WSC[key]
        tile_, b = slot if slot is not None else wslot()
        kc = nrows // 128
        o = 0
        for (c0, ncols) in cols:
            src = w2d[r0:r0 + nrows, c0:c0 + ncols].rearrange("(k p) n -> p k n", p=128)
            S.dma("sp", tile_[:, 0:kc, o:o + ncols], src, reads=[WBUF], writes=[b])
            o += ncols
        return tile_, b

    if layers is None:
        layers = list(range(n_layers))
    if layers:
        for jb in conv_jobs(layers[0]):
            jb()
    S.dma("sp", gc[:], gc_d, writes=[GCb])
    S.dma("sp", identf[:], ident_d, writes=[IDF])
    S.dma("pool", identb[:], ident_d, writes=[IDB])
    memset("dve", onesb[:], 1.0, [ONB])
    memset("dve", onesf[:], 1.0, [ONF])
    memset("dve", epst[:], EPS, [EPSb])

    C1 = 6.28125
    C2 = float(2 * np.pi - 6.28125)
    PCH = 2048
    cvp = Carver()
    posi = cvp.get([128, PCH], F32).bitcast(I32); POSI = Buf()
    posf = cvp.get([128, PCH], F32); POSF = Buf()
    angb = cvp.get([128, PCH], F32); ANG = Buf()
    a2b = cvp.get([128, PCH], F32); A2 = Buf()
    kfb = cvp.get([128, PCH], F32); KF = Buf()
    kib = cvp.get([128, PCH], F32).bitcast(I32); KI = Buf()
    for ch in range(T // PCH):
        tsl = slice(ch * PCH, (ch + 1) * PCH)
        S.dma("sp", posi[:], pos_d[0:1, tsl].partition_broadcast(128), writes=[POSI])
        cp("dve", posf[:], posi[:], [POSI], [POSF])
        for sset in range(2):
            ts("dve", angb[:], posf[:], gc[:, GC_INVF + sset:GC_INVF + sset + 1], None, ALU.mult, None,
               [POSF, GCb], [ANG])
            for which in range(2):
                shift = float(np.pi / 2) if which == 0 else 0.0
                ts("dve", a2b[:], angb[:], shift, None, ALU.add, None, [ANG], [A2])
                ts("dve", kfb[:], a2b[:], float(1.0 / (2 * np.pi)), None, ALU.mult, None, [A2], [KF])
                cp("dve", kib[:], kfb[:], [KF], [KI])
                cp("dve", kfb[:], kib[:], [KI], [KF])
                stt("dve", a2b[:], kfb[:], -C1, a2b[:], ALU.mult, ALU.add, [KF, A2], [A2])
                stt("dve", a2b[:], kfb[:], -C2, a2b[:], ALU.mult, ALU.add, [KF, A2], [A2])
                ts("dve", kfb[:], a2b[:], float(np.pi), float(-2 * np.pi), ALU.is_gt, ALU.mult, [A2], [KF])
                tt("dve", a2b[:], a2b[:], kfb[:], ALU.add, [A2, KF], [A2])
                ts("dve", a2b[:], a2b[:], -3.1415925, 3.1415925, ALU.max, ALU.min, [A2], [A2])
                act(a2b[:], a2b[:], AF.Sin, [A2], [A2])
                if sset == 1 and which == 1:
                    ts("dve", a2b[:], a2b[:], gc[:, GC_SGN:GC_SGN + 1], None, ALU.mult, None, [A2, GCb], [A2])
                S.dma("pool", tab_d[sset * 2 + which, :, tsl], a2b[:], reads=[A2])

    xin = cvp.get([128, D], F32); XIN = Buf()
    pin = cvp.get([128, 4, 256], F32); PIN = Buf()
    xst = cvp.get([128, 8, 512], F32); XST = Buf()
    pst = cvp.get([128, 8, 512], BF16); PST = Buf()
    for it in range(T // 512):
        for bl in range(4):
            t0 = it * 512 + bl * 128
            S.dma("sp", xin[:], x_d[t0:t0 + 128, :], writes=[XIN])
            S.dma("sp", pin[:], p_d[:, t0:t0 + 128, :].rearrange("l t e -> t l e"), writes=[PIN])
            for half in range(2):
                bank, B = pb[3 + half], PB[3 + half]
                for c4 in range(4):
                    c = half * 4 + c4
                    tr(bank[:, c4 * 128:(c4 + 1) * 128], xin[:, c * 128:(c + 1) * 128], identf[:],
                       [XIN, IDF], [B])
                cp("act" if half == 0 else "dve",
                   xst[:, half * 4:half * 4 + 4, bl * 128:(bl + 1) * 128],
                   bank[:].rearrange("p (c n) -> p c n", c=4), [B], [XST])
            for half in range(2):
                bank, B = pb[5 + half], PB[5 + half]
                for c4 in range(4):
                    c = half * 4 + c4
                    tr(bank[:, c4 * 128:(c4 + 1) * 128], pin[:, c // 2, (c % 2) * 128:(c % 2 + 1) * 128],
                       identf[:], [PIN, IDF], [B])
                cp("act" if half == 1 else "dve",
                   pst[:, half * 4:half * 4 + 4, bl * 128:(bl + 1) * 128],
                   bank[:].rearrange("p (c n) -> p c n", c=4), [B], [PST])
        S.dma("pool", xT_d[:, :, it * 512:(it + 1) * 512], xst[:], reads=[XST])
        for l in range(4):
            S.dma("pool", pT_d[l, :, :, it * 512:(it + 1) * 512], pst[:, l * 2:l * 2 + 2, :], reads=[PST])
    XTD = Buf()
    XTD_t = bufs(NTILES)
    S.barrier()

    def prenorm(l, ti):
        tsl = slice(ti * NT, (ti + 1) * NT)
        conv_step(2)
        S.dma("sp", xt[:], xT_d[:, :, tsl], reads=[XTD_t[ti]], writes=XT)
        act(hT[:].rearrange("p c n -> p (c n)"), xt[:].rearrange("p c n -> p (c n)"), AF.Square, XT, HT)
        for c in range(8):
            mm(pb[0][:, 0:NT], onesb[:], hT[:, c, :], c == 0, c == 7, [ONB, HT[c]], [PB[0]])
        act(rstd[:], pb[0][:, 0:NT], AF.Sqrt, [PB[0], EPSb], [RSTD], scale=1.0 / D, bias=epst[:])
        S.op("dve", lambda e: e.reciprocal(out=rstd[:], in_=rstd[:]), [RSTD], [RSTD])
        for c in range(8):
            stt("dve" if c % 2 == 0 else "pool", hT[:, c, :], xt[:, c, :],
                gc[:, GC_PRE + l * 8 + c:GC_PRE + l * 8 + c + 1], rstd[:], ALU.mult, ALU.mult,
                [XT[c], GCb, RSTD], [HT[c]])

    def tail(l, ti, wout_key, last):
        tsl = slice(ti * NT, (ti + 1) * NT)
        for og in range(2):
            wa, WA = wload(wout_key, 0, 1024, [(og * 512, 512)])
            wb, WB = wload(wout_key, 1024, 1024, [(og * 512, 512)])
            for o4 in range(4):
                oc = og * 4 + o4
                bank, B = pj()
                KK = DBG.get('kk', 16)
                for kc in range(KK):
                    w_, W_ = (wa, WA) if kc < 8 else (wb, WB)
                    mm(bank[:, 0:NT], w_[:, kc % 8, o4 * 128:(o4 + 1) * 128], u[:, kc, :], kc == 0, kc == KK - 1,
                       [W_, U[kc]], [B])
                cp("act", ybuf[:, oc, :], bank[:, 0:NT], [B], [YB[oc]])
                act(ysq[:, oc, :], bank[:, 0:NT], AF.Square, [B], [YSQ[oc]])
        if DBG['stop'] < 6:
            return
        for c in range(8):
            mm(pb[0][:, 0:NT], onesb[:], ysq[:, c, :], c == 0, c == 7, [ONB, YSQ[c]], [PB[0]])
        act(rstd[:], pb[0][:, 0:NT], AF.Sqrt, [PB[0], EPSb], [RSTD], scale=1.0 / D, bias=epst[:])
        S.op("dve", lambda e: e.reciprocal(out=rstd[:], in_=rstd[:]), [RSTD], [RSTD])
        for c in range(8):
            e1 = "dve" if c % 2 == 0 else "pool"
            stt(e1, ybuf[:, c, :], ybuf[:, c, :], gc[:, GC_POST + l * 8 + c:GC_POST + l * 8 + c + 1], rstd[:],
                ALU.mult, ALU.mult, [YB[c], GCb, RSTD], [YB[c]])
            tt(e1, xt[:, c, :], xt[:, c, :], ybuf[:, c, :], ALU.add, [XT[c], YB[c]], [XT[c]])
            cp("act", hT[:, c, :], xt[:, c, :], [XT[c]], [HT[c]])
        if DBG['stop'] < 7:
            return
        S.dma("sp", pt[:], pT_d[l, :, :, tsl], writes=[PTb])
        for og in range(2):
            wg, WG = wload(("ple_w_gate", l), 0, 1024, [(og * 512, 512)])
            wq, WQ = wload(("ple_w_proj", l), 0, 256, [(og * 512, 512)])
            for o4 in range(4):
                oc = og * 4 + o4
                bank, B = pj()
                for kc in range(8):
                    mm(bank[:, 0:NT], wg[:, kc, o4 * 128:(o4 + 1) * 128], hT[:, kc, :], kc == 0, kc == 7,
                       [WG, HT[kc]], [B])
                for kc in range(2):
                    mm(bank[:, NT:2 * NT], wq[:, kc, o4 * 128:(o4 + 1) * 128], pt[:, kc, :], kc == 0, kc == 1,
                       [WQ, PTb], [B])
                act(sig[:], bank[:, 0:NT], AF.Sigmoid, [B], [SIG])
                tt("dve", tA[:], bank[:, NT:2 * NT], sig[:], ALU.mult, [B, SIG], [TA])
                tt("pool", xt[:, oc, :], xt[:, oc, :], tA[:], ALU.add, [XT[oc], TA], [XT[oc]])
        if DBG['stop'] < 8:
            return
        if not last:
            S.dma("pool", xT_d[:, :, tsl], xt[:], reads=XT, writes=[XTD_t[ti]])
        else:
            otile = ybuf[:].rearrange("p c n -> p (c n)")[:, 0:1024]
            for bl in range(NBLK):
                for half in range(2):
                    bank, B = pb[3 + half], PB[3 + half]
                    for c4 in range(4):
                        c = half * 4 + c4
                        tr(bank[:, c4 * 128:(c4 + 1) * 128], xt[:, c, bl * 128:(bl + 1) * 128], identf[:],
                           [XT[c], IDF], [B])
                    cp("act" if half == 0 else "dve", otile[:, half * 512:(half + 1) * 512], bank[:],
                       [B], YB)
                t0 = ti * NT + bl * 128
                S.dma("pool", out_d[t0:t0 + 128, :], otile, reads=YB)

    def ret_layer(l, last):
        j = l // 2
        w_in = ("ret_w_in", j)
        cv = Carver()
        dec = cv.get([128, 8, 128], F32); DEC = Buf()
        mask01 = cv.get([128, 128], F32); M01 = Buf()
        stage = cv.get([128, 2, NT], F32); STG = bufs(2)
        qh2 = [cv.get([128, 2, NT], BF16) for _ in range(2)]; QH2 = bufs(2)
        kh2 = [cv.get([128, 2, NT], BF16) for _ in range(2)]; KH2 = bufs(2)
        ktok2 = [cv.get([128, NBLK, 256], BF16) for _ in range(2)]; KTOK2 = [bufs(NBLK) for _ in range(2)]
        vt2 = [cv.get([128, NBLK, 512], BF16) for _ in range(2)]; VT2 = [bufs(NBLK) for _ in range(2)]
        sg2 = [cv.get([128, 4, NT], BF16) for _ in range(2)]; SG2 = [bufs(4) for _ in range(2)]
        Sst = [cv.get([128, 2, 512], F32) for _ in range(4)]; SST = [bufs(2) for _ in range(4)]
        Sbf = [cv.get([128, 2, 512], BF16) for _ in range(4)]; SBF = [bufs(2) for _ in range(4)]
        am2 = [cv.get([128, 128], BF16) for _ in range(2)]; AM2 = bufs(2)
        oraw2 = [cv.get([128, 4, NT], BF16) for _ in range(2)]; ORAW2 = bufs(2)
        osq2 = [cv.get([128, 4, NT], BF16) for _ in range(2)]; OSQ2 = bufs(2)
        rso = cv.get([128, NT], F32); RSO = Buf()
        S.dma("sp", dec, dec_d, writes=[DEC])
        S.dma("sp", mask01, mask01_d, writes=[M01])
        for h in range(4):
            for dc in range(2):
                memset("pool", Sst[h][:, dc, :], 0.0, [SST[h][dc]])
                memset("pool", Sbf[h][:, dc, :], 0.0, [SBF[h][dc]])
        pbT = pb[7][:].bitcast(BF16)

        def proj_groups(h):
            s_ = h % 2
            qh, QH, kh, KH = qh2[s_], QH2[s_], kh2[s_], KH2[s_]
            ktok, KTOK, vt, VT, sg, SG = ktok2[s_], KTOK2[s_], vt2[s_], VT2[s_], sg2[s_], SG2[s_]
            wl = {}

            def g_qk(which, half):
                if which == 0 and half == 0:
                    wl['qk'] = wload(w_in, 0, 1024, [(h * 256, 256), (1024 + h * 256, 256)])
                    wl['v'] = wload(w_in, 0, 1024, [(2048 + h * 512, 512)])
                    wl['g'] = wload(w_in, 0, 1024, [(4096 + h * 512, 512)])
                wqk, WQK = wl['qk']
                dst, DST = (qh, QH) if which == 0 else (kh, KH)
                bank, B = pj()
                c0 = which * 256 + half * 128
                for kc in range(8):
                    mm(bank[:, 0:NT], wqk[:, kc, c0:c0 + 128], hT[:, kc, :], kc == 0, kc == 7,
                       [WQK, HT[kc]], [B])
                tt("dve", stage[:, half, :].rearrange("p (c n) -> p c n", c=NBLK),
                   bank[:, 0:NT].rearrange("p (c n) -> p c n", c=NBLK),
                   dec[:, which * 4 + h, :].unsqueeze(1).to_broadcast([128, NBLK, 128]),
                   ALU.mult, [B, DEC], [STG[half]])
                if half == 1:
                    tt("dve", tA[:], stage[:, 0, :], cs[:, 0, :], ALU.mult, [STG[0], CS], [TA])
                    tt("pool", tB[:], stage[:, 1, :], cs[:, 1, :], ALU.mult, [STG[1], CS], [TB])
                    tt("dve", dst[:, 0, :], tA[:], tB[:], ALU.subtract, [TA, TB], [DST])
                    tt("pool", tA[:], stage[:, 1, :], cs[:, 0, :], ALU.mult, [STG[1], CS], [TA])
                    tt("dve", tB[:], stage[:, 0, :], cs[:, 1, :], ALU.mult, [STG[0], CS], [TB])
                    tt("pool", dst[:, 1, :], tA[:], tB[:], ALU.add, [TA, TB], [DST])

            def g_ktok():
                for c in range(NBLK):
                    for dc in range(2):
                        tr(pbT[:, dc * 128:(dc + 1) * 128], kh[:, dc, c * 128:(c + 1) * 128], identb[:],
                           [KH, IDB], [PB[7]])
                    S.op("act", lambda e, c=c: e.mul(out=ktok[:, c, :], in_=pbT[:, 0:256], mul=float(CDEC[h])),
                         [PB[7]], [KTOK[c]])

            def g_v(c):
                wv, WV = wl['v']
                bank, B = pj()
                for kc in range(8):
                    mm(bank[:, 0:512], hT[:, kc, c * 128:(c + 1) * 128], wv[:, kc, :], kc == 0, kc == 7,
                       [WV, HT[kc]], [B])
                cp("act", vt[:, c, :], bank[:, 0:512], [B], [VT[c]])

            def g_gate(vc):
                wg, WG = wl['g']
                bank, B = pj()
                for kc in range(8):
                    mm(bank[:, 0:NT], wg[:, kc, vc * 128:(vc + 1) * 128], hT[:, kc, :], kc == 0, kc == 7,
                       [WG, HT[kc]], [B])
                act(sg[:, vc, :], bank[:, 0:NT], AF.Silu, [B], [SG[vc]])

            gl = [lambda: g_qk(0, 0), lambda: g_qk(0, 1), lambda: g_qk(1, 0), lambda: g_qk(1, 1)]
            gl += [lambda c=c: g_v(c) for c in range(NBLK)]
            gl += [g_ktok]
            gl += [lambda vc=vc: g_gate(vc) for vc in range(4)]
            return gl

        def scan_steps(h, ti):
            s_ = h % 2
            qh, QH, kh, KH = qh2[s_], QH2[s_], kh2[s_], KH2[s_]
            ktok, KTOK, vt, VT = ktok2[s_], KTOK2[s_], vt2[s_], VT2[s_]
            oraw, ORAW, osq, OSQ = oraw2[s_], ORAW2[s_], osq2[s_], OSQ2[s_]
            steps = []
            for c in range(NBLK):
                csl = slice(c * 128, (c + 1) * 128)
                am, AM = am2[c % 2], AM2[c % 2]
                sb_i = 3 + (c % 2)
                first = (ti == 0 and c == 0)

                def s_at(c=c, csl=csl, am=am, AM=AM, sb_i=sb_i):
                    for dc in range(2):
                        mm(pb[sb_i][:, 0:128], kh[:, dc, csl], qh[:, dc, csl], dc == 0, dc == 1,
                           [KH, QH], [PB[sb_i]])
                    tt("dve", am, pb[sb_i][:, 0:128], mask01, ALU.mult, [PB[sb_i], M01], [AM])

                def s_o(c=c, csl=csl, am=am, AM=AM, first=first):
                    ob, OB = pb[5], PB[5]
                    for vc in range(4):
                        mm(ob[:, vc * 128:(vc + 1) * 128], vt[:, c, vc * 128:(vc + 1) * 128], am, True, first,
                           [VT[c], AM], [OB])
                        if not first:
                            for dc in range(2):
                                mm(ob[:, vc * 128:(vc + 1) * 128], Sbf[h][:, dc, vc * 128:(vc + 1) * 128],
                                   qh[:, dc, csl], False, dc == 1, [SBF[h][dc], QH], [OB])
                    obv = ob[:].rearrange("p (v n) -> p v n", v=4)
                    cp("act", oraw[:, :, csl], obv, [OB], [ORAW])
                    act(osq[:, :, csl], obv, AF.Square, [OB], [OSQ])

                def s_state(c=c):
                    for dc in range(2):
                        sbk, SBK = pb[6], PB[6]
                        mm(sbk[:, 0:512], ktok[:, c, dc * 128:(dc + 1) * 128], vt[:, c, :], True, True,
                           [KTOK[c], VT[c]], [SBK])
                        stt("dve", Sst[h][:, dc, :], Sst[h][:, dc, :], float(CDEC[h]), sbk[:, 0:512],
                            ALU.mult, ALU.add, [SST[h][dc], SBK], [SST[h][dc]])
                        cp("pool", Sbf[h][:, dc, :], Sst[h][:, dc, :], [SST[h][dc]], [SBF[h][dc]])
                steps += [s_at, s_o, s_state]
            return steps

        def norm_head(h):
            s_ = h % 2
            oraw, ORAW, osq, OSQ, sg, SG = oraw2[s_], ORAW2[s_], osq2[s_], OSQ2[s_], sg2[s_], SG2[s_]
            for vc in range(4):
                mm(pb[0][:, 0:NT], onesb[:], osq[:, vc, :], vc == 0, vc == 3, [ONB, OSQ], [PB[0]])
            act(rso, pb[0][:, 0:NT], AF.Sqrt, [PB[0], EPSb], [RSO], scale=1.0 / 512, bias=epst[:])
            S.op("dve", lambda e: e.reciprocal(out=rso, in_=rso), [RSO], [RSO])
            for vc in range(4):
                gcol = GC_GN + j * 16 + h * 4 + vc
                stt("dve", oraw[:, vc, :], oraw[:, vc, :], gc[:, gcol:gcol + 1], rso, ALU.mult, ALU.mult,
                    [ORAW, GCb, RSO], [ORAW])
                tt("dve", u[:, h * 4 + vc, :], oraw[:, vc, :], sg[:, vc, :], ALU.mult,
                   [ORAW, SG[vc]], [U[h * 4 + vc]])

        for ti in range(NTILES if DBG['tiles'] is None else DBG['tiles']):
            tsl = slice(ti * NT, (ti + 1) * NT)
            prenorm(l, ti)
            S.dma("sp", cs[:, 0, :], tab_d[0, :, tsl], writes=[CS])
            S.dma("sp", cs[:, 1, :], tab_d[1, :, tsl], writes=[CS])
            for g in proj_groups(0):
                g()
            for h in range(4):
                P = proj_groups(h + 1) if h + 1 < 4 else []
                Sx = scan_steps(h, ti)
                while P or Sx:
                    for _ in range(2):
                        if P:
                            P.pop(0)()
                    if Sx:
                        Sx.pop(0)()
                norm_head(h)
            if DBG['stop'] < 5:
                continue
            tail(l, ti, ("ret_w_out", j), last)

    def mla_layer(l, last):
        j = l // 2
        w_in = ("mla_w_in", j)
        w_uq = ("mla_w_uq", j)
        cv = Carver()
        KT = cv.get([128, 3, T], BF16); KTB = bufs(T // 128)
        V = cv.get([128, T // 128, 256], BF16); VB = bufs(T // 128)
        cqn = cv.get([128, 6, NT], BF16); CQN = bufs(6)
        wukT = cv.get([128, 16, 256], BF16); WUKT = Buf()
        wuv = cv.get([128, 2, 2048], BF16); WUV = Buf()
        qrz = [cv.get([128, 2 * NT], BF16) for _ in range(2)]; QRZ = bufs(2)
        qn = cv.get([128, NT], BF16); QN = Buf()
        qc = [cv.get([128, 2, 2 * NT], BF16) for _ in range(2)]; QC = bufs(2)
        Pt = [cv.get([128, 2 * NT], BF16) for _ in range(3)]; PT3 = bufs(3)
        acc = [cv.get([128, 2 * NT], F32) for _ in range(2)]; ACC = bufs(2)
        rinv = cv.get([128, 2 * NT], F32); RINV = Buf()
        olat = cv.get([128, 2, 2 * NT], BF16); OLAT = Buf()
        sgm = [cv.get([128, 2 * NT], BF16) for _ in range(2)]; SGM = bufs(2)
        maskb = cv.get([128, 2, 2 * NT], BF16); MKB = Buf()
        for s_ in range(2):
            memset("pool", qrz[s_], 0.0, [QRZ[s_]])
        pbT = pb[7][:].bitcast(BF16)
        S.dma("pool", maskb.rearrange("p a b -> p (a b)"), maskb_d, writes=[MKB])
        S.dma("sp", wuv, WSC[("mla_w_uv", j)][0].rearrange("(k p) n -> p k n", p=128),
              reads=[WSC[("mla_w_uv", j)][1]], writes=[WUV])
        wl, WL = wslot()
        wlv = wl[:].rearrange("p a b -> p (a b)").rearrange("p (k n) -> p k n", k=2)
        S.dma("sp", wlv, WSC[("mla_w_uk", j)][0].rearrange("(k p) n -> p k n", p=128),
              reads=[WSC[("mla_w_uk", j)][1]], writes=[WL])
        for h in range(16):
            for cc in range(2):
                tr(pbT[:, cc * 128:(cc + 1) * 128], wlv[:, cc, h * 128:(h + 1) * 128], identb[:],
                   [WL, IDB], [PB[7]])
            cp("act", wukT[:, h, :], pbT[:, 0:256], [PB[7]], [WUKT])
        pcount = [0]
        ptcount = [0]
        for ti in range(NTILES if DBG['tiles'] is None else DBG['tiles']):
            tsl = slice(ti * NT, (ti + 1) * NT)
            prenorm(l, ti)
            S.dma("sp", cs[:, 0, :], tab_d[2, :, tsl], writes=[CS])
            S.dma("sp", cs[:, 1, :], tab_d[3, :, tsl], writes=[CS])
            for og in range(2):
                w_, W_ = wload(w_in, 0, 1024, [(og * 512, 512)])
                for o4 in range(4):
                    oc = og * 4 + o4
                    bank, B = pj()
                    for kc in range(8):
                        mm(bank[:, 0:NT], w_[:, kc, o4 * 128:(o4 + 1) * 128], hT[:, kc, :], kc == 0, kc == 7,
                           [W_, HT[kc]], [B])
                    cp("act", ybuf[:, oc, :], bank[:, 0:NT], [B], [YB[oc]])
                    act(ysq[:, oc, :], bank[:, 0:NT], AF.Square, [B], [YSQ[oc]])
            for c in range(6):
                mm(pb[0][:, 0:NT], onesb[:], ysq[:, c, :], c == 0, c == 5, [ONB, YSQ[c]], [PB[0]])
            act(rstd[:], pb[0][:, 0:NT], AF.Sqrt, [PB[0], EPSb], [RSTD], scale=1.0 / 768, bias=epst[:])
            S.op("dve", lambda e: e.reciprocal(out=rstd[:], in_=rstd[:]), [RSTD], [RSTD])
            for c in range(6):
                gcol = GC_QN + j * 6 + c
                stt("dve" if c % 2 == 0 else "pool", cqn[:, c, :], ybuf[:, c, :], gc[:, gcol:gcol + 1], rstd[:],
                    ALU.mult, ALU.mult, [YB[c], GCb, RSTD], [CQN[c]])
            kb0 = ti * NBLK
            for c in range(2):
                mm(pb[0][:, 0:NT], onesb[:], ysq[:, 6 + c, :], c == 0, c == 1, [ONB, YSQ[6 + c]], [PB[0]])
            act(rstd[:], pb[0][:, 0:NT], AF.Sqrt, [PB[0], EPSb], [RSTD], scale=1.0 / 256, bias=epst[:])
            S.op("dve", lambda e: e.reciprocal(out=rstd[:], in_=rstd[:]), [RSTD], [RSTD])
            for c in range(2):
                gcol = GC_KVN + j * 2 + c
                stt("dve" if c % 2 == 0 else "pool", KT[:, c, tsl], ybuf[:, 6 + c, :], gc[:, gcol:gcol + 1], rstd[:],
                    ALU.mult, ALU.mult, [YB[6 + c], GCb, RSTD], [KTB[kb0 + b] for b in range(NBLK)])
            for b in range(NBLK):
                for c in range(2):
                    tr(pbT[:, c * 128:(c + 1) * 128], KT[:, c, (kb0 + b) * 128:(kb0 + b + 1) * 128], identb[:],
                       [KTB[kb0 + b], IDB], [PB[7]])
                cp("act", V[:, kb0 + b, :], pbT[:, 0:256], [PB[7]], [VB[kb0 + b]])
            wk, WK = wload(("wk", j), 0, 1024, [(0, 256)])
            bank, B = pj()
            for kc in range(8):
                mm(bank[:, 0:NT], wk[:, kc, 0:128], hT[:, kc, :], kc == 0, kc == 7, [WK, HT[kc]], [B])
            for kc in range(8):
                mm(bank[:, NT:2 * NT], wk[:, kc, 128:256], hT[:, kc, :], kc == 0, kc == 7, [WK, HT[kc]], [B])
            tt("dve", tA[:], bank[:, 0:NT], cs[:, 0, :], ALU.mult, [B, CS], [TA])
            tt("dve", tB[:], bank[:, NT:2 * NT], cs[:, 1, :], ALU.mult, [B, CS], [TB])
            tt("pool", KT[:, 2, tsl], tA[:], tB[:], ALU.add, [TA, TB], [KTB[kb0 + b] for b in range(NBLK)])
            if ti == DBG.get('dump_ti', 0):
                dump("d_hT", hT[:], HT)
                dump("d_cqn", cqn, CQN)
                dump("d_KT", KT[:, :, 0:(ti + 1) * NT], KTB[0:(ti + 1) * NBLK])
                dump("d_V", V[:, 0:(ti + 1) * NBLK, :], VB[0:(ti + 1) * NBLK])
                dump("d_cs", cs[:], [CS])
            nkt = 2 * (ti + 1)
            pctx = {}
            gstate = {}

            def prep(m):
                hg, m_loc = divmod(m, 4)
                s_ = m % 2
                if m_loc == 0:
                    for (dst, DSTB, key) in ((wr_t, WRb, ("wr", j)), (wrs_t, WRSb, ("wrs", j))):
                        S.dma("sp", dst[:], WSC[key][0][:, hg * 512:(hg + 1) * 512].rearrange(
                            "(k p) n -> p k n", p=128), reads=[WSC[key][1]], writes=[DSTB])
                if m % 2 == 0:
                    gstate['wgt'] = wload(w_in, 0, 1024, [(1088 + 2 * m * 128, 512)])
                wgt, WGT = gstate['wgt']
                bank, B = pj()
                for kc in range(6):
                    mm(bank[:, 0:NT], wr_t[:, kc, m_loc * 128:(m_loc + 1) * 128], cqn[:, kc, :],
                       kc == 0, kc == 5, [WRb, CQN[kc]], [B])
                for kc in range(6):
                    mm(bank[:, NT:2 * NT], wrs_t[:, kc, m_loc * 128:(m_loc + 1) * 128], cqn[:, kc, :],
                       kc == 0, kc == 5, [WRSb, CQN[kc]], [B])
                tt("dve", tA[:], bank[:, 0:NT], cs[:, 0, :], ALU.mult, [B, CS], [TA])
                tt("dve", tB[:], bank[:, NT:2 * NT], cs[:, 1, :], ALU.mult, [B, CS], [TB])
                tt("pool", qrz[s_][0:64, 0:NT], tA[0:64, :], tB[0:64, :], ALU.add, [TA, TB], [QRZ[s_]])
                tt("pool", qrz[s_][64:128, NT:2 * NT], tA[64:128, :], tB[64:128, :], ALU.add, [TA, TB], [QRZ[s_]])
                wn, WN = wload(w_uq, 0, 768, [(2 * m * 192, 128), ((2 * m + 1) * 192, 128)])
                for hh in range(2):
                    h = 2 * m + hh
                    hsl = slice(hh * NT, (hh + 1) * NT)
                    bank, B = pj()
                    for kc in range(6):
                        mm(bank[:, 0:NT], wn[:, kc, hh * 128:(hh + 1) * 128], cqn[:, kc, :],
                           kc == 0, kc == 5, [WN, CQN[kc]], [B])
                    hl = h % 4
                    for kc in range(8):
                        mm(bank[:, NT:2 * NT], wgt[:, kc, hl * 128:(hl + 1) * 128], hT[:, kc, :], kc == 0,
                           kc == 7, [WGT, HT[kc]], [B])
                    cp("act", qn, bank[:, 0:NT], [B], [QN])
                    act(sgm[s_][:, hsl], bank[:, NT:2 * NT], AF.Silu, [B], [SGM[s_]])
                    bank, B = pj()
                    for cc in range(2):
                        mm(bank[:, cc * NT:(cc + 1) * NT], wukT[:, h, cc * 128:(cc + 1) * 128], qn,
                           True, True, [WUKT, QN], [B])
                    cp("dve", qc[s_][:, :, hsl], bank[:].rearrange("p (a b) -> p a b", a=2), [B], [QC[s_]])
                pctx[m] = dict(sb={}, pt={})

            def emit_qk(m, kt):
                c = pctx[m]
                if kt in c['sb']:
                    return
                s_ = m % 2
                sbi = 3 + (pcount[0] % 2)
                pcount[0] += 1
                c['sb'][kt] = sbi
                sbank, SBK = pb[sbi], PB[sbi]
                diag = kt >= 2 * ti
                ksl = slice(kt * 128, (kt + 1) * 128)
                mm(sbank[:], KT[:, 0, ksl], qc[s_][:, 0, :], True, False, [KTB[kt], QC[s_]], [SBK])
                mm(sbank[:], KT[:, 1, ksl], qc[s_][:, 1, :], False, False, [KTB[kt], QC[s_]], [SBK])
                mm(sbank[:], KT[:, 2, ksl], qrz[s_], False, not diag, [KTB[kt], QRZ[s_]], [SBK])
                if diag:
                    mm(sbank[:], identb[:], maskb[:, kt - 2 * ti, :], False, True, [IDB, MKB], [SBK])

            def emit_exp_acc(m, kt):
                c = pctx[m]
                sbi = c['sb'][kt]
                pti = ptcount[0] % 3
                ptcount[0] += 1
                c['pt'][kt] = pti
                s_ = m % 2
                act(Pt[pti], pb[sbi][:], AF.Exp, [PB[sbi]], [PT3[pti]], scale=SM_SCALE)
                if kt == 0:
                    cp("dve", acc[s_], Pt[pti], [PT3[pti]], [ACC[s_]])
                else:
                    tt("dve", acc[s_], acc[s_], Pt[pti], ALU.add, [ACC[s_], PT3[pti]], [ACC[s_]])

            def emit_pv(m, kt):
                pti = pctx[m]['pt'][kt]
                for cc in range(2):
                    mm(pb[5 + cc][:], V[:, kt, cc * 128:(cc + 1) * 128], Pt[pti], kt == 0, kt == nkt - 1,
                       [VB[kt], PT3[pti]], [PB[5 + cc]])

            def finish_den(m):
                s_ = m % 2
                mm(pb[0][:], onesf[:], acc[s_], True, True, [ONF, ACC[s_]], [PB[0]])
                S.op("dve", lambda e: e.reciprocal(out=rinv, in_=pb[0][:]), [PB[0]], [RINV])
                for cc in range(2):
                    tt("dve", olat[:, cc, :], pb[5 + cc][:], rinv, ALU.mult, [PB[5 + cc], RINV], [OLAT])

            def finish_uv(m):
                s_ = m % 2
                bank, B = pj()
                for hh in range(2):
                    h = 2 * m + hh
                    for cc in range(2):
                        mm(bank[:, hh * NT:(hh + 1) * NT], wuv[:, cc, h * 128:(h + 1) * 128],
                           olat[:, cc, hh * NT:(hh + 1) * NT], cc == 0, cc == 1, [WUV, OLAT], [B])
                tt("dve", u[:, 2 * m:2 * m + 2, :].rearrange("p a b -> p (a b)"), bank[:], sgm[s_], ALU.mult,
                   [B, SGM[s_]], [U[2 * m], U[2 * m + 1]])

            prep(0)
            emit_qk(0, 0)
            for m in range(8):
                for kt in range(nkt):
                    lastk = (kt == nkt - 1)
                    if not lastk:
                        emit_qk(m, kt + 1)
                    emit_exp_acc(m, kt)
                    if lastk and m + 1 < 8:
                        prep(m + 1)
                    emit_pv(m, kt)
                if m + 1 < 8:
                    emit_qk(m + 1, 0)
                finish_den(m)
                if m + 1 < 8:
                    emit_qk(m + 1, 1)
                finish_uv(m)
            if ti == DBG.get('dump_ti', 0):
                dump("d_u", u[:], U)
            tail(l, ti, ("mla_w_out", j), last)

    if n_layers == 0:
        for ti in range(NTILES):
            tsl = slice(ti * NT, (ti + 1) * NT)
            S.dma("sp", xt[:], xT_d[:, :, tsl], writes=XT)
            otile = ybuf[:].rearrange("p c n -> p (c n)")[:, 0:1024]
            for bl in range(NBLK):
                for half in range(2):
                    bank, B = pb[3 + half], PB[3 + half]
                    for c4 in range(4):
                        c = half * 4 + c4
                        tr(bank[:, c4 * 128:(c4 + 1) * 128], xt[:, c, bl * 128:(bl + 1) * 128], identf[:],
                           [XT[c], IDF], [B])
                    cp("act" if half == 0 else "dve", otile[:, half * 512:(half + 1) * 512], bank[:],
                       [B], YB)
                t0 = ti * NT + bl * 128
                S.dma("pool", out_d[t0:t0 + 128, :], otile, reads=YB)
    for li, l in enumerate(layers):
        last = (li == len(layers) - 1)
        if li + 1 < len(layers):
            pending.extend(conv_jobs(layers[li + 1]))
        if l % 2 == 0:
            ret_layer(l, last)
        else:
            mla_layer(l, last)
        conv_step(len(pending))
        S.barrier()

    stats = S.emit(st)
    st.close()
    return nc, stats


def _consts():
    ident = np.eye(128, dtype=np.float32)
    i = np.arange(128, dtype=np.float64)
    dec = np.zeros((128, 8, 128), np.float32)
    for h in range(4):
        lg = np.log(GAMMAS[h])
        dec[:, h, :] = np.exp((i + 1.0) * lg)[None, :]
        dec[:, 4 + h, :] = (np.exp(-(i + 1.0) * lg) * (256.0 ** -0.5))[None, :]
    jj = np.arange(128)[:, None]
    ii = np.arange(128)[None, :]
    mask01 = (ii >= jj).astype(np.float32)
    qi = np.arange(NT)[None, :]
    maskb = np.zeros((128, 2, 2, NT), np.float32)
    for o in range(2):
        maskb[:, o, :, :] = np.where((128 * o + jj) <= qi, 0.0, -30000.0)[:, None, :]
    return ident, dec, mask01, maskb.reshape(128, 4 * NT)


def _gcols(inp):
    g = np.zeros((128, GC_N), np.float32)

    def put(col0, vec):
        k = vec.shape[0] // 128
        g[:, col0:col0 + k] = vec.reshape(k, 128).T
    for l in range(4):
        put(GC_PRE + l * 8, inp["pre_norm_g"][l])
        put(GC_POST + l * 8, inp["post_norm_g"][l])
    for j in range(2):
        put(GC_GN + j * 16, inp["ret_gn_g"][j])
        put(GC_QN + j * 6, inp["mla_q_norm_g"][j])
        put(GC_KVN + j * 2, inp["mla_kv_norm_g"][j])
    r = np.arange(128)
    g[:, GC_INVF] = (np.float32(10000.0) ** (-(r.astype(np.float32)) / np.float32(128.0))).astype(np.float32)
    g[:, GC_INVF + 1] = (np.float32(10000.0) ** (-((r % 32).astype(np.float32)) / np.float32(32.0))).astype(np.float32)
    g[:, GC_SGN] = np.where((r % 64) < 32, -1.0, 1.0)
    return g


_CACHE = {}


def kernel(**inputs):
    inp = {k: np.asarray(v) for k, v in inputs.items()}
    n_layers = 4
    if "prog" not in _CACHE:
        _CACHE["prog"] = build_program(n_layers)
    nc, _ = _CACHE["prog"]
    ident, dec, mask01, maskb = _consts()
    gcols = _gcols(inp)
    shared = {
        "gcols": gcols, "ident": ident, "dec": dec, "mask01": mask01, "maskb": maskb,
        "ret_w_in": np.ascontiguousarray(inp["ret_w_in"], dtype=np.float32),
        "ret_w_out": np.ascontiguousarray(inp["ret_w_out"], dtype=np.float32),
        "mla_w_in": np.ascontiguousarray(inp["mla_w_in"], dtype=np.float32),
        "mla_w_uq": np.ascontiguousarray(inp["mla_w_uq"], dtype=np.float32),
        "mla_w_uk": np.ascontiguousarray(inp["mla_w_uk"], dtype=np.float32).reshape(2, 256, 2048),
        "mla_w_uv": np.ascontiguousarray(inp["mla_w_uv"], dtype=np.float32).reshape(2, 256, 2048),
        "mla_w_out": np.ascontiguousarray(inp["mla_w_out"], dtype=np.float32),
        "ple_w_proj": np.ascontiguousarray(inp["ple_w_proj"], dtype=np.float32),
        "ple_w_gate": np.ascontiguousarray(inp["ple_w_gate"], dtype=np.float32),
    }
    in_maps = []
    for core in range(8):
        b = core % 4
        m = dict(shared)
        m["x"] = np.ascontiguousarray(inp["x"][b], dtype=np.float32)
        m["p"] = np.ascontiguousarray(inp["p"][:, b], dtype=np.float32)
        m["pos"] = np.ascontiguousarray(inp["positions"][b].reshape(1, T), dtype=np.int32)
        in_maps.append(m)
    res = run_bass_kernel_spmd(nc, in_maps, core_ids=list(range(8)))
    out = np.stack([np.asarray(res.results[b]["out"], dtype=np.float32) for b in range(4)], axis=0)
    return out
```

```python
from contextlib import ExitStack
import numpy as np
import concourse.bass as bass
import concourse.mybir as mybir
from concourse.bass_utils import run_bass_kernel_spmd

F32 = mybir.dt.float32
BF16 = mybir.dt.bfloat16
I32 = mybir.dt.int32
ALU = mybir.AluOpType
AF = mybir.ActivationFunctionType

T = 8192
D = 1024
NT = 256
NTILES = T // NT
NBLK = NT // 128
EPS = 1e-6
GAMMAS = [1.0 - 2.0 ** (-5.0 - h) for h in range(4)]
CDEC = [g ** 128 for g in GAMMAS]
SM_SCALE = float(192.0 ** -0.5)

GC_PRE, GC_POST, GC_GN, GC_QN, GC_KVN, GC_INVF, GC_SGN, GC_N = 0, 32, 64, 96, 108, 112, 114, 115

EPOCH = 24000
DBG = {'stop': 99, 'tiles': None}
DMA_RING = 8
RING = {'sp': 8, 'pool': 2}


class Tok:
    __slots__ = ("eng", "idx", "needed", "sem", "val")

    def __init__(self, eng, idx):
        self.eng = eng
        self.idx = idx
        self.needed = False
        self.sem = None
        self.val = None


class Buf:
    __slots__ = ("name", "w", "r", "rd")

    def __init__(self, name=""):
        self.name = name
        self.w = None
        self.r = {}
        self.rd = []


def bufs(n, name=""):
    return [Buf(f"{name}{i}") for i in range(n)]


class Sched:
    QUEUES = ("sp", "pool")

    def __init__(self, nc):
        self.nc = nc
        self.ops = {e: [] for e in ("pe", "act", "dve", "pool", "sp")}
        self.dma_count = {"sp": 0, "pool": 0}
        self.dma_last = {q: [None] * RING[q] for q in ("sp", "pool")}
        self.last_tok = {}

    def _deps(self, eng, reads, writes, is_dma):
        waits = []
        for b in reads:
            if b.w is not None:
                waits.append(b.w)
        for b in writes:
            if b.w is not None:
                waits.append(b.w)
            waits.extend(b.r.values())
            waits.extend(b.rd)
        if eng == "pe" and not is_dma:
            waits = [t for t in waits if t.eng != "pe"]
        return waits

    def op(self, eng, fn, reads=(), writes=()):
        tok = Tok(eng, len(self.ops[eng]))
        waits = self._deps(eng, reads, writes, False)
        for t in waits:
            t.needed = True
        for b in reads:
            b.r[eng] = tok
        for b in writes:
            b.w = tok
            b.r = {}
            b.rd = []
        self.ops[eng].append((fn, waits, tok, False))
        self.last_tok[eng] = tok
        return tok

    def dma(self, q, out_ap, in_ap, reads=(), writes=()):
        n = self.dma_count[q]
        self.dma_count[q] = n + 1
        slot = n % RING[q]
        tok = Tok("dma_" + q, slot)
        tok.val = 16 * (n // RING[q] + 1)
        tok.needed = True
        waits = self._deps(q, reads, writes, True)
        prev = self.dma_last[q][slot]
        if prev is not None:
            waits.append(prev)
        self.dma_last[q][slot] = tok
        for t in waits:
            t.needed = True
        for b in reads:
            b.rd.append(tok)
        for b in writes:
            b.w = tok
            b.r = {}
            b.rd = []

        def fn(e, out_ap=out_ap, in_ap=in_ap):
            return e.dma_start(out=out_ap, in_=in_ap)
        self.ops[q].append((fn, waits, tok, True))
        return tok

    def barrier(self):
        lasts = []
        for e in ("pe", "act", "dve", "pool"):
            t = self.last_tok.get(e)
            if t is not None:
                t.needed = True
                lasts.append(t)
        for q in self.QUEUES:
            for t in self.dma_last[q]:
                if t is not None:
                    lasts.append(t)
        for e in ("act", "dve", "pool", "sp"):
            self.ops[e].append((None, list(lasts), None, False))

    def emit(self, stack):
        nc = self.nc
        n_sig = {e: sum(1 for o in self.ops[e] if o[0] is not None and (not o[3]) and o[2].needed)
                 for e in self.ops}
        eng_sems = {}
        for e in self.ops:
            k = max(1, (n_sig[e] + EPOCH - 1) // EPOCH)
            eng_sems[e] = [stack.enter_context(nc.semaphore(f"s_{e}_{i}")) for i in range(k)]
        dma_sems = {q: [stack.enter_context(nc.semaphore(f"d_{q}_{i}")) for i in range(RING[q])]
                    for q in self.QUEUES}
        for e in self.ops:
            c = 0
            for (fn, waits, tok, is_dma) in self.ops[e]:
                if fn is None:
                    continue
                if is_dma:
                    tok.sem = dma_sems[e][tok.idx]
                elif tok.needed:
                    tok.sem = eng_sems[e][c // EPOCH]
                    tok.val = c % EPOCH + 1
                    c += 1
        block = stack.enter_context(nc.Block())
        engobj = {"pe": "tensor", "act": "scalar", "dve": "vector", "pool": "gpsimd", "sp": "sync"}
        stats = {}
        for e in ("sp", "pool", "pe", "act", "dve"):
            ops = self.ops[e]

            def body(eng, ops=ops, e=e):
                seen = {}
                nw = 0
                for (fn, waits, tok, is_dma) in ops:
                    need = {}
                    for t in waits:
                        key = id(t.sem)
                        if seen.get(key, 0) >= t.val:
                            continue
                        if key not in need or need[key][1] < t.val:
                            need[key] = (t.sem, t.val)
                    for key, (sem, val) in need.items():
                        eng.wait_ge(sem, val)
                        seen[key] = val
                        nw += 1
                    if fn is None:
                        continue
                    ins = fn(eng)
                    if is_dma:
                        ins.then_inc(tok.sem, 16)
                    elif tok.needed:
                        ins.then_inc(tok.sem, 1)
                if e in self.QUEUES:
                    for t in self.dma_last[e]:
                        if t is not None and seen.get(id(t.sem), 0) < t.val:
                            eng.wait_ge(t.sem, t.val)
                            seen[id(t.sem)] = t.val
                stats[e] = (len(ops), nw)
            getattr(block, engobj[e])(body)
        return stats


def build_program(n_layers=4, layers=None):
    nc = bass.Bass("TRN2", target_bir_lowering=False)

    def din(name, shape, dt=F32):
        return nc.dram_tensor(name, shape, dt, kind="ExternalInput").ap()

    x_d = din("x", [T, D])
    p_d = din("p", [4, T, 256])
    pos_d = din("pos", [1, T], I32)
    gc_d = din("gcols", [128, GC_N])
    ident_d = din("ident", [128, 128])
    dec_d = din("dec", [128, 8, 128])
    mask01_d = din("mask01", [128, 128])
    maskb_d = din("maskb", [128, 4 * NT])
    ret_w_in = din("ret_w_in", [2, D, 6144])
    ret_w_out = din("ret_w_out", [2, 2048, D])
    mla_w_in = din("mla_w_in", [2, D, 3136])
    mla_w_uq = din("mla_w_uq", [2, 768, 3072])
    mla_w_uk = din("mla_w_uk", [2, 256, 2048])
    mla_w_uv = din("mla_w_uv", [2, 256, 2048])
    mla_w_out = din("mla_w_out", [2, 2048, D])
    ple_w_proj = din("ple_w_proj", [4, 256, D])
    ple_w_gate = din("ple_w_gate", [4, D, D])
    out_d = nc.dram_tensor("out", [T, D], F32, kind="ExternalOutput").ap()
    xT_d = nc.dram_tensor("xT_s", [128, 8, T], F32).ap()
    pT_d = nc.dram_tensor("pT_s", [4, 128, 2, T], BF16).ap()
    tab_d = nc.dram_tensor("tab_s", [4, 128, T], F32).ap()

    S = Sched(nc)
    st = ExitStack()

    def sb(name, shape, dt):
        return st.enter_context(nc.sbuf_tensor(name, shape, dt))

    def psum(name):
        return st.enter_context(nc.psum_tensor(name, [128, 512], F32))

    def mm(out, lhsT, rhs, start, stop, reads, writes):
        S.op("pe", lambda e: e.matmul(out, lhsT=lhsT, rhs=rhs, start=start, stop=stop), reads, writes)

    def tr(out, in_, ident, reads, writes):
        S.op("pe", lambda e: e.transpose(out=out, in_=in_, identity=ident), reads, writes)

    def act(out, in_, func, reads, writes, scale=1.0, bias=None):
        if bias is None:
            S.op("act", lambda e: e.activation(out=out, in_=in_, func=func, scale=scale), reads, writes)
        else:
            S.op("act", lambda e: e.activation(out=out, in_=in_, func=func, scale=scale, bias=bias),
                 reads, writes)

    def cp(eng, out, in_, reads, writes):
        if eng == "act":
            S.op("act", lambda e: e.copy(out=out, in_=in_), reads, writes)
        else:
            S.op(eng, lambda e: e.tensor_copy(out=out, in_=in_), reads, writes)

    def tt(eng, out, in0, in1, op, reads, writes):
        S.op(eng, lambda e: e.tensor_tensor(out=out, in0=in0, in1=in1, op=op), reads, writes)

    def ts(eng, out, in0, s1, s2, op0, op1, reads, writes):
        if s2 is None:
            S.op(eng, lambda e: e.tensor_scalar(out=out, in0=in0, scalar1=s1, scalar2=None, op0=op0),
                 reads, writes)
        else:
            S.op(eng, lambda e: e.tensor_scalar(out=out, in0=in0, scalar1=s1, scalar2=s2, op0=op0, op1=op1),
                 reads, writes)

    def stt(eng, out, in0, scalar, in1, op0, op1, reads, writes):
        eng = "dve"
        S.op(eng, lambda e: e.scalar_tensor_tensor(out=out, in0=in0, scalar=scalar, in1=in1, op0=op0, op1=op1),
             reads, writes)

    def memset(eng, ap, val, writes):
        S.op(eng, lambda e: e.memset(ap, val), (), writes)

    dumps = []

    def dump(name, ap, reads):
        if not DBG.get('dump'):
            return
        dd = nc.dram_tensor(name, list(ap.shape), ap.dtype, kind="ExternalOutput").ap()
        S.dma("sp", dd, ap, reads=reads)
        dumps.append(name)

    gc = sb("gc", [128, GC_N], F32); GCb = Buf()
    identf = sb("identf", [128, 128], F32); IDF = Buf()
    identb = sb("identb", [128, 128], BF16); IDB = Buf()
    onesb = sb("onesb", [128, 128], BF16); ONB = Buf()
    onesf = sb("onesf", [128, 128], F32); ONF = Buf()
    epst = sb("epst", [128, 1], F32); EPSb = Buf()
    xt = sb("xt", [128, 8, NT], F32); XT = bufs(8)
    hT = sb("hT", [128, 8, NT], BF16); HT = bufs(8)
    rstd = sb("rstd", [128, NT], F32); RSTD = Buf()
    NW = 4
    wp = [sb(f"wp{i}", [128, 8, 512], BF16) for i in range(NW)]
    WP = bufs(NW)
    ybuf = sb("ybuf", [128, 8, NT], F32); YB = bufs(8)
    ysq = sb("ysq", [128, 8, NT], BF16); YSQ = bufs(8)
    u = sb("u", [128, 16, NT], BF16); U = bufs(16)
    pt = sb("pt", [128, 2, NT], BF16); PTb = Buf()
    sig = sb("sig", [128, NT], F32); SIG = Buf()
    tA = sb("tA", [128, NT], F32); TA = Buf()
    tB = sb("tB", [128, NT], F32); TB = Buf()
    cs = sb("cs", [128, 2, NT], F32); CS = Buf()
    wr_t = sb("wr_t", [128, 6, 512], BF16); WRb = Buf()
    wrs_t = sb("wrs_t", [128, 6, 512], BF16); WRSb = Buf()
    UN_BYTES = 121 * 1024
    un = sb("un", [128, UN_BYTES // 2], BF16)

    class Carver:
        def __init__(self):
            self.off = 0

        def get(self, shape, dt):
            n = int(np.prod(shape[1:]))
            nb = n * (4 if dt == F32 else 2)
            nb_al = (nb + 63) // 64 * 64
            assert self.off + nb_al <= UN_BYTES, (self.off, nb_al)
            v = un[:, self.off // 2: self.off // 2 + nb // 2]
            self.off += nb_al
            if dt == F32:
                v = v.bitcast(F32)
            if len(shape) == 3:
                v = v.rearrange("p (a b) -> p a b", a=shape[1])
            return v

    pb = [psum(f"pb{i}") for i in range(8)]
    PB = bufs(8, "pb")
    pj_rr = [0]

    def pj():
        i = 1 + (pj_rr[0] % 2)
        pj_rr[0] += 1
        return pb[i], PB[i]

    wp_rr = [0]

    def wslot():
        i = wp_rr[0] % NW
        wp_rr[0] += 1
        return wp[i], WP[i]

    WSC = {}

    def wscr(key, shape):
        ap = nc.dram_tensor("wb_" + "_".join(str(k) for k in key), shape, BF16).ap()
        WSC[key] = (ap, Buf())
        return WSC[key]

    def conv_jobs(l):
        jobs = []

        def full(key, src2d):
            ap, B = wscr(key, list(src2d.shape))
            jobs.append(lambda: S.dma("pool", ap.rearrange("(a b) n -> a (b n)", a=128),
                                      src2d.rearrange("(a b) n -> a (b n)", a=128), writes=[B]))
        j = l // 2
        if l % 2 == 0:
            full(("ret_w_in", j), ret_w_in[j])
            full(("ret_w_out", j), ret_w_out[j])
        else:
            full(("mla_w_in", j), mla_w_in[j])
            full(("mla_w_uk", j), mla_w_uk[j])
            full(("mla_w_uv", j), mla_w_uv[j])
            apk, BK = wscr(("wk", j), [1024, 256])
            wi = mla_w_in[j]
            for (d0, s0, n) in ((0, 1024, 64), (64, 1024, 64), (128, 1056, 32), (160, 1024, 32),
                                (192, 1056, 32), (224, 1024, 32)):
                jobs.append(lambda d0=d0, s0=s0, n=n: S.dma("pool", apk[:, d0:d0 + n], wi[:, s0:s0 + n],
                                                           writes=[BK]))
            apr, BR = wscr(("wr", j), [768, 1024])
            aps, BS = wscr(("wrs", j), [768, 1024])
            uq4 = mla_w_uq[j].rearrange("r (h e) -> r h e", e=192)
            apr4 = apr.rearrange("r (h e) -> r h e", e=64)
            aps4 = aps.rearrange("r (h e) -> r h e", e=64)
            for kc in range(6):
                rs = slice(kc * 128, (kc + 1) * 128)
                jobs.append(lambda rs=rs: S.dma("pool", apr4[rs, :, :], uq4[rs, :, 128:192], writes=[BR]))
                jobs.append(lambda rs=rs: S.dma("pool", aps4[rs, :, 0:32], uq4[rs, :, 160:192], writes=[BS]))
                jobs.append(lambda rs=rs: S.dma("pool", aps4[rs, :, 32:64], uq4[rs, :, 128:160], writes=[BS]))
            full(("mla_w_uq", j), mla_w_uq[j])
            full(("mla_w_out", j), mla_w_out[j])
        full(("ple_w_gate", l), ple_w_gate[l])
        full(("ple_w_proj", l), ple_w_proj[l])
        return jobs

    pending = []

    def conv_step(n=1):
        for _ in range(n):
            if pending:
                pending.pop(0)()

    def wload(key, r0, nrows, cols, slot=None):
        w2d, WBUF = WSC[key]
        tile_, b = slot if slot is not None else wslot()
        kc = nrows // 128
        o = 0
        for (c0, ncols) in cols:
            src = w2d[r0:r0 + nrows, c0:c0 + ncols].rearrange("(k p) n -> p k n", p=128)
            S.dma("sp", tile_[:, 0:kc, o:o + ncols], src, reads=[WBUF], writes=[b])
            o += ncols
        return tile_, b

    if layers is None:
        layers = list(range(n_layers))
    if layers:
        for jb in conv_jobs(layers[0]):
            jb()
    S.dma("sp", gc[:], gc_d, writes=[GCb])
    S.dma("sp", identf[:], ident_d, writes=[IDF])
    S.dma("pool", identb[:], ident_d, writes=[IDB])
    memset("dve", onesb[:], 1.0, [ONB])
    memset("dve", onesf[:], 1.0, [ONF])
    memset("dve", epst[:], EPS, [EPSb])

    C1 = 6.28125
    C2 = float(2 * np.pi - 6.28125)
    PCH = 2048
    cvp = Carver()
    posi = cvp.get([128, PCH], F32).bitcast(I32); POSI = Buf()
    posf = cvp.get([128, PCH], F32); POSF = Buf()
    angb = cvp.get([128, PCH], F32); ANG = Buf()
    a2b = cvp.get([128, PCH], F32); A2 = Buf()
    kfb = cvp.get([128, PCH], F32); KF = Buf()
    kib = cvp.get([128, PCH], F32).bitcast(I32); KI = Buf()
    for ch in range(T // PCH):
        tsl = slice(ch * PCH, (ch + 1) * PCH)
        S.dma("sp", posi[:], pos_d[0:1, tsl].partition_broadcast(128), writes=[POSI])
        cp("dve", posf[:], posi[:], [POSI], [POSF])
        for sset in range(2):
            ts("dve", angb[:], posf[:], gc[:, GC_INVF + sset:GC_INVF + sset + 1], None, ALU.mult, None,
               [POSF, GCb], [ANG])
            for which in range(2):
                shift = float(np.pi / 2) if which == 0 else 0.0
                ts("dve", a2b[:], angb[:], shift, None, ALU.add, None, [ANG], [A2])
                ts("dve", kfb[:], a2b[:], float(1.0 / (2 * np.pi)), None, ALU.mult, None, [A2], [KF])
                cp("dve", kib[:], kfb[:], [KF], [KI])
                cp("dve", kfb[:], kib[:], [KI], [KF])
                stt("dve", a2b[:], kfb[:], -C1, a2b[:], ALU.mult, ALU.add, [KF, A2], [A2])
                stt("dve", a2b[:], kfb[:], -C2, a2b[:], ALU.mult, ALU.add, [KF, A2], [A2])
                ts("dve", kfb[:], a2b[:], float(np.pi), float(-2 * np.pi), ALU.is_gt, ALU.mult, [A2], [KF])
                tt("dve", a2b[:], a2b[:], kfb[:], ALU.add, [A2, KF], [A2])
                ts("dve", a2b[:], a2b[:], -3.1415925, 3.1415925, ALU.max, ALU.min, [A2], [A2])
                act(a2b[:], a2b[:], AF.Sin, [A2], [A2])
                if sset == 1 and which == 1:
                    ts("dve", a2b[:], a2b[:], gc[:, GC_SGN:GC_SGN + 1], None, ALU.mult, None, [A2, GCb], [A2])
                S.dma("pool", tab_d[sset * 2 + which, :, tsl], a2b[:], reads=[A2])

    xin = cvp.get([128, D], F32); XIN = Buf()
    pin = cvp.get([128, 4, 256], F32); PIN = Buf()
    xst = cvp.get([128, 8, 512], F32); XST = Buf()
    pst = cvp.get([128, 8, 512], BF16); PST = Buf()
    for it in range(T // 512):
        for bl in range(4):
            t0 = it * 512 + bl * 128
            S.dma("sp", xin[:], x_d[t0:t0 + 128, :], writes=[XIN])
            S.dma("sp", pin[:], p_d[:, t0:t0 + 128, :].rearrange("l t e -> t l e"), writes=[PIN])
            for half in range(2):
                bank, B = pb[3 + half], PB[3 + half]
                for c4 in range(4):
                    c = half * 4 + c4
                    tr(bank[:, c4 * 128:(c4 + 1) * 128], xin[:, c * 128:(c + 1) * 128], identf[:],
                       [XIN, IDF], [B])
                cp("act" if half == 0 else "dve",
                   xst[:, half * 4:half * 4 + 4, bl * 128:(bl + 1) * 128],
                   bank[:].rearrange("p (c n) -> p c n", c=4), [B], [XST])
            for half in range(2):
                bank, B = pb[5 + half], PB[5 + half]
                for c4 in range(4):
                    c = half * 4 + c4
                    tr(bank[:, c4 * 128:(c4 + 1) * 128], pin[:, c // 2, (c % 2) * 128:(c % 2 + 1) * 128],
                       identf[:], [PIN, IDF], [B])
                cp("act" if half == 1 else "dve",
                   pst[:, half * 4:half * 4 + 4, bl * 128:(bl + 1) * 128],
                   bank[:].rearrange("p (c n) -> p c n", c=4), [B], [PST])
        S.dma("pool", xT_d[:, :, it * 512:(it + 1) * 512], xst[:], reads=[XST])
        for l in range(4):
            S.dma("pool", pT_d[l, :, :, it * 512:(it + 1) * 512], pst[:, l * 2:l * 2 + 2, :], reads=[PST])
    XTD = Buf()
    XTD_t = bufs(NTILES)
    S.barrier()

    def prenorm(l, ti):
        tsl = slice(ti * NT, (ti + 1) * NT)
        conv_step(2)
        S.dma("sp", xt[:], xT_d[:, :, tsl], reads=[XTD_t[ti]], writes=XT)
        act(hT[:].rearrange("p c n -> p (c n)"), xt[:].rearrange("p c n -> p (c n)"), AF.Square, XT, HT)
        for c in range(8):
            mm(pb[0][:, 0:NT], onesb[:], hT[:, c, :], c == 0, c == 7, [ONB, HT[c]], [PB[0]])
        act(rstd[:], pb[0][:, 0:NT], AF.Ln, [PB[0], EPSb], [RSTD], scale=1.0 / D, bias=epst[:])
        act(rstd[:], rstd[:], AF.Exp, [RSTD], [RSTD], scale=-0.5)
        for c in range(8):
            stt("dve" if c % 2 == 0 else "pool", hT[:, c, :], xt[:, c, :],
                gc[:, GC_PRE + l * 8 + c:GC_PRE + l * 8 + c + 1], rstd[:], ALU.mult, ALU.mult,
                [XT[c], GCb, RSTD], [HT[c]])

    def tail(l, ti, wout_key, last):
        tsl = slice(ti * NT, (ti + 1) * NT)
        for og in range(2):
            wa, WA = wload(wout_key, 0, 1024, [(og * 512, 512)])
            wb, WB = wload(wout_key, 1024, 1024, [(og * 512, 512)])
            for o4 in range(4):
                oc = og * 4 + o4
                bank, B = pj()
                KK = DBG.get('kk', 16)
                for kc in range(KK):
                    w_, W_ = (wa, WA) if kc < 8 else (wb, WB)
                    mm(bank[:, 0:NT], w_[:, kc % 8, o4 * 128:(o4 + 1) * 128], u[:, kc, :], kc == 0, kc == KK - 1,
                       [W_, U[kc]], [B])
                cp("act", ybuf[:, oc, :], bank[:, 0:NT], [B], [YB[oc]])
                act(ysq[:, oc, :], bank[:, 0:NT], AF.Square, [B], [YSQ[oc]])
        if DBG['stop'] < 6:
            return
        for c in range(8):
            mm(pb[0][:, 0:NT], onesb[:], ysq[:, c, :], c == 0, c == 7, [ONB, YSQ[c]], [PB[0]])
        act(rstd[:], pb[0][:, 0:NT], AF.Ln, [PB[0], EPSb], [RSTD], scale=1.0 / D, bias=epst[:])
        act(rstd[:], rstd[:], AF.Exp, [RSTD], [RSTD], scale=-0.5)
        for c in range(8):
            e1 = "dve" if c % 2 == 0 else "pool"
            stt(e1, ybuf[:, c, :], ybuf[:, c, :], gc[:, GC_POST + l * 8 + c:GC_POST + l * 8 + c + 1], rstd[:],
                ALU.mult, ALU.mult, [YB[c], GCb, RSTD], [YB[c]])
            tt(e1, xt[:, c, :], xt[:, c, :], ybuf[:, c, :], ALU.add, [XT[c], YB[c]], [XT[c]])
            cp("act", hT[:, c, :], xt[:, c, :], [XT[c]], [HT[c]])
        if DBG['stop'] < 7:
            return
        S.dma("sp", pt[:], pT_d[l, :, :, tsl], writes=[PTb])
        for og in range(2):
            wg, WG = wload(("ple_w_gate", l), 0, 1024, [(og * 512, 512)])
            wq, WQ = wload(("ple_w_proj", l), 0, 256, [(og * 512, 512)])
            for o4 in range(4):
                oc = og * 4 + o4
                bank, B = pj()
                for kc in range(8):
                    mm(bank[:, 0:NT], wg[:, kc, o4 * 128:(o4 + 1) * 128], hT[:, kc, :], kc == 0, kc == 7,
                       [WG, HT[kc]], [B])
                for kc in range(2):
                    mm(bank[:, NT:2 * NT], wq[:, kc, o4 * 128:(o4 + 1) * 128], pt[:, kc, :], kc == 0, kc == 1,
                       [WQ, PTb], [B])
                act(sig[:], bank[:, 0:NT], AF.Sigmoid, [B], [SIG])
                tt("dve", tA[:], bank[:, NT:2 * NT], sig[:], ALU.mult, [B, SIG], [TA])
                tt("pool", xt[:, oc, :], xt[:, oc, :], tA[:], ALU.add, [XT[oc], TA], [XT[oc]])
        if DBG['stop'] < 8:
            return
        if not last:
            S.dma("pool", xT_d[:, :, tsl], xt[:], reads=XT, writes=[XTD_t[ti]])
        else:
            otile = ybuf[:].rearrange("p c n -> p (c n)")[:, 0:1024]
            for bl in range(NBLK):
                for half in range(2):
                    bank, B = pb[3 + half], PB[3 + half]
                    for c4 in range(4):
                        c = half * 4 + c4
                        tr(bank[:, c4 * 128:(c4 + 1) * 128], xt[:, c, bl * 128:(bl + 1) * 128], identf[:],
                           [XT[c], IDF], [B])
                    cp("act" if half == 0 else "dve", otile[:, half * 512:(half + 1) * 512], bank[:],
                       [B], YB)
                t0 = ti * NT + bl * 128
                S.dma("pool", out_d[t0:t0 + 128, :], otile, reads=YB)

    def ret_layer(l, last):
        j = l // 2
        w_in = ("ret_w_in", j)
        cv = Carver()
        dec = cv.get([128, 8, 128], F32); DEC = Buf()
        mask01 = cv.get([128, 128], F32); M01 = Buf()
        stage = cv.get([128, 2, NT], F32); STG = bufs(2)
        qh2 = [cv.get([128, 2, NT], BF16) for _ in range(2)]; QH2 = bufs(2)
        kh2 = [cv.get([128, 2, NT], BF16) for _ in range(2)]; KH2 = bufs(2)
        ktok2 = [cv.get([128, NBLK, 256], BF16) for _ in range(2)]; KTOK2 = [bufs(NBLK) for _ in range(2)]
        vt2 = [cv.get([128, NBLK, 512], BF16) for _ in range(2)]; VT2 = [bufs(NBLK) for _ in range(2)]
        sg2 = [cv.get([128, 4, NT], BF16) for _ in range(2)]; SG2 = [bufs(4) for _ in range(2)]
        Sst = [cv.get([128, 2, 512], F32) for _ in range(4)]; SST = [bufs(2) for _ in range(4)]
        Sbf = [cv.get([128, 2, 512], BF16) for _ in range(4)]; SBF = [bufs(2) for _ in range(4)]
        am2 = [cv.get([128, 128], BF16) for _ in range(2)]; AM2 = bufs(2)
        oraw2 = [cv.get([128, 4, NT], BF16) for _ in range(2)]; ORAW2 = bufs(2)
        osq2 = [cv.get([128, 4, NT], BF16) for _ in range(2)]; OSQ2 = bufs(2)
        rso = cv.get([128, NT], F32); RSO = Buf()
        S.dma("sp", dec, dec_d, writes=[DEC])
        S.dma("sp", mask01, mask01_d, writes=[M01])
        for h in range(4):
            for dc in range(2):
                memset("pool", Sst[h][:, dc, :], 0.0, [SST[h][dc]])
                memset("pool", Sbf[h][:, dc, :], 0.0, [SBF[h][dc]])
        pbT = pb[7][:].bitcast(BF16)

        def proj_groups(h):
            s_ = h % 2
            qh, QH, kh, KH = qh2[s_], QH2[s_], kh2[s_], KH2[s_]
            ktok, KTOK, vt, VT, sg, SG = ktok2[s_], KTOK2[s_], vt2[s_], VT2[s_], sg2[s_], SG2[s_]
            wl = {}

            def g_qk(which, half):
                if which == 0 and half == 0:
                    wl['qk'] = wload(w_in, 0, 1024, [(h * 256, 256), (1024 + h * 256, 256)])
                    wl['v'] = wload(w_in, 0, 1024, [(2048 + h * 512, 512)])
                    wl['g'] = wload(w_in, 0, 1024, [(4096 + h * 512, 512)])
                wqk, WQK = wl['qk']
                dst, DST = (qh, QH) if which == 0 else (kh, KH)
                bank, B = pj()
                c0 = which * 256 + half * 128
                for kc in range(8):
                    mm(bank[:, 0:NT], wqk[:, kc, c0:c0 + 128], hT[:, kc, :], kc == 0, kc == 7,
                       [WQK, HT[kc]], [B])
                tt("dve", stage[:, half, :].rearrange("p (c n) -> p c n", c=NBLK),
                   bank[:, 0:NT].rearrange("p (c n) -> p c n", c=NBLK),
                   dec[:, which * 4 + h, :].unsqueeze(1).to_broadcast([128, NBLK, 128]),
                   ALU.mult, [B, DEC], [STG[half]])
                if half == 1:
                    tt("dve", tA[:], stage[:, 0, :], cs[:, 0, :], ALU.mult, [STG[0], CS], [TA])
                    tt("pool", tB[:], stage[:, 1, :], cs[:, 1, :], ALU.mult, [STG[1], CS], [TB])
                    tt("dve", dst[:, 0, :], tA[:], tB[:], ALU.subtract, [TA, TB], [DST])
                    tt("pool", tA[:], stage[:, 1, :], cs[:, 0, :], ALU.mult, [STG[1], CS], [TA])
                    tt("dve", tB[:], stage[:, 0, :], cs[:, 1, :], ALU.mult, [STG[0], CS], [TB])
                    tt("pool", dst[:, 1, :], tA[:], tB[:], ALU.add, [TA, TB], [DST])

            def g_ktok():
                for c in range(NBLK):
                    for dc in range(2):
                        tr(pbT[:, dc * 128:(dc + 1) * 128], kh[:, dc, c * 128:(c + 1) * 128], identb[:],
                           [KH, IDB], [PB[7]])
                    S.op("act", lambda e, c=c: e.mul(out=ktok[:, c, :], in_=pbT[:, 0:256], mul=float(CDEC[h])),
                         [PB[7]], [KTOK[c]])

            def g_v(c):
                wv, WV = wl['v']
                bank, B = pj()
                for kc in range(8):
                    mm(bank[:, 0:512], hT[:, kc, c * 128:(c + 1) * 128], wv[:, kc, :], kc == 0, kc == 7,
                       [WV, HT[kc]], [B])
                cp("act", vt[:, c, :], bank[:, 0:512], [B], [VT[c]])

            def g_gate(vc):
                wg, WG = wl['g']
                bank, B = pj()
                for kc in range(8):
                    mm(bank[:, 0:NT], wg[:, kc, vc * 128:(vc + 1) * 128], hT[:, kc, :], kc == 0, kc == 7,
                       [WG, HT[kc]], [B])
                act(sg[:, vc, :], bank[:, 0:NT], AF.Silu, [B], [SG[vc]])

            gl = [lambda: g_qk(0, 0), lambda: g_qk(0, 1), lambda: g_qk(1, 0), lambda: g_qk(1, 1)]
            gl += [lambda c=c: g_v(c) for c in range(NBLK)]
            gl += [g_ktok]
            gl += [lambda vc=vc: g_gate(vc) for vc in range(4)]
            return gl

        def scan_steps(h, ti):
            s_ = h % 2
            qh, QH, kh, KH = qh2[s_], QH2[s_], kh2[s_], KH2[s_]
            ktok, KTOK, vt, VT = ktok2[s_], KTOK2[s_], vt2[s_], VT2[s_]
            oraw, ORAW, osq, OSQ = oraw2[s_], ORAW2[s_], osq2[s_], OSQ2[s_]
            steps = []
            for c in range(NBLK):
                csl = slice(c * 128, (c + 1) * 128)
                am, AM = am2[c % 2], AM2[c % 2]
                sb_i = 3 + (c % 2)
                first = (ti == 0 and c == 0)

                def s_at(c=c, csl=csl, am=am, AM=AM, sb_i=sb_i):
                    for dc in range(2):
                        mm(pb[sb_i][:, 0:128], kh[:, dc, csl], qh[:, dc, csl], dc == 0, dc == 1,
                           [KH, QH], [PB[sb_i]])
                    tt("dve", am, pb[sb_i][:, 0:128], mask01, ALU.mult, [PB[sb_i], M01], [AM])

                def s_o(c=c, csl=csl, am=am, AM=AM, first=first):
                    ob, OB = pb[5], PB[5]
                    for vc in range(4):
                        mm(ob[:, vc * 128:(vc + 1) * 128], vt[:, c, vc * 128:(vc + 1) * 128], am, True, first,
                           [VT[c], AM], [OB])
                        if not first:
                            for dc in range(2):
                                mm(ob[:, vc * 128:(vc + 1) * 128], Sbf[h][:, dc, vc * 128:(vc + 1) * 128],
                                   qh[:, dc, csl], False, dc == 1, [SBF[h][dc], QH], [OB])
                    obv = ob[:].rearrange("p (v n) -> p v n", v=4)
                    cp("act", oraw[:, :, csl], obv, [OB], [ORAW])
                    act(osq[:, :, csl], obv, AF.Square, [OB], [OSQ])

                def s_state(c=c):
                    for dc in range(2):
                        sbk, SBK = pb[6], PB[6]
                        mm(sbk[:, 0:512], ktok[:, c, dc * 128:(dc + 1) * 128], vt[:, c, :], True, True,
                           [KTOK[c], VT[c]], [SBK])
                        stt("dve", Sst[h][:, dc, :], Sst[h][:, dc, :], float(CDEC[h]), sbk[:, 0:512],
                            ALU.mult, ALU.add, [SST[h][dc], SBK], [SST[h][dc]])
                        cp("pool", Sbf[h][:, dc, :], Sst[h][:, dc, :], [SST[h][dc]], [SBF[h][dc]])
                steps += [s_at, s_o, s_state]
            return steps

        def norm_head(h):
            s_ = h % 2
            oraw, ORAW, osq, OSQ, sg, SG = oraw2[s_], ORAW2[s_], osq2[s_], OSQ2[s_], sg2[s_], SG2[s_]
            for vc in range(4):
                mm(pb[0][:, 0:NT], onesb[:], osq[:, vc, :], vc == 0, vc == 3, [ONB, OSQ], [PB[0]])
            act(rso, pb[0][:, 0:NT], AF.Ln, [PB[0], EPSb], [RSO], scale=1.0 / 512, bias=epst[:])
            act(rso, rso, AF.Exp, [RSO], [RSO], scale=-0.5)
            for vc in range(4):
                gcol = GC_GN + j * 16 + h * 4 + vc
                stt("dve", oraw[:, vc, :], oraw[:, vc, :], gc[:, gcol:gcol + 1], rso, ALU.mult, ALU.mult,
                    [ORAW, GCb, RSO], [ORAW])
                tt("dve", u[:, h * 4 + vc, :], oraw[:, vc, :], sg[:, vc, :], ALU.mult,
                   [ORAW, SG[vc]], [U[h * 4 + vc]])

        for ti in range(NTILES if DBG['tiles'] is None else DBG['tiles']):
            tsl = slice(ti * NT, (ti + 1) * NT)
            prenorm(l, ti)
            S.dma("sp", cs[:, 0, :], tab_d[0, :, tsl], writes=[CS])
            S.dma("sp", cs[:, 1, :], tab_d[1, :, tsl], writes=[CS])
            for g in proj_groups(0):
                g()
            for h in range(4):
                P = proj_groups(h + 1) if h + 1 < 4 else []
                Sx = scan_steps(h, ti)
                while P or Sx:
                    for _ in range(2):
                        if P:
                            P.pop(0)()
                    if Sx:
                        Sx.pop(0)()
                norm_head(h)
            if DBG['stop'] < 5:
                continue
            tail(l, ti, ("ret_w_out", j), last)

    def mla_layer(l, last):
        j = l // 2
        w_in = ("mla_w_in", j)
        w_uq = ("mla_w_uq", j)
        cv = Carver()
        KT = cv.get([128, 3, T], BF16); KTB = bufs(T // 128)
        V = cv.get([128, T // 128, 256], BF16); VB = bufs(T // 128)
        cqn = cv.get([128, 6, NT], BF16); CQN = bufs(6)
        wukT = cv.get([128, 16, 256], BF16); WUKT = Buf()
        wuv = cv.get([128, 2, 2048], BF16); WUV = Buf()
        qrz = [cv.get([128, 2 * NT], BF16) for _ in range(2)]; QRZ = bufs(2)
        qn = cv.get([128, NT], BF16); QN = Buf()
        qc = [cv.get([128, 2, 2 * NT], BF16) for _ in range(2)]; QC = bufs(2)
        Pt = [cv.get([128, 2 * NT], BF16) for _ in range(3)]; PT3 = bufs(3)
        acc = [cv.get([128, 2 * NT], F32) for _ in range(2)]; ACC = bufs(2)
        rinv = cv.get([128, 2 * NT], F32); RINV = Buf()
        olat = cv.get([128, 2, 2 * NT], BF16); OLAT = Buf()
        sgm = [cv.get([128, 2 * NT], BF16) for _ in range(2)]; SGM = bufs(2)
        maskb = cv.get([128, 2, 2 * NT], BF16); MKB = Buf()
        for s_ in range(2):
            memset("pool", qrz[s_], 0.0, [QRZ[s_]])
        pbT = pb[7][:].bitcast(BF16)
        S.dma("pool", maskb.rearrange("p a b -> p (a b)"), maskb_d, writes=[MKB])
        S.dma("sp", wuv, WSC[("mla_w_uv", j)][0].rearrange("(k p) n -> p k n", p=128),
              reads=[WSC[("mla_w_uv", j)][1]], writes=[WUV])
        wl, WL = wslot()
        wlv = wl[:].rearrange("p a b -> p (a b)").rearrange("p (k n) -> p k n", k=2)
        S.dma("sp", wlv, WSC[("mla_w_uk", j)][0].rearrange("(k p) n -> p k n", p=128),
              reads=[WSC[("mla_w_uk", j)][1]], writes=[WL])
        for h in range(16):
            for cc in range(2):
                tr(pbT[:, cc * 128:(cc + 1) * 128], wlv[:, cc, h * 128:(h + 1) * 128], identb[:],
                   [WL, IDB], [PB[7]])
            cp("act", wukT[:, h, :], pbT[:, 0:256], [PB[7]], [WUKT])
        pcount = [0]
        ptcount = [0]
        for ti in range(NTILES if DBG['tiles'] is None else DBG['tiles']):
            tsl = slice(ti * NT, (ti + 1) * NT)
            prenorm(l, ti)
            S.dma("sp", cs[:, 0, :], tab_d[2, :, tsl], writes=[CS])
            S.dma("sp", cs[:, 1, :], tab_d[3, :, tsl], writes=[CS])
            for og in range(2):
                w_, W_ = wload(w_in, 0, 1024, [(og * 512, 512)])
                for o4 in range(4):
                    oc = og * 4 + o4
                    bank, B = pj()
                    for kc in range(8):
                        mm(bank[:, 0:NT], w_[:, kc, o4 * 128:(o4 + 1) * 128], hT[:, kc, :], kc == 0, kc == 7,
                           [W_, HT[kc]], [B])
                    cp("act", ybuf[:, oc, :], bank[:, 0:NT], [B], [YB[oc]])
                    act(ysq[:, oc, :], bank[:, 0:NT], AF.Square, [B], [YSQ[oc]])
            for c in range(6):
                mm(pb[0][:, 0:NT], onesb[:], ysq[:, c, :], c == 0, c == 5, [ONB, YSQ[c]], [PB[0]])
            act(rstd[:], pb[0][:, 0:NT], AF.Ln, [PB[0], EPSb], [RSTD], scale=1.0 / 768, bias=epst[:])
            act(rstd[:], rstd[:], AF.Exp, [RSTD], [RSTD], scale=-0.5)
            for c in range(6):
                gcol = GC_QN + j * 6 + c
                stt("dve" if c % 2 == 0 else "pool", cqn[:, c, :], ybuf[:, c, :], gc[:, gcol:gcol + 1], rstd[:],
                    ALU.mult, ALU.mult, [YB[c], GCb, RSTD], [CQN[c]])
            kb0 = ti * NBLK
            for c in range(2):
                mm(pb[0][:, 0:NT], onesb[:], ysq[:, 6 + c, :], c == 0, c == 1, [ONB, YSQ[6 + c]], [PB[0]])
            act(rstd[:], pb[0][:, 0:NT], AF.Ln, [PB[0], EPSb], [RSTD], scale=1.0 / 256, bias=epst[:])
            act(rstd[:], rstd[:], AF.Exp, [RSTD], [RSTD], scale=-0.5)
            for c in range(2):
                gcol = GC_KVN + j * 2 + c
                stt("dve" if c % 2 == 0 else "pool", KT[:, c, tsl], ybuf[:, 6 + c, :], gc[:, gcol:gcol + 1], rstd[:],
                    ALU.mult, ALU.mult, [YB[6 + c], GCb, RSTD], [KTB[kb0 + b] for b in range(NBLK)])
            for b in range(NBLK):
                for c in range(2):
                    tr(pbT[:, c * 128:(c + 1) * 128], KT[:, c, (kb0 + b) * 128:(kb0 + b + 1) * 128], identb[:],
                       [KTB[kb0 + b], IDB], [PB[7]])
                cp("act", V[:, kb0 + b, :], pbT[:, 0:256], [PB[7]], [VB[kb0 + b]])
            wk, WK = wload(("wk", j), 0, 1024, [(0, 256)])
            bank, B = pj()
            for kc in range(8):
                mm(bank[:, 0:NT], wk[:, kc, 0:128], hT[:, kc, :], kc == 0, kc == 7, [WK, HT[kc]], [B])
            for kc in range(8):
                mm(bank[:, NT:2 * NT], wk[:, kc, 128:256], hT[:, kc, :], kc == 0, kc == 7, [WK, HT[kc]], [B])
            tt("dve", tA[:], bank[:, 0:NT], cs[:, 0, :], ALU.mult, [B, CS], [TA])
            tt("dve", tB[:], bank[:, NT:2 * NT], cs[:, 1, :], ALU.mult, [B, CS], [TB])
            tt("pool", KT[:, 2, tsl], tA[:], tB[:], ALU.add, [TA, TB], [KTB[kb0 + b] for b in range(NBLK)])
            if ti == DBG.get('dump_ti', 0):
                dump("d_hT", hT[:], HT)
                dump("d_cqn", cqn, CQN)
                dump("d_KT", KT[:, :, 0:(ti + 1) * NT], KTB[0:(ti + 1) * NBLK])
                dump("d_V", V[:, 0:(ti + 1) * NBLK, :], VB[0:(ti + 1) * NBLK])
                dump("d_cs", cs[:], [CS])
            for hq in range(4):
                wgt, WGT = wload(w_in, 0, 1024, [(1088 + hq * 512, 512)])
                for hp in range(2):
                    bank, B = pj()
                    for hh in range(2):
                        hl = hp * 2 + hh
                        for kc in range(8):
                            mm(bank[:, hh * NT:(hh + 1) * NT], wgt[:, kc, hl * 128:(hl + 1) * 128], hT[:, kc, :],
                               kc == 0, kc == 7, [WGT, HT[kc]], [B])
                    h0 = hq * 4 + hp * 2
                    act(u[:, h0:h0 + 2, :].rearrange("p a b -> p (a b)"), bank[:], AF.Silu, [B],
                        [U[h0], U[h0 + 1]])
            nkt = 2 * (ti + 1)
            pctx = {}
            gstate = {}

            def prep(m):
                hg, m_loc = divmod(m, 4)
                s_ = m % 2
                if m_loc == 0:
                    for (dst, DSTB, key) in ((wr_t, WRb, ("wr", j)), (wrs_t, WRSb, ("wrs", j))):
                        S.dma("sp", dst[:], WSC[key][0][:, hg * 512:(hg + 1) * 512].rearrange(
                            "(k p) n -> p k n", p=128), reads=[WSC[key][1]], writes=[DSTB])
                bank, B = pj()
                for kc in range(6):
                    mm(bank[:, 0:NT], wr_t[:, kc, m_loc * 128:(m_loc + 1) * 128], cqn[:, kc, :],
                       kc == 0, kc == 5, [WRb, CQN[kc]], [B])
                for kc in range(6):
                    mm(bank[:, NT:2 * NT], wrs_t[:, kc, m_loc * 128:(m_loc + 1) * 128], cqn[:, kc, :],
                       kc == 0, kc == 5, [WRSb, CQN[kc]], [B])
                tt("dve", tA[:], bank[:, 0:NT], cs[:, 0, :], ALU.mult, [B, CS], [TA])
                tt("dve", tB[:], bank[:, NT:2 * NT], cs[:, 1, :], ALU.mult, [B, CS], [TB])
                tt("pool", qrz[s_][0:64, 0:NT], tA[0:64, :], tB[0:64, :], ALU.add, [TA, TB], [QRZ[s_]])
                tt("pool", qrz[s_][64:128, NT:2 * NT], tA[64:128, :], tB[64:128, :], ALU.add, [TA, TB], [QRZ[s_]])
                wn, WN = wload(w_uq, 0, 768, [(2 * m * 192, 128), ((2 * m + 1) * 192, 128)])
                for hh in range(2):
                    h = 2 * m + hh
                    hsl = slice(hh * NT, (hh + 1) * NT)
                    bank, B = pj()
                    for kc in range(6):
                        mm(bank[:, 0:NT], wn[:, kc, hh * 128:(hh + 1) * 128], cqn[:, kc, :],
                           kc == 0, kc == 5, [WN, CQN[kc]], [B])
                    cp("act", qn, bank[:, 0:NT], [B], [QN])
                    bank, B = pj()
                    for cc in range(2):
                        mm(bank[:, cc * NT:(cc + 1) * NT], wukT[:, h, cc * 128:(cc + 1) * 128], qn,
                           True, True, [WUKT, QN], [B])
                    cp("dve", qc[s_][:, :, hsl], bank[:].rearrange("p (a b) -> p a b", a=2), [B], [QC[s_]])
                pctx[m] = dict(sb={}, pt={})

            def emit_qk(m, kt):
                c = pctx[m]
                if kt in c['sb']:
                    return
                s_ = m % 2
                sbi = 3 + (pcount[0] % 2)
                pcount[0] += 1
                c['sb'][kt] = sbi
                sbank, SBK = pb[sbi], PB[sbi]
                diag = kt >= 2 * ti
                ksl = slice(kt * 128, (kt + 1) * 128)
                mm(sbank[:], KT[:, 0, ksl], qc[s_][:, 0, :], True, False, [KTB[kt], QC[s_]], [SBK])
                mm(sbank[:], KT[:, 1, ksl], qc[s_][:, 1, :], False, False, [KTB[kt], QC[s_]], [SBK])
                mm(sbank[:], KT[:, 2, ksl], qrz[s_], False, not diag, [KTB[kt], QRZ[s_]], [SBK])
                if diag:
                    mm(sbank[:], identb[:], maskb[:, kt - 2 * ti, :], False, True, [IDB, MKB], [SBK])

            def emit_exp_acc(m, kt):
                c = pctx[m]
                sbi = c['sb'][kt]
                pti = ptcount[0] % 3
                ptcount[0] += 1
                c['pt'][kt] = pti
                s_ = m % 2
                act(Pt[pti], pb[sbi][:], AF.Exp, [PB[sbi]], [PT3[pti]], scale=SM_SCALE)
                if kt == 0:
                    cp("dve", acc[s_], Pt[pti], [PT3[pti]], [ACC[s_]])
                else:
                    tt("dve", acc[s_], acc[s_], Pt[pti], ALU.add, [ACC[s_], PT3[pti]], [ACC[s_]])

            def emit_pv(m, kt):
                pti = pctx[m]['pt'][kt]
                for cc in range(2):
                    mm(pb[5 + cc][:], V[:, kt, cc * 128:(cc + 1) * 128], Pt[pti], kt == 0, kt == nkt - 1,
                       [VB[kt], PT3[pti]], [PB[5 + cc]])

            def finish_den(m):
                s_ = m % 2
                mm(pb[0][:], onesf[:], acc[s_], True, True, [ONF, ACC[s_]], [PB[0]])
                cp("dve", olat[:, 0, :], pb[5][:], [PB[5]], [OLAT])
                cp("dve", olat[:, 1, :], pb[6][:], [PB[6]], [OLAT])
                act(rinv, pb[0][:], AF.Ln, [PB[0]], [RINV])
                act(rinv, rinv, AF.Exp, [RINV], [RINV], scale=-1.0)

            def finish_uv(m):
                bank, B = pj()
                for hh in range(2):
                    h = 2 * m + hh
                    for cc in range(2):
                        mm(bank[:, hh * NT:(hh + 1) * NT], wuv[:, cc, h * 128:(h + 1) * 128],
                           olat[:, cc, hh * NT:(hh + 1) * NT], cc == 0, cc == 1, [WUV, OLAT], [B])
                u2 = u[:, 2 * m:2 * m + 2, :].rearrange("p a b -> p (a b)")
                tt("dve", rinv, bank[:], rinv, ALU.mult, [B, RINV], [RINV])
                tt("dve", u2, rinv, u2, ALU.mult, [RINV, U[2 * m], U[2 * m + 1]], [U[2 * m], U[2 * m + 1]])

            prep(0)
            emit_qk(0, 0)
            for m in range(8):
                for kt in range(nkt):
                    lastk = (kt == nkt - 1)
                    if not lastk:
                        emit_qk(m, kt + 1)
                    emit_exp_acc(m, kt)
                    if lastk and m + 1 < 8:
                        prep(m + 1)
                    emit_pv(m, kt)
                if m + 1 < 8:
                    emit_qk(m + 1, 0)
                finish_den(m)
                if m + 1 < 8:
                    emit_qk(m + 1, 1)
                finish_uv(m)
            if ti == DBG.get('dump_ti', 0):
                dump("d_u", u[:], U)
            tail(l, ti, ("mla_w_out", j), last)

    if n_layers == 0:
        for ti in range(NTILES):
            tsl = slice(ti * NT, (ti + 1) * NT)
            S.dma("sp", xt[:], xT_d[:, :, tsl], writes=XT)
            otile = ybuf[:].rearrange("p c n -> p (c n)")[:, 0:1024]
            for bl in range(NBLK):
                for half in range(2):
                    bank, B = pb[3 + half], PB[3 + half]
                    for c4 in range(4):
                        c = half * 4 + c4
                        tr(bank[:, c4 * 128:(c4 + 1) * 128], xt[:, c, bl * 128:(bl + 1) * 128], identf[:],
                           [XT[c], IDF], [B])
                    cp("act" if half == 0 else "dve", otile[:, half * 512:(half + 1) * 512], bank[:],
                       [B], YB)
                t0 = ti * NT + bl * 128
                S.dma("pool", out_d[t0:t0 + 128, :], otile, reads=YB)
    for li, l in enumerate(layers):
        last = (li == len(layers) - 1)
        if li + 1 < len(layers):
            pending.extend(conv_jobs(layers[li + 1]))
        if l % 2 == 0:
            ret_layer(l, last)
        else:
            mla_layer(l, last)
        conv_step(len(pending))
        S.barrier()

    stats = S.emit(st)
    st.close()
    return nc, stats


def _consts():
    ident = np.eye(128, dtype=np.float32)
    i = np.arange(128, dtype=np.float64)
    dec = np.zeros((128, 8, 128), np.float32)
    for h in range(4):
        lg = np.log(GAMMAS[h])
        dec[:, h, :] = np.exp((i + 1.0) * lg)[None, :]
        dec[:, 4 + h, :] = (np.exp(-(i + 1.0) * lg) * (256.0 ** -0.5))[None, :]
    jj = np.arange(128)[:, None]
    ii = np.arange(128)[None, :]
    mask01 = (ii >= jj).astype(np.float32)
    qi = np.arange(NT)[None, :]
    maskb = np.zeros((128, 2, 2, NT), np.float32)
    for o in range(2):
        maskb[:, o, :, :] = np.where((128 * o + jj) <= qi, 0.0, -30000.0)[:, None, :]
    return ident, dec, mask01, maskb.reshape(128, 4 * NT)


def _gcols(inp):
    g = np.zeros((128, GC_N), np.float32)

    def put(col0, vec):
        k = vec.shape[0] // 128
        g[:, col0:col0 + k] = vec.reshape(k, 128).T
    for l in range(4):
        put(GC_PRE + l * 8, inp["pre_norm_g"][l])
        put(GC_POST + l * 8, inp["post_norm_g"][l])
    for j in range(2):
        put(GC_GN + j * 16, inp["ret_gn_g"][j])
        put(GC_QN + j * 6, inp["mla_q_norm_g"][j])
        put(GC_KVN + j * 2, inp["mla_kv_norm_g"][j])
    r = np.arange(128)
    g[:, GC_INVF] = (np.float32(10000.0) ** (-(r.astype(np.float32)) / np.float32(128.0))).astype(np.float32)
    g[:, GC_INVF + 1] = (np.float32(10000.0) ** (-((r % 32).astype(np.float32)) / np.float32(32.0))).astype(np.float32)
    g[:, GC_SGN] = np.where((r % 64) < 32, -1.0, 1.0)
    return g


_CACHE = {}


def kernel(**inputs):
    inp = {k: np.asarray(v) for k, v in inputs.items()}
    n_layers = 4
    if "prog" not in _CACHE:
        _CACHE["prog"] = build_program(n_layers)
    nc, _ = _CACHE["prog"]
    ident, dec, mask01, maskb = _consts()
    gcols = _gcols(inp)
    shared = {
        "gcols": gcols, "ident": ident, "dec": dec, "mask01": mask01, "maskb": maskb,
        "ret_w_in": np.ascontiguousarray(inp["ret_w_in"], dtype=np.float32),
        "ret_w_out": np.ascontiguousarray(inp["ret_w_out"], dtype=np.float32),
        "mla_w_in": np.ascontiguousarray(inp["mla_w_in"], dtype=np.float32),
        "mla_w_uq": np.ascontiguousarray(inp["mla_w_uq"], dtype=np.float32),
        "mla_w_uk": np.ascontiguousarray(inp["mla_w_uk"], dtype=np.float32).reshape(2, 256, 2048),
        "mla_w_uv": np.ascontiguousarray(inp["mla_w_uv"], dtype=np.float32).reshape(2, 256, 2048),
        "mla_w_out": np.ascontiguousarray(inp["mla_w_out"], dtype=np.float32),
        "ple_w_proj": np.ascontiguousarray(inp["ple_w_proj"], dtype=np.float32),
        "ple_w_gate": np.ascontiguousarray(inp["ple_w_gate"], dtype=np.float32),
    }
    in_maps = []
    for core in range(8):
        b = core % 4
        m = dict(shared)
        m["x"] = np.ascontiguousarray(inp["x"][b], dtype=np.float32)
        m["p"] = np.ascontiguousarray(inp["p"][:, b], dtype=np.float32)
        m["pos"] = np.ascontiguousarray(inp["positions"][b].reshape(1, T), dtype=np.int32)
        in_maps.append(m)
    res = run_bass_kernel_spmd(nc, in_maps, core_ids=list(range(8)))
    out = np.stack([np.asarray(res.results[b]["out"], dtype=np.float32) for b in range(4)], axis=0)
    return out
```

```python
from contextlib import ExitStack
import numpy as np
import concourse.bass as bass
import concourse.mybir as mybir
from concourse.bass_utils import run_bass_kernel_spmd

F32 = mybir.dt.float32
BF16 = mybir.dt.bfloat16
I32 = mybir.dt.int32
ALU = mybir.AluOpType
AF = mybir.ActivationFunctionType

T = 8192
D = 1024
NT = 256
NTILES = T // NT
NBLK = NT // 128
EPS = 1e-6
GAMMAS = [1.0 - 2.0 ** (-5.0 - h) for h in range(4)]
CDEC = [g ** 128 for g in GAMMAS]
SM_SCALE = float(192.0 ** -0.5)

GC_PRE, GC_POST, GC_GN, GC_QN, GC_KVN, GC_INVF, GC_SGN, GC_N = 0, 32, 64, 96, 108, 112, 114, 115

EPOCH = 24000
DBG = {'stop': 99, 'tiles': None}
DMA_RING = 8
RING = {'sp': 8, 'pool': 2}


class Tok:
    __slots__ = ("eng", "idx", "needed", "sem", "val")

    def __init__(self, eng, idx):
        self.eng = eng
        self.idx = idx
        self.needed = False
        self.sem = None
        self.val = None


class Buf:
    __slots__ = ("name", "w", "r", "rd")

    def __init__(self, name=""):
        self.name = name
        self.w = None
        self.r = {}
        self.rd = []


def bufs(n, name=""):
    return [Buf(f"{name}{i}") for i in range(n)]


class Sched:
    QUEUES = ("sp", "pool")

    def __init__(self, nc):
        self.nc = nc
        self.ops = {e: [] for e in ("pe", "act", "dve", "pool", "sp")}
        self.dma_count = {"sp": 0, "pool": 0}
        self.dma_last = {q: [None] * RING[q] for q in ("sp", "pool")}
        self.last_tok = {}

    def _deps(self, eng, reads, writes, is_dma):
        waits = []
        for b in reads:
            if b.w is not None:
                waits.append(b.w)
        for b in writes:
            if b.w is not None:
                waits.append(b.w)
            waits.extend(b.r.values())
            waits.extend(b.rd)
        if eng == "pe" and not is_dma:
            waits = [t for t in waits if t.eng != "pe"]
        return waits

    def op(self, eng, fn, reads=(), writes=()):
        tok = Tok(eng, len(self.ops[eng]))
        waits = self._deps(eng, reads, writes, False)
        for t in waits:
            t.needed = True
        for b in reads:
            b.r[eng] = tok
        for b in writes:
            b.w = tok
            b.r = {}
            b.rd = []
        self.ops[eng].append((fn, waits, tok, False))
        self.last_tok[eng] = tok
        return tok

    def dma(self, q, out_ap, in_ap, reads=(), writes=()):
        n = self.dma_count[q]
        self.dma_count[q] = n + 1
        slot = n % RING[q]
        tok = Tok("dma_" + q, slot)
        tok.val = 16 * (n // RING[q] + 1)
        tok.needed = True
        waits = self._deps(q, reads, writes, True)
        prev = self.dma_last[q][slot]
        if prev is not None:
            waits.append(prev)
        self.dma_last[q][slot] = tok
        for t in waits:
            t.needed = True
        for b in reads:
            b.rd.append(tok)
        for b in writes:
            b.w = tok
            b.r = {}
            b.rd = []

        def fn(e, out_ap=out_ap, in_ap=in_ap):
            return e.dma_start(out=out_ap, in_=in_ap)
        self.ops[q].append((fn, waits, tok, True))
        return tok

    def barrier(self):
        lasts = []
        for e in ("pe", "act", "dve", "pool"):
            t = self.last_tok.get(e)
            if t is not None:
                t.needed = True
                lasts.append(t)
        for q in self.QUEUES:
            for t in self.dma_last[q]:
                if t is not None:
                    lasts.append(t)
        for e in ("act", "dve", "pool", "sp"):
            self.ops[e].append((None, list(lasts), None, False))

    def emit(self, stack):
        nc = self.nc
        n_sig = {e: sum(1 for o in self.ops[e] if o[0] is not None and (not o[3]) and o[2].needed)
                 for e in self.ops}
        eng_sems = {}
        for e in self.ops:
            k = max(1, (n_sig[e] + EPOCH - 1) // EPOCH)
            eng_sems[e] = [stack.enter_context(nc.semaphore(f"s_{e}_{i}")) for i in range(k)]
        dma_sems = {q: [stack.enter_context(nc.semaphore(f"d_{q}_{i}")) for i in range(RING[q])]
                    for q in self.QUEUES}
        for e in self.ops:
            c = 0
            for (fn, waits, tok, is_dma) in self.ops[e]:
                if fn is None:
                    continue
                if is_dma:
                    tok.sem = dma_sems[e][tok.idx]
                elif tok.needed:
                    tok.sem = eng_sems[e][c // EPOCH]
                    tok.val = c % EPOCH + 1
                    c += 1
        block = stack.enter_context(nc.Block())
        engobj = {"pe": "tensor", "act": "scalar", "dve": "vector", "pool": "gpsimd", "sp": "sync"}
        stats = {}
        for e in ("sp", "pool", "pe", "act", "dve"):
            ops = self.ops[e]

            def body(eng, ops=ops, e=e):
                seen = {}
                nw = 0
                for (fn, waits, tok, is_dma) in ops:
                    need = {}
                    for t in waits:
                        key = id(t.sem)
                        if seen.get(key, 0) >= t.val:
                            continue
                        if key not in need or need[key][1] < t.val:
                            need[key] = (t.sem, t.val)
                    for key, (sem, val) in need.items():
                        eng.wait_ge(sem, val)
                        seen[key] = val
                        nw += 1
                    if fn is None:
                        continue
                    ins = fn(eng)
                    if is_dma:
                        ins.then_inc(tok.sem, 16)
                    elif tok.needed:
                        ins.then_inc(tok.sem, 1)
                if e in self.QUEUES:
                    for t in self.dma_last[e]:
                        if t is not None and seen.get(id(t.sem), 0) < t.val:
                            eng.wait_ge(t.sem, t.val)
                            seen[id(t.sem)] = t.val
                stats[e] = (len(ops), nw)
            getattr(block, engobj[e])(body)
        return stats


def build_program(n_layers=4, layers=None):
    nc = bass.Bass("TRN2", target_bir_lowering=False)

    def din(name, shape, dt=F32):
        return nc.dram_tensor(name, shape, dt, kind="ExternalInput").ap()

    x_d = din("x", [T, D])
    p_d = din("p", [4, T, 256])
    pos_d = din("pos", [1, T], I32)
    gc_d = din("gcols", [128, GC_N])
    ident_d = din("ident", [128, 128])
    dec_d = din("dec", [128, 8, 128])
    mask01_d = din("mask01", [128, 128])
    maskb_d = din("maskb", [128, 4 * NT])
    ret_w_in = din("ret_w_in", [2, D, 6144])
    ret_w_out = din("ret_w_out", [2, 2048, D])
    mla_w_in = din("mla_w_in", [2, D, 3136])
    mla_w_uq = din("mla_w_uq", [2, 768, 3072])
    mla_w_uk = din("mla_w_uk", [2, 256, 2048])
    mla_w_uv = din("mla_w_uv", [2, 256, 2048])
    mla_w_out = din("mla_w_out", [2, 2048, D])
    ple_w_proj = din("ple_w_proj", [4, 256, D])
    ple_w_gate = din("ple_w_gate", [4, D, D])
    out_d = nc.dram_tensor("out", [T, D], F32, kind="ExternalOutput").ap()
    xT_d = nc.dram_tensor("xT_s", [128, 8, T], F32).ap()
    pT_d = nc.dram_tensor("pT_s", [4, 128, 2, T], BF16).ap()
    tab_d = nc.dram_tensor("tab_s", [4, 128, T], F32).ap()

    S = Sched(nc)
    st = ExitStack()

    def sb(name, shape, dt):
        return st.enter_context(nc.sbuf_tensor(name, shape, dt))

    def psum(name):
        return st.enter_context(nc.psum_tensor(name, [128, 512], F32))

    def mm(out, lhsT, rhs, start, stop, reads, writes):
        S.op("pe", lambda e: e.matmul(out, lhsT=lhsT, rhs=rhs, start=start, stop=stop), reads, writes)

    def tr(out, in_, ident, reads, writes):
        S.op("pe", lambda e: e.transpose(out=out, in_=in_, identity=ident), reads, writes)

    def act(out, in_, func, reads, writes, scale=1.0, bias=None):
        if bias is None:
            S.op("act", lambda e: e.activation(out=out, in_=in_, func=func, scale=scale), reads, writes)
        else:
            S.op("act", lambda e: e.activation(out=out, in_=in_, func=func, scale=scale, bias=bias),
                 reads, writes)

    def cp(eng, out, in_, reads, writes):
        if eng == "act":
            S.op("act", lambda e: e.copy(out=out, in_=in_), reads, writes)
        else:
            S.op(eng, lambda e: e.tensor_copy(out=out, in_=in_), reads, writes)

    def tt(eng, out, in0, in1, op, reads, writes):
        S.op(eng, lambda e: e.tensor_tensor(out=out, in0=in0, in1=in1, op=op), reads, writes)

    def ts(eng, out, in0, s1, s2, op0, op1, reads, writes):
        if s2 is None:
            S.op(eng, lambda e: e.tensor_scalar(out=out, in0=in0, scalar1=s1, scalar2=None, op0=op0),
                 reads, writes)
        else:
            S.op(eng, lambda e: e.tensor_scalar(out=out, in0=in0, scalar1=s1, scalar2=s2, op0=op0, op1=op1),
                 reads, writes)

    def stt(eng, out, in0, scalar, in1, op0, op1, reads, writes):
        eng = "dve"
        S.op(eng, lambda e: e.scalar_tensor_tensor(out=out, in0=in0, scalar=scalar, in1=in1, op0=op0, op1=op1),
             reads, writes)

    def memset(eng, ap, val, writes):
        S.op(eng, lambda e: e.memset(ap, val), (), writes)

    dumps = []

    def dump(name, ap, reads):
        if not DBG.get('dump'):
            return
        dd = nc.dram_tensor(name, list(ap.shape), ap.dtype, kind="ExternalOutput").ap()
        S.dma("sp", dd, ap, reads=reads)
        dumps.append(name)

    gc = sb("gc", [128, GC_N], F32); GCb = Buf()
    identf = sb("identf", [128, 128], F32); IDF = Buf()
    identb = sb("identb", [128, 128], BF16); IDB = Buf()
    onesb = sb("onesb", [128, 128], BF16); ONB = Buf()
    onesf = sb("onesf", [128, 128], F32); ONF = Buf()
    epst = sb("epst", [128, 1], F32); EPSb = Buf()
    xt = sb("xt", [128, 8, NT], F32); XT = bufs(8)
    hT = sb("hT", [128, 8, NT], BF16); HT = bufs(8)
    rstd = sb("rstd", [128, NT], F32); RSTD = Buf()
    NW = 4
    wp = [sb(f"wp{i}", [128, 8, 512], BF16) for i in range(NW)]
    WP = bufs(NW)
    ybuf = sb("ybuf", [128, 8, NT], F32); YB = bufs(8)
    ysq = sb("ysq", [128, 8, NT], BF16); YSQ = bufs(8)
    u = sb("u", [128, 16, NT], BF16); U = bufs(16)
    pt = sb("pt", [128, 2, NT], BF16); PTb = Buf()
    sig = sb("sig", [128, NT], F32); SIG = Buf()
    tA = sb("tA", [128, NT], F32); TA = Buf()
    tB = sb("tB", [128, NT], F32); TB = Buf()
    cs = sb("cs", [128, 2, NT], F32); CS = Buf()
    wr_t = sb("wr_t", [128, 6, 512], BF16); WRb = Buf()
    wrs_t = sb("wrs_t", [128, 6, 512], BF16); WRSb = Buf()
    UN_BYTES = 121 * 1024
    un = sb("un", [128, UN_BYTES // 2], BF16)

    class Carver:
        def __init__(self):
            self.off = 0

        def get(self, shape, dt):
            n = int(np.prod(shape[1:]))
            nb = n * (4 if dt == F32 else 2)
            nb_al = (nb + 63) // 64 * 64
            assert self.off + nb_al <= UN_BYTES, (self.off, nb_al)
            v = un[:, self.off // 2: self.off // 2 + nb // 2]
            self.off += nb_al
            if dt == F32:
                v = v.bitcast(F32)
            if len(shape) == 3:
                v = v.rearrange("p (a b) -> p a b", a=shape[1])
            return v

    pb = [psum(f"pb{i}") for i in range(8)]
    PB = bufs(8, "pb")
    pj_rr = [0]

    def pj():
        i = 1 + (pj_rr[0] % 2)
        pj_rr[0] += 1
        return pb[i], PB[i]

    wp_rr = [0]

    def wslot():
        i = wp_rr[0] % NW
        wp_rr[0] += 1
        return wp[i], WP[i]

    WSC = {}

    def wscr(key, shape):
        ap = nc.dram_tensor("wb_" + "_".join(str(k) for k in key), shape, BF16).ap()
        WSC[key] = (ap, Buf())
        return WSC[key]

    def conv_jobs(l):
        jobs = []

        def full(key, src2d):
            ap, B = wscr(key, list(src2d.shape))
            jobs.append(lambda: S.dma("pool", ap.rearrange("(a b) n -> a (b n)", a=128),
                                      src2d.rearrange("(a b) n -> a (b n)", a=128), writes=[B]))
        j = l // 2
        if l % 2 == 0:
            full(("ret_w_in", j), ret_w_in[j])
            full(("ret_w_out", j), ret_w_out[j])
        else:
            full(("mla_w_in", j), mla_w_in[j])
            full(("mla_w_uk", j), mla_w_uk[j])
            full(("mla_w_uv", j), mla_w_uv[j])
            apk, BK = wscr(("wk", j), [1024, 256])
            wi = mla_w_in[j]
            for (d0, s0, n) in ((0, 1024, 64), (64, 1024, 64), (128, 1056, 32), (160, 1024, 32),
                                (192, 1056, 32), (224, 1024, 32)):
                jobs.append(lambda d0=d0, s0=s0, n=n: S.dma("pool", apk[:, d0:d0 + n], wi[:, s0:s0 + n],
                                                           writes=[BK]))
            apr, BR = wscr(("wr", j), [768, 1024])
            aps, BS = wscr(("wrs", j), [768, 1024])
            uq4 = mla_w_uq[j].rearrange("r (h e) -> r h e", e=192)
            apr4 = apr.rearrange("r (h e) -> r h e", e=64)
            aps4 = aps.rearrange("r (h e) -> r h e", e=64)
            for kc in range(6):
                rs = slice(kc * 128, (kc + 1) * 128)
                jobs.append(lambda rs=rs: S.dma("pool", apr4[rs, :, :], uq4[rs, :, 128:192], writes=[BR]))
                jobs.append(lambda rs=rs: S.dma("pool", aps4[rs, :, 0:32], uq4[rs, :, 160:192], writes=[BS]))
                jobs.append(lambda rs=rs: S.dma("pool", aps4[rs, :, 32:64], uq4[rs, :, 128:160], writes=[BS]))
            full(("mla_w_uq", j), mla_w_uq[j])
            full(("mla_w_out", j), mla_w_out[j])
        full(("ple_w_gate", l), ple_w_gate[l])
        full(("ple_w_proj", l), ple_w_proj[l])
        return jobs

    pending = []

    def conv_step(n=1):
        for _ in range(n):
            if pending:
                pending.pop(0)()

    def wload(key, r0, nrows, cols, slot=None):
        w2d, WBUF = WSC[key]
        tile_, b = slot if slot is not None else wslot()
        kc = nrows // 128
        o = 0
        for (c0, ncols) in cols:
            src = w2d[r0:r0 + nrows, c0:c0 + ncols].rearrange("(k p) n -> p k n", p=128)
            S.dma("sp", tile_[:, 0:kc, o:o + ncols], src, reads=[WBUF], writes=[b])
            o += ncols
        return tile_, b

    if layers is None:
        layers = list(range(n_layers))
    if layers:
        for jb in conv_jobs(layers[0]):
            jb()
    S.dma("sp", gc[:], gc_d, writes=[GCb])
    S.dma("sp", identf[:], ident_d, writes=[IDF])
    S.dma("pool", identb[:], ident_d, writes=[IDB])
    memset("dve", onesb[:], 1.0, [ONB])
    memset("dve", onesf[:], 1.0, [ONF])
    memset("dve", epst[:], EPS, [EPSb])

    C1 = 6.28125
    C2 = float(2 * np.pi - 6.28125)
    PCH = 2048
    cvp = Carver()
    posi = cvp.get([128, PCH], F32).bitcast(I32); POSI = Buf()
    posf = cvp.get([128, PCH], F32); POSF = Buf()
    angb = cvp.get([128, PCH], F32); ANG = Buf()
    a2b = cvp.get([128, PCH], F32); A2 = Buf()
    kfb = cvp.get([128, PCH], F32); KF = Buf()
    kib = cvp.get([128, PCH], F32).bitcast(I32); KI = Buf()
    for ch in range(T // PCH):
        tsl = slice(ch * PCH, (ch + 1) * PCH)
        S.dma("sp", posi[:], pos_d[0:1, tsl].partition_broadcast(128), writes=[POSI])
        cp("dve", posf[:], posi[:], [POSI], [POSF])
        for sset in range(2):
            ts("dve", angb[:], posf[:], gc[:, GC_INVF + sset:GC_INVF + sset + 1], None, ALU.mult, None,
               [POSF, GCb], [ANG])
            for which in range(2):
                shift = float(np.pi / 2) if which == 0 else 0.0
                ts("dve", a2b[:], angb[:], shift, None, ALU.add, None, [ANG], [A2])
                ts("dve", kfb[:], a2b[:], float(1.0 / (2 * np.pi)), None, ALU.mult, None, [A2], [KF])
                cp("dve", kib[:], kfb[:], [KF], [KI])
                cp("dve", kfb[:], kib[:], [KI], [KF])
                stt("dve", a2b[:], kfb[:], -C1, a2b[:], ALU.mult, ALU.add, [KF, A2], [A2])
                stt("dve", a2b[:], kfb[:], -C2, a2b[:], ALU.mult, ALU.add, [KF, A2], [A2])
                ts("dve", kfb[:], a2b[:], float(np.pi), float(-2 * np.pi), ALU.is_gt, ALU.mult, [A2], [KF])
                tt("dve", a2b[:], a2b[:], kfb[:], ALU.add, [A2, KF], [A2])
                ts("dve", a2b[:], a2b[:], -3.1415925, 3.1415925, ALU.max, ALU.min, [A2], [A2])
                act(a2b[:], a2b[:], AF.Sin, [A2], [A2])
                if sset == 1 and which == 1:
                    ts("dve", a2b[:], a2b[:], gc[:, GC_SGN:GC_SGN + 1], None, ALU.mult, None, [A2, GCb], [A2])
                S.dma("pool", tab_d[sset * 2 + which, :, tsl], a2b[:], reads=[A2])

    xin = cvp.get([128, D], F32); XIN = Buf()
    pin = cvp.get([128, 4, 256], F32); PIN = Buf()
    xst = cvp.get([128, 8, 512], F32); XST = Buf()
    pst = cvp.get([128, 8, 512], BF16); PST = Buf()
    for it in range(T // 512):
        for bl in range(4):
            t0 = it * 512 + bl * 128
            S.dma("sp", xin[:], x_d[t0:t0 + 128, :], writes=[XIN])
            S.dma("sp", pin[:], p_d[:, t0:t0 + 128, :].rearrange("l t e -> t l e"), writes=[PIN])
            for half in range(2):
                bank, B = pb[3 + half], PB[3 + half]
                for c4 in range(4):
                    c = half * 4 + c4
                    tr(bank[:, c4 * 128:(c4 + 1) * 128], xin[:, c * 128:(c + 1) * 128], identf[:],
                       [XIN, IDF], [B])
                cp("act" if half == 0 else "dve",
                   xst[:, half * 4:half * 4 + 4, bl * 128:(bl + 1) * 128],
                   bank[:].rearrange("p (c n) -> p c n", c=4), [B], [XST])
            for half in range(2):
                bank, B = pb[5 + half], PB[5 + half]
                for c4 in range(4):
                    c = half * 4 + c4
                    tr(bank[:, c4 * 128:(c4 + 1) * 128], pin[:, c // 2, (c % 2) * 128:(c % 2 + 1) * 128],
                       identf[:], [PIN, IDF], [B])
                cp("act" if half == 1 else "dve",
                   pst[:, half * 4:half * 4 + 4, bl * 128:(bl + 1) * 128],
                   bank[:].rearrange("p (c n) -> p c n", c=4), [B], [PST])
        S.dma("pool", xT_d[:, :, it * 512:(it + 1) * 512], xst[:], reads=[XST])
        for l in range(4):
            S.dma("pool", pT_d[l, :, :, it * 512:(it + 1) * 512], pst[:, l * 2:l * 2 + 2, :], reads=[PST])
    XTD = Buf()
    XTD_t = bufs(NTILES)
    S.barrier()

    def prenorm(l, ti):
        tsl = slice(ti * NT, (ti + 1) * NT)
        conv_step(2)
        S.dma("sp", xt[:], xT_d[:, :, tsl], reads=[XTD_t[ti]], writes=XT)
        act(hT[:].rearrange("p c n -> p (c n)"), xt[:].rearrange("p c n -> p (c n)"), AF.Square, XT, HT)
        for c in range(8):
            mm(pb[0][:, 0:NT], onesb[:], hT[:, c, :], c == 0, c == 7, [ONB, HT[c]], [PB[0]])
        act(rstd[:], pb[0][:, 0:NT], AF.Ln, [PB[0], EPSb], [RSTD], scale=1.0 / D, bias=epst[:])
        act(rstd[:], rstd[:], AF.Exp, [RSTD], [RSTD], scale=-0.5)
        for c in range(8):
            stt("dve" if c % 2 == 0 else "pool", hT[:, c, :], xt[:, c, :],
                gc[:, GC_PRE + l * 8 + c:GC_PRE + l * 8 + c + 1], rstd[:], ALU.mult, ALU.mult,
                [XT[c], GCb, RSTD], [HT[c]])

    def tail(l, ti, wout_key, last):
        tsl = slice(ti * NT, (ti + 1) * NT)
        for og in range(2):
            wa, WA = wload(wout_key, 0, 1024, [(og * 512, 512)])
            wb, WB = wload(wout_key, 1024, 1024, [(og * 512, 512)])
            for o4 in range(4):
                oc = og * 4 + o4
                bank, B = pj()
                KK = DBG.get('kk', 16)
                for kc in range(KK):
                    w_, W_ = (wa, WA) if kc < 8 else (wb, WB)
                    mm(bank[:, 0:NT], w_[:, kc % 8, o4 * 128:(o4 + 1) * 128], u[:, kc, :], kc == 0, kc == KK - 1,
                       [W_, U[kc]], [B])
                cp("act", ybuf[:, oc, :], bank[:, 0:NT], [B], [YB[oc]])
                act(ysq[:, oc, :], bank[:, 0:NT], AF.Square, [B], [YSQ[oc]])
        if DBG['stop'] < 6:
            return
        for c in range(8):
            mm(pb[0][:, 0:NT], onesb[:], ysq[:, c, :], c == 0, c == 7, [ONB, YSQ[c]], [PB[0]])
        act(rstd[:], pb[0][:, 0:NT], AF.Ln, [PB[0], EPSb], [RSTD], scale=1.0 / D, bias=epst[:])
        act(rstd[:], rstd[:], AF.Exp, [RSTD], [RSTD], scale=-0.5)
        for c in range(8):
            e1 = "dve" if c % 2 == 0 else "pool"
            stt(e1, ybuf[:, c, :], ybuf[:, c, :], gc[:, GC_POST + l * 8 + c:GC_POST + l * 8 + c + 1], rstd[:],
                ALU.mult, ALU.mult, [YB[c], GCb, RSTD], [YB[c]])
            tt(e1, xt[:, c, :], xt[:, c, :], ybuf[:, c, :], ALU.add, [XT[c], YB[c]], [XT[c]])
            cp("act", hT[:, c, :], xt[:, c, :], [XT[c]], [HT[c]])
        if DBG['stop'] < 7:
            return
        S.dma("sp", pt[:], pT_d[l, :, :, tsl], writes=[PTb])
        for og in range(2):
            wg, WG = wload(("ple_w_gate", l), 0, 1024, [(og * 512, 512)])
            wq, WQ = wload(("ple_w_proj", l), 0, 256, [(og * 512, 512)])
            for o4 in range(4):
                oc = og * 4 + o4
                bank, B = pj()
                for kc in range(8):
                    mm(bank[:, 0:NT], wg[:, kc, o4 * 128:(o4 + 1) * 128], hT[:, kc, :], kc == 0, kc == 7,
                       [WG, HT[kc]], [B])
                for kc in range(2):
                    mm(bank[:, NT:2 * NT], wq[:, kc, o4 * 128:(o4 + 1) * 128], pt[:, kc, :], kc == 0, kc == 1,
                       [WQ, PTb], [B])
                act(sig[:], bank[:, 0:NT], AF.Sigmoid, [B], [SIG])
                tt("dve", tA[:], bank[:, NT:2 * NT], sig[:], ALU.mult, [B, SIG], [TA])
                tt("pool", xt[:, oc, :], xt[:, oc, :], tA[:], ALU.add, [XT[oc], TA], [XT[oc]])
        if DBG['stop'] < 8:
            return
        if not last:
            S.dma("pool", xT_d[:, :, tsl], xt[:], reads=XT, writes=[XTD_t[ti]])
        else:
            otile = ybuf[:].rearrange("p c n -> p (c n)")[:, 0:1024]
            for bl in range(NBLK):
                for half in range(2):
                    bank, B = pb[3 + half], PB[3 + half]
                    for c4 in range(4):
                        c = half * 4 + c4
                        tr(bank[:, c4 * 128:(c4 + 1) * 128], xt[:, c, bl * 128:(bl + 1) * 128], identf[:],
                           [XT[c], IDF], [B])
                    cp("act" if half == 0 else "dve", otile[:, half * 512:(half + 1) * 512], bank[:],
                       [B], YB)
                t0 = ti * NT + bl * 128
                S.dma("pool", out_d[t0:t0 + 128, :], otile, reads=YB)

    def ret_layer(l, last):
        j = l // 2
        w_in = ("ret_w_in", j)
        cv = Carver()
        dec = cv.get([128, 8, 128], F32); DEC = Buf()
        mask01 = cv.get([128, 128], F32); M01 = Buf()
        stage = cv.get([128, 2, NT], F32); STG = bufs(2)
        qh2 = [cv.get([128, 2, NT], BF16) for _ in range(2)]; QH2 = bufs(2)
        kh2 = [cv.get([128, 2, NT], BF16) for _ in range(2)]; KH2 = bufs(2)
        ktok2 = [cv.get([128, NBLK, 256], BF16) for _ in range(2)]; KTOK2 = [bufs(NBLK) for _ in range(2)]
        vt2 = [cv.get([128, NBLK, 512], BF16) for _ in range(2)]; VT2 = [bufs(NBLK) for _ in range(2)]
        sg2 = [cv.get([128, 4, NT], BF16) for _ in range(2)]; SG2 = [bufs(4) for _ in range(2)]
        Sst = [cv.get([128, 2, 512], F32) for _ in range(4)]; SST = [bufs(2) for _ in range(4)]
        Sbf = [cv.get([128, 2, 512], BF16) for _ in range(4)]; SBF = [bufs(2) for _ in range(4)]
        am2 = [cv.get([128, 128], BF16) for _ in range(2)]; AM2 = bufs(2)
        oraw2 = [cv.get([128, 4, NT], BF16) for _ in range(2)]; ORAW2 = bufs(2)
        osq2 = [cv.get([128, 4, NT], BF16) for _ in range(2)]; OSQ2 = bufs(2)
        rso = cv.get([128, NT], F32); RSO = Buf()
        S.dma("sp", dec, dec_d, writes=[DEC])
        S.dma("sp", mask01, mask01_d, writes=[M01])
        for h in range(4):
            for dc in range(2):
                memset("pool", Sst[h][:, dc, :], 0.0, [SST[h][dc]])
                memset("pool", Sbf[h][:, dc, :], 0.0, [SBF[h][dc]])
        pbT = pb[7][:].bitcast(BF16)

        def proj_groups(h):
            s_ = h % 2
            qh, QH, kh, KH = qh2[s_], QH2[s_], kh2[s_], KH2[s_]
            ktok, KTOK, vt, VT, sg, SG = ktok2[s_], KTOK2[s_], vt2[s_], VT2[s_], sg2[s_], SG2[s_]
            wl = {}

            def g_qk(which, half):
                if which == 0 and half == 0:
                    wl['qk'] = wload(w_in, 0, 1024, [(h * 256, 256), (1024 + h * 256, 256)])
                    wl['v'] = wload(w_in, 0, 1024, [(2048 + h * 512, 512)])
                    wl['g'] = wload(w_in, 0, 1024, [(4096 + h * 512, 512)])
                wqk, WQK = wl['qk']
                dst, DST = (qh, QH) if which == 0 else (kh, KH)
                bank, B = pj()
                c0 = which * 256 + half * 128
                for kc in range(8):
                    mm(bank[:, 0:NT], wqk[:, kc, c0:c0 + 128], hT[:, kc, :], kc == 0, kc == 7,
                       [WQK, HT[kc]], [B])
                tt("dve", stage[:, half, :].rearrange("p (c n) -> p c n", c=NBLK),
                   bank[:, 0:NT].rearrange("p (c n) -> p c n", c=NBLK),
                   dec[:, which * 4 + h, :].unsqueeze(1).to_broadcast([128, NBLK, 128]),
                   ALU.mult, [B, DEC], [STG[half]])
                if half == 1:
                    tt("dve", tA[:], stage[:, 0, :], cs[:, 0, :], ALU.mult, [STG[0], CS], [TA])
                    tt("pool", tB[:], stage[:, 1, :], cs[:, 1, :], ALU.mult, [STG[1], CS], [TB])
                    tt("dve", dst[:, 0, :], tA[:], tB[:], ALU.subtract, [TA, TB], [DST])
                    tt("pool", tA[:], stage[:, 1, :], cs[:, 0, :], ALU.mult, [STG[1], CS], [TA])
                    tt("dve", tB[:], stage[:, 0, :], cs[:, 1, :], ALU.mult, [STG[0], CS], [TB])
                    tt("pool", dst[:, 1, :], tA[:], tB[:], ALU.add, [TA, TB], [DST])

            def g_ktok():
                for c in range(NBLK):
                    for dc in range(2):
                        tr(pbT[:, dc * 128:(dc + 1) * 128], kh[:, dc, c * 128:(c + 1) * 128], identb[:],
                           [KH, IDB], [PB[7]])
                    S.op("act", lambda e, c=c: e.mul(out=ktok[:, c, :], in_=pbT[:, 0:256], mul=float(CDEC[h])),
                         [PB[7]], [KTOK[c]])

            def g_v(c):
                wv, WV = wl['v']
                bank, B = pj()
                for kc in range(8):
                    mm(bank[:, 0:512], hT[:, kc, c * 128:(c + 1) * 128], wv[:, kc, :], kc == 0, kc == 7,
                       [WV, HT[kc]], [B])
                cp("act", vt[:, c, :], bank[:, 0:512], [B], [VT[c]])

            def g_gate(vc):
                wg, WG = wl['g']
                bank, B = pj()
                for kc in range(8):
                    mm(bank[:, 0:NT], wg[:, kc, vc * 128:(vc + 1) * 128], hT[:, kc, :], kc == 0, kc == 7,
                       [WG, HT[kc]], [B])
                act(sg[:, vc, :], bank[:, 0:NT], AF.Silu, [B], [SG[vc]])

            gl = [lambda: g_qk(0, 0), lambda: g_qk(0, 1), lambda: g_qk(1, 0), lambda: g_qk(1, 1)]
            gl += [lambda c=c: g_v(c) for c in range(NBLK)]
            gl += [g_ktok]
            gl += [lambda vc=vc: g_gate(vc) for vc in range(4)]
            return gl

        def scan_steps(h, ti):
            s_ = h % 2
            qh, QH, kh, KH = qh2[s_], QH2[s_], kh2[s_], KH2[s_]
            ktok, KTOK, vt, VT = ktok2[s_], KTOK2[s_], vt2[s_], VT2[s_]
            oraw, ORAW, osq, OSQ = oraw2[s_], ORAW2[s_], osq2[s_], OSQ2[s_]
            steps = []
            for c in range(NBLK):
                csl = slice(c * 128, (c + 1) * 128)
                am, AM = am2[c % 2], AM2[c % 2]
                sb_i = 3 + (c % 2)
                first = (ti == 0 and c == 0)

                def s_at(c=c, csl=csl, am=am, AM=AM, sb_i=sb_i):
                    for dc in range(2):
                        mm(pb[sb_i][:, 0:128], kh[:, dc, csl], qh[:, dc, csl], dc == 0, dc == 1,
                           [KH, QH], [PB[sb_i]])
                    tt("dve", am, pb[sb_i][:, 0:128], mask01, ALU.mult, [PB[sb_i], M01], [AM])

                def s_o(c=c, csl=csl, am=am, AM=AM, first=first):
                    ob, OB = pb[5], PB[5]
                    for vc in range(4):
                        mm(ob[:, vc * 128:(vc + 1) * 128], vt[:, c, vc * 128:(vc + 1) * 128], am, True, first,
                           [VT[c], AM], [OB])
                        if not first:
                            for dc in range(2):
                                mm(ob[:, vc * 128:(vc + 1) * 128], Sbf[h][:, dc, vc * 128:(vc + 1) * 128],
                                   qh[:, dc, csl], False, dc == 1, [SBF[h][dc], QH], [OB])
                    obv = ob[:].rearrange("p (v n) -> p v n", v=4)
                    cp("act", oraw[:, :, csl], obv, [OB], [ORAW])
                    act(osq[:, :, csl], obv, AF.Square, [OB], [OSQ])

                def s_state(c=c):
                    for dc in range(2):
                        sbk, SBK = pb[6], PB[6]
                        mm(sbk[:, 0:512], ktok[:, c, dc * 128:(dc + 1) * 128], vt[:, c, :], True, True,
                           [KTOK[c], VT[c]], [SBK])
                        stt("dve", Sst[h][:, dc, :], Sst[h][:, dc, :], float(CDEC[h]), sbk[:, 0:512],
                            ALU.mult, ALU.add, [SST[h][dc], SBK], [SST[h][dc]])
                        cp("pool", Sbf[h][:, dc, :], Sst[h][:, dc, :], [SST[h][dc]], [SBF[h][dc]])
                steps += [s_at, s_o, s_state]
            return steps

        def norm_head(h):
            s_ = h % 2
            oraw, ORAW, osq, OSQ, sg, SG = oraw2[s_], ORAW2[s_], osq2[s_], OSQ2[s_], sg2[s_], SG2[s_]
            for vc in range(4):
                mm(pb[0][:, 0:NT], onesb[:], osq[:, vc, :], vc == 0, vc == 3, [ONB, OSQ], [PB[0]])
            act(rso, pb[0][:, 0:NT], AF.Ln, [PB[0], EPSb], [RSO], scale=1.0 / 512, bias=epst[:])
            act(rso, rso, AF.Exp, [RSO], [RSO], scale=-0.5)
            for vc in range(4):
                gcol = GC_GN + j * 16 + h * 4 + vc
                stt("dve", oraw[:, vc, :], oraw[:, vc, :], gc[:, gcol:gcol + 1], rso, ALU.mult, ALU.mult,
                    [ORAW, GCb, RSO], [ORAW])
                tt("dve", u[:, h * 4 + vc, :], oraw[:, vc, :], sg[:, vc, :], ALU.mult,
                   [ORAW, SG[vc]], [U[h * 4 + vc]])

        for ti in range(NTILES if DBG['tiles'] is None else DBG['tiles']):
            tsl = slice(ti * NT, (ti + 1) * NT)
            prenorm(l, ti)
            S.dma("sp", cs[:, 0, :], tab_d[0, :, tsl], writes=[CS])
            S.dma("sp", cs[:, 1, :], tab_d[1, :, tsl], writes=[CS])
            for g in proj_groups(0):
                g()
            for h in range(4):
                P = proj_groups(h + 1) if h + 1 < 4 else []
                Sx = scan_steps(h, ti)
                while P or Sx:
                    for _ in range(2):
                        if P:
                            P.pop(0)()
                    if Sx:
                        Sx.pop(0)()
                norm_head(h)
            if DBG['stop'] < 5:
                continue
            tail(l, ti, ("ret_w_out", j), last)

    def mla_layer(l, last):
        j = l // 2
        w_in = ("mla_w_in", j)
        w_uq = ("mla_w_uq", j)
        cv = Carver()
        KT = cv.get([128, 3, T], BF16); KTB = bufs(T // 128)
        V = cv.get([128, T // 128, 256], BF16); VB = bufs(T // 128)
        cqn = cv.get([128, 6, NT], BF16); CQN = bufs(6)
        wukT = cv.get([128, 16, 256], BF16); WUKT = Buf()
        wuv = cv.get([128, 2, 2048], BF16); WUV = Buf()
        qrz = [cv.get([128, 2 * NT], BF16) for _ in range(2)]; QRZ = bufs(2)
        qn = cv.get([128, NT], BF16); QN = Buf()
        qc = [cv.get([128, 2, 2 * NT], BF16) for _ in range(2)]; QC = bufs(2)
        Pt = [cv.get([128, 2 * NT], BF16) for _ in range(3)]; PT3 = bufs(3)
        acc = [cv.get([128, 2 * NT], F32) for _ in range(2)]; ACC = bufs(2)
        rinv = cv.get([128, 2 * NT], F32); RINV = Buf()
        olat = cv.get([128, 2, 2 * NT], BF16); OLAT = Buf()
        sgm = [cv.get([128, 2 * NT], BF16) for _ in range(2)]; SGM = bufs(2)
        maskb = cv.get([128, 2, 2 * NT], BF16); MKB = Buf()
        for s_ in range(2):
            memset("pool", qrz[s_], 0.0, [QRZ[s_]])
        pbT = pb[7][:].bitcast(BF16)
        S.dma("pool", maskb.rearrange("p a b -> p (a b)"), maskb_d, writes=[MKB])
        S.dma("sp", wuv, WSC[("mla_w_uv", j)][0].rearrange("(k p) n -> p k n", p=128),
              reads=[WSC[("mla_w_uv", j)][1]], writes=[WUV])
        wl, WL = wslot()
        wlv = wl[:].rearrange("p a b -> p (a b)").rearrange("p (k n) -> p k n", k=2)
        S.dma("sp", wlv, WSC[("mla_w_uk", j)][0].rearrange("(k p) n -> p k n", p=128),
              reads=[WSC[("mla_w_uk", j)][1]], writes=[WL])
        for h in range(16):
            for cc in range(2):
                tr(pbT[:, cc * 128:(cc + 1) * 128], wlv[:, cc, h * 128:(h + 1) * 128], identb[:],
                   [WL, IDB], [PB[7]])
            cp("act", wukT[:, h, :], pbT[:, 0:256], [PB[7]], [WUKT])
        pcount = [0]
        ptcount = [0]
        for ti in range(NTILES if DBG['tiles'] is None else DBG['tiles']):
            tsl = slice(ti * NT, (ti + 1) * NT)
            prenorm(l, ti)
            S.dma("sp", cs[:, 0, :], tab_d[2, :, tsl], writes=[CS])
            S.dma("sp", cs[:, 1, :], tab_d[3, :, tsl], writes=[CS])
            for og in range(2):
                w_, W_ = wload(w_in, 0, 1024, [(og * 512, 512)])
                for o4 in range(4):
                    oc = og * 4 + o4
                    bank, B = pj()
                    for kc in range(8):
                        mm(bank[:, 0:NT], w_[:, kc, o4 * 128:(o4 + 1) * 128], hT[:, kc, :], kc == 0, kc == 7,
                           [W_, HT[kc]], [B])
                    cp("act", ybuf[:, oc, :], bank[:, 0:NT], [B], [YB[oc]])
                    act(ysq[:, oc, :], bank[:, 0:NT], AF.Square, [B], [YSQ[oc]])
            for c in range(6):
                mm(pb[0][:, 0:NT], onesb[:], ysq[:, c, :], c == 0, c == 5, [ONB, YSQ[c]], [PB[0]])
            act(rstd[:], pb[0][:, 0:NT], AF.Ln, [PB[0], EPSb], [RSTD], scale=1.0 / 768, bias=epst[:])
            act(rstd[:], rstd[:], AF.Exp, [RSTD], [RSTD], scale=-0.5)
            for c in range(6):
                gcol = GC_QN + j * 6 + c
                stt("dve" if c % 2 == 0 else "pool", cqn[:, c, :], ybuf[:, c, :], gc[:, gcol:gcol + 1], rstd[:],
                    ALU.mult, ALU.mult, [YB[c], GCb, RSTD], [CQN[c]])
            kb0 = ti * NBLK
            for c in range(2):
                mm(pb[0][:, 0:NT], onesb[:], ysq[:, 6 + c, :], c == 0, c == 1, [ONB, YSQ[6 + c]], [PB[0]])
            act(rstd[:], pb[0][:, 0:NT], AF.Ln, [PB[0], EPSb], [RSTD], scale=1.0 / 256, bias=epst[:])
            act(rstd[:], rstd[:], AF.Exp, [RSTD], [RSTD], scale=-0.5)
            for c in range(2):
                gcol = GC_KVN + j * 2 + c
                stt("dve" if c % 2 == 0 else "pool", KT[:, c, tsl], ybuf[:, 6 + c, :], gc[:, gcol:gcol + 1], rstd[:],
                    ALU.mult, ALU.mult, [YB[6 + c], GCb, RSTD], [KTB[kb0 + b] for b in range(NBLK)])
            for hq in range(4):
                wgt, WGT = wload(w_in, 0, 1024, [(1088 + hq * 512, 512)])
                for hp in range(2):
                    bank, B = pj()
                    for hh in range(2):
                        hl = hp * 2 + hh
                        for kc in range(8):
                            mm(bank[:, hh * NT:(hh + 1) * NT], wgt[:, kc, hl * 128:(hl + 1) * 128], hT[:, kc, :],
                               kc == 0, kc == 7, [WGT, HT[kc]], [B])
                    h0 = hq * 4 + hp * 2
                    act(u[:, h0:h0 + 2, :].rearrange("p a b -> p (a b)"), bank[:], AF.Silu, [B],
                        [U[h0], U[h0 + 1]])
            for b in range(NBLK):
                for c in range(2):
                    tr(pbT[:, c * 128:(c + 1) * 128], KT[:, c, (kb0 + b) * 128:(kb0 + b + 1) * 128], identb[:],
                       [KTB[kb0 + b], IDB], [PB[7]])
                cp("act", V[:, kb0 + b, :], pbT[:, 0:256], [PB[7]], [VB[kb0 + b]])
            wk, WK = wload(("wk", j), 0, 1024, [(0, 256)])
            bank, B = pj()
            for kc in range(8):
                mm(bank[:, 0:NT], wk[:, kc, 0:128], hT[:, kc, :], kc == 0, kc == 7, [WK, HT[kc]], [B])
            for kc in range(8):
                mm(bank[:, NT:2 * NT], wk[:, kc, 128:256], hT[:, kc, :], kc == 0, kc == 7, [WK, HT[kc]], [B])
            tt("dve", tA[:], bank[:, 0:NT], cs[:, 0, :], ALU.mult, [B, CS], [TA])
            tt("dve", tB[:], bank[:, NT:2 * NT], cs[:, 1, :], ALU.mult, [B, CS], [TB])
            tt("pool", KT[:, 2, tsl], tA[:], tB[:], ALU.add, [TA, TB], [KTB[kb0 + b] for b in range(NBLK)])
            if ti == DBG.get('dump_ti', 0):
                dump("d_hT", hT[:], HT)
                dump("d_cqn", cqn, CQN)
                dump("d_KT", KT[:, :, 0:(ti + 1) * NT], KTB[0:(ti + 1) * NBLK])
                dump("d_V", V[:, 0:(ti + 1) * NBLK, :], VB[0:(ti + 1) * NBLK])
                dump("d_cs", cs[:], [CS])
            nkt = 2 * (ti + 1)
            pctx = {}
            gstate = {}

            def prep(m):
                hg, m_loc = divmod(m, 4)
                s_ = m % 2
                if m_loc == 0:
                    for (dst, DSTB, key) in ((wr_t, WRb, ("wr", j)), (wrs_t, WRSb, ("wrs", j))):
                        S.dma("sp", dst[:], WSC[key][0][:, hg * 512:(hg + 1) * 512].rearrange(
                            "(k p) n -> p k n", p=128), reads=[WSC[key][1]], writes=[DSTB])
                bank, B = pj()
                for kc in range(6):
                    mm(bank[:, 0:NT], wr_t[:, kc, m_loc * 128:(m_loc + 1) * 128], cqn[:, kc, :],
                       kc == 0, kc == 5, [WRb, CQN[kc]], [B])
                for kc in range(6):
                    mm(bank[:, NT:2 * NT], wrs_t[:, kc, m_loc * 128:(m_loc + 1) * 128], cqn[:, kc, :],
                       kc == 0, kc == 5, [WRSb, CQN[kc]], [B])
                tt("dve", tA[:], bank[:, 0:NT], cs[:, 0, :], ALU.mult, [B, CS], [TA])
                tt("dve", tB[:], bank[:, NT:2 * NT], cs[:, 1, :], ALU.mult, [B, CS], [TB])
                tt("pool", qrz[s_][0:64, 0:NT], tA[0:64, :], tB[0:64, :], ALU.add, [TA, TB], [QRZ[s_]])
                tt("pool", qrz[s_][64:128, NT:2 * NT], tA[64:128, :], tB[64:128, :], ALU.add, [TA, TB], [QRZ[s_]])
                wn, WN = wload(w_uq, 0, 768, [(2 * m * 192, 128), ((2 * m + 1) * 192, 128)])
                for hh in range(2):
                    h = 2 * m + hh
                    hsl = slice(hh * NT, (hh + 1) * NT)
                    bank, B = pj()
                    for kc in range(6):
                        mm(bank[:, 0:NT], wn[:, kc, hh * 128:(hh + 1) * 128], cqn[:, kc, :],
                           kc == 0, kc == 5, [WN, CQN[kc]], [B])
                    cp("act", qn, bank[:, 0:NT], [B], [QN])
                    bank, B = pj()
                    for cc in range(2):
                        mm(bank[:, cc * NT:(cc + 1) * NT], wukT[:, h, cc * 128:(cc + 1) * 128], qn,
                           True, True, [WUKT, QN], [B])
                    cp("dve", qc[s_][:, :, hsl], bank[:].rearrange("p (a b) -> p a b", a=2), [B], [QC[s_]])
                pctx[m] = dict(sb={}, pt={})

            def emit_qk(m, kt):
                c = pctx[m]
                if kt in c['sb']:
                    return
                s_ = m % 2
                sbi = 3 + (pcount[0] % 2)
                pcount[0] += 1
                c['sb'][kt] = sbi
                sbank, SBK = pb[sbi], PB[sbi]
                diag = kt >= 2 * ti
                ksl = slice(kt * 128, (kt + 1) * 128)
                mm(sbank[:], KT[:, 0, ksl], qc[s_][:, 0, :], True, False, [KTB[kt], QC[s_]], [SBK])
                mm(sbank[:], KT[:, 1, ksl], qc[s_][:, 1, :], False, False, [KTB[kt], QC[s_]], [SBK])
                mm(sbank[:], KT[:, 2, ksl], qrz[s_], False, not diag, [KTB[kt], QRZ[s_]], [SBK])
                if diag:
                    mm(sbank[:], identb[:], maskb[:, kt - 2 * ti, :], False, True, [IDB, MKB], [SBK])

            def emit_exp_acc(m, kt):
                c = pctx[m]
                sbi = c['sb'][kt]
                pti = ptcount[0] % 3
                ptcount[0] += 1
                c['pt'][kt] = pti
                s_ = m % 2
                act(Pt[pti], pb[sbi][:], AF.Exp, [PB[sbi]], [PT3[pti]], scale=SM_SCALE)
                if kt == 0:
                    cp("dve", acc[s_], Pt[pti], [PT3[pti]], [ACC[s_]])
                else:
                    tt("dve", acc[s_], acc[s_], Pt[pti], ALU.add, [ACC[s_], PT3[pti]], [ACC[s_]])

            def emit_pv(m, kt):
                pti = pctx[m]['pt'][kt]
                for cc in range(2):
                    mm(pb[5 + cc][:], V[:, kt, cc * 128:(cc + 1) * 128], Pt[pti], kt == 0, kt == nkt - 1,
                       [VB[kt], PT3[pti]], [PB[5 + cc]])

            def finish_den(m):
                s_ = m % 2
                mm(pb[0][:], onesf[:], acc[s_], True, True, [ONF, ACC[s_]], [PB[0]])
                cp("dve", olat[:, 0, :], pb[5][:], [PB[5]], [OLAT])
                cp("dve", olat[:, 1, :], pb[6][:], [PB[6]], [OLAT])
                act(rinv, pb[0][:], AF.Ln, [PB[0]], [RINV])
                act(rinv, rinv, AF.Exp, [RINV], [RINV], scale=-1.0)

            def finish_uv(m):
                bank, B = pj()
                for hh in range(2):
                    h = 2 * m + hh
                    for cc in range(2):
                        mm(bank[:, hh * NT:(hh + 1) * NT], wuv[:, cc, h * 128:(h + 1) * 128],
                           olat[:, cc, hh * NT:(hh + 1) * NT], cc == 0, cc == 1, [WUV, OLAT], [B])
                u2 = u[:, 2 * m:2 * m + 2, :].rearrange("p a b -> p (a b)")
                tt("dve", rinv, bank[:], rinv, ALU.mult, [B, RINV], [RINV])
                tt("dve", u2, rinv, u2, ALU.mult, [RINV, U[2 * m], U[2 * m + 1]], [U[2 * m], U[2 * m + 1]])

            prep(0)
            emit_qk(0, 0)
            for m in range(8):
                for kt in range(nkt):
                    lastk = (kt == nkt - 1)
                    if not lastk:
                        emit_qk(m, kt + 1)
                    emit_exp_acc(m, kt)
                    if lastk and m + 1 < 8:
                        prep(m + 1)
                    emit_pv(m, kt)
                if m + 1 < 8:
                    emit_qk(m + 1, 0)
                finish_den(m)
                if m + 1 < 8:
                    emit_qk(m + 1, 1)
                finish_uv(m)
            if ti == DBG.get('dump_ti', 0):
                dump("d_u", u[:], U)
            tail(l, ti, ("mla_w_out", j), last)

    if n_layers == 0:
        for ti in range(NTILES):
            tsl = slice(ti * NT, (ti + 1) * NT)
            S.dma("sp", xt[:], xT_d[:, :, tsl], writes=XT)
            otile = ybuf[:].rearrange("p c n -> p (c n)")[:, 0:1024]
            for bl in range(NBLK):
                for half in range(2):
                    bank, B = pb[3 + half], PB[3 + half]
                    for c4 in range(4):
                        c = half * 4 + c4
                        tr(bank[:, c4 * 128:(c4 + 1) * 128], xt[:, c, bl * 128:(bl + 1) * 128], identf[:],
                           [XT[c], IDF], [B])
                    cp("act" if half == 0 else "dve", otile[:, half * 512:(half + 1) * 512], bank[:],
                       [B], YB)
                t0 = ti * NT + bl * 128
                S.dma("pool", out_d[t0:t0 + 128, :], otile, reads=YB)
    for li, l in enumerate(layers):
        last = (li == len(layers) - 1)
        if li + 1 < len(layers):
            pending.extend(conv_jobs(layers[li + 1]))
        if l % 2 == 0:
            ret_layer(l, last)
        else:
            mla_layer(l, last)
        conv_step(len(pending))
        S.barrier()

    stats = S.emit(st)
    st.close()
    return nc, stats


def _consts():
    ident = np.eye(128, dtype=np.float32)
    i = np.arange(128, dtype=np.float64)
    dec = np.zeros((128, 8, 128), np.float32)
    for h in range(4):
        lg = np.log(GAMMAS[h])
        dec[:, h, :] = np.exp((i + 1.0) * lg)[None, :]
        dec[:, 4 + h, :] = (np.exp(-(i + 1.0) * lg) * (256.0 ** -0.5))[None, :]
    jj = np.arange(128)[:, None]
    ii = np.arange(128)[None, :]
    mask01 = (ii >= jj).astype(np.float32)
    qi = np.arange(NT)[None, :]
    maskb = np.zeros((128, 2, 2, NT), np.float32)
    for o in range(2):
        maskb[:, o, :, :] = np.where((128 * o + jj) <= qi, 0.0, -30000.0)[:, None, :]
    return ident, dec, mask01, maskb.reshape(128, 4 * NT)


def _gcols(inp):
    g = np.zeros((128, GC_N), np.float32)

    def put(col0, vec):
        k = vec.shape[0] // 128
        g[:, col0:col0 + k] = vec.reshape(k, 128).T
    for l in range(4):
        put(GC_PRE + l * 8, inp["pre_norm_g"][l])
        put(GC_POST + l * 8, inp["post_norm_g"][l])
    for j in range(2):
        put(GC_GN + j * 16, inp["ret_gn_g"][j])
        put(GC_QN + j * 6, inp["mla_q_norm_g"][j])
        put(GC_KVN + j * 2, inp["mla_kv_norm_g"][j])
    r = np.arange(128)
    g[:, GC_INVF] = (np.float32(10000.0) ** (-(r.astype(np.float32)) / np.float32(128.0))).astype(np.float32)
    g[:, GC_INVF + 1] = (np.float32(10000.0) ** (-((r % 32).astype(np.float32)) / np.float32(32.0))).astype(np.float32)
    g[:, GC_SGN] = np.where((r % 64) < 32, -1.0, 1.0)
    return g


_CACHE = {}


def kernel(**inputs):
    inp = {k: np.asarray(v) for k, v in inputs.items()}
    n_layers = 4
    if "prog" not in _CACHE:
        _CACHE["prog"] = build_program(n_layers)
    nc, _ = _CACHE["prog"]
    ident, dec, mask01, maskb = _consts()
    gcols = _gcols(inp)
    shared = {
        "gcols": gcols, "ident": ident, "dec": dec, "mask01": mask01, "maskb": maskb,
        "ret_w_in": np.ascontiguousarray(inp["ret_w_in"], dtype=np.float32),
        "ret_w_out": np.ascontiguousarray(inp["ret_w_out"], dtype=np.float32),
        "mla_w_in": np.ascontiguousarray(inp["mla_w_in"], dtype=np.float32),
        "mla_w_uq": np.ascontiguousarray(inp["mla_w_uq"], dtype=np.float32),
        "mla_w_uk": np.ascontiguousarray(inp["mla_w_uk"], dtype=np.float32).reshape(2, 256, 2048),
        "mla_w_uv": np.ascontiguousarray(inp["mla_w_uv"], dtype=np.float32).reshape(2, 256, 2048),
        "mla_w_out": np.ascontiguousarray(inp["mla_w_out"], dtype=np.float32),
        "ple_w_proj": np.ascontiguousarray(inp["ple_w_proj"], dtype=np.float32),
        "ple_w_gate": np.ascontiguousarray(inp["ple_w_gate"], dtype=np.float32),
    }
    in_maps = []
    for core in range(8):
        b = core % 4
        m = dict(shared)
        m["x"] = np.ascontiguousarray(inp["x"][b], dtype=np.float32)
        m["p"] = np.ascontiguousarray(inp["p"][:, b], dtype=np.float32)
        m["pos"] = np.ascontiguousarray(inp["positions"][b].reshape(1, T), dtype=np.int32)
        in_maps.append(m)
    res = run_bass_kernel_spmd(nc, in_maps, core_ids=list(range(8)))
    out = np.stack([np.asarray(res.results[b]["out"], dtype=np.float32) for b in range(4)], axis=0)
    return out
```
